# Optimizing a Trainium2 kernel written in Bass

```python
import math
import jax, jax.numpy as jnp
from jax import lax
import numpy as np

D_MODEL = 1024
BATCH = 8
SEQ = 2048
DEPTH = 1
DEC_BATCH = 128
DEC_SEQ = 4
PAST_LEN = 16384
PAGE_SIZE = 128

SSM_EXPAND = 2
SSM_D_INNER = SSM_EXPAND * D_MODEL
SSM_HEAD_DIM = 64
SSM_HEADS = SSM_D_INNER // SSM_HEAD_DIM
SSM_GROUPS = 4
SSM_STATE = 128
SSM_CONV = 4
SSM_CONV_DIM = SSM_D_INNER + 2 * SSM_GROUPS * SSM_STATE
DT_MIN = 0.001
DT_MAX = 0.1
RET_HEADS = 4
RET_QK_DIM = D_MODEL // RET_HEADS
RET_V_DIM = 2 * RET_QK_DIM
RET_QK = RET_HEADS * RET_QK_DIM
RET_V = RET_HEADS * RET_V_DIM
ROPE_BASE = 10000.0
D_FF = 2816
CHUNK = 128
NORM_EPS = 1e-6
GATED_NORM_EPS = 1e-5

IN_SPLITS = (SSM_D_INNER, SSM_CONV_DIM, SSM_HEADS, RET_QK, RET_QK, RET_V, RET_V, D_MODEL, D_MODEL)
IN_DIM = sum(IN_SPLITS)
IN_OFFSETS = [int(v) for v in np.cumsum(IN_SPLITS)[:-1]]

kernel_name = 'hybrid_ssd_retention_macaron_step'


def rms_norm(x, g, eps=NORM_EPS):
    xf = x.astype(jnp.float32)
    y = xf * lax.rsqrt(jnp.mean(xf * xf, axis=-1, keepdims=True) + eps)
    return (y * g.astype(jnp.float32)).astype(x.dtype)


def swiglu(x, w1, w3, w2):
    return (jax.nn.silu(x @ w1) * (x @ w3)) @ w2


def causal_conv(xbc, buf, w, b):
    L = xbc.shape[1]
    full = jnp.concatenate([buf, xbc], axis=1)
    out = b + full[:, 0:L] * w[0]
    for k in range(1, SSM_CONV):
        out = out + full[:, k:k + L] * w[k]
    return jax.nn.silu(out), full[:, L:]


def to_chunks(t, nc, q):
    return jnp.moveaxis(t.reshape((t.shape[0], nc, q) + t.shape[2:]), 1, 0)


def ssd_scan(x, dt, a, bm, cm, h0):
    Bsz, L = x.shape[:2]
    q = math.gcd(L, CHUNK)
    nc = L // q
    hg = SSM_HEADS // SSM_GROUPS
    xs = to_chunks(x.reshape(Bsz, L, SSM_GROUPS, hg, SSM_HEAD_DIM), nc, q)
    dts = to_chunks(dt.reshape(Bsz, L, SSM_GROUPS, hg), nc, q)
    bs = to_chunks(bm, nc, q)
    cs = to_chunks(cm, nc, q)
    a_g = a.reshape(SSM_GROUPS, hg)
    causal = jnp.tril(jnp.ones((q, q), dtype=bool))[None, :, :, None, None]

    def step(h, inp):
        xc, dtc, bc, cc = inp
        la = jnp.cumsum(dtc * a_g, axis=1)
        seg = la[:, :, None] - la[:, None, :]
        decay = jnp.exp(jnp.where(causal, seg, -jnp.inf))
        cb = jnp.einsum('btgn,bsgn->btsg', cc, bc)
        w = cb[..., None] * decay * dtc[:, None]
        y = jnp.einsum('btsgh,bsghp->btghp', w, xc)
        y = y + jnp.einsum('btgn,bghpn->btghp', cc, h) * jnp.exp(la)[..., None]
        tail = jnp.exp(la[:, -1:] - la) * dtc
        h = h * jnp.exp(la[:, -1])[..., None, None] + jnp.einsum('bsgn,bsgh,bsghp->bghpn', bc, tail, xc)
        return h, y

    h0g = h0.reshape(Bsz, SSM_GROUPS, hg, SSM_HEAD_DIM, SSM_STATE)
    h, ys = lax.scan(step, h0g, (xs, dts, bs, cs))
    y = jnp.moveaxis(ys, 0, 1).reshape(Bsz, L, SSM_HEADS, SSM_HEAD_DIM)
    return y, h.reshape(Bsz, SSM_HEADS, SSM_HEAD_DIM, SSM_STATE)


def ret_log_decay():
    return jnp.log1p(-jnp.exp2(-5.0 - jnp.arange(RET_HEADS, dtype=jnp.float32)))


def retention_scan(q, k, v, h0):
    Bsz, L = q.shape[:2]
    c = math.gcd(L, CHUNK)
    nc = L // c
    lg = ret_log_decay()
    idx = jnp.arange(c, dtype=jnp.float32)
    rel = idx[:, None] - idx[None, :]
    dmat = jnp.where((rel >= 0)[..., None], jnp.exp(jnp.maximum(rel, 0.0)[..., None] * lg), 0.0)
    dmat = dmat.transpose(2, 0, 1)
    q_dec = jnp.exp((idx + 1.0)[:, None] * lg)[:, :, None]
    k_dec = jnp.exp((c - 1.0 - idx)[:, None] * lg)[:, :, None]
    chunk_dec = jnp.exp(c * lg)[:, None, None]

    def step(h, inp):
        qc, kc, vc = inp
        s = jnp.einsum('bthd,bshd->bhts', qc, kc) * dmat
        y = jnp.einsum('bhts,bshe->bthe', s, vc)
        y = y + jnp.einsum('bthd,bhde->bthe', qc * q_dec, h)
        h = h * chunk_dec + jnp.einsum('bshd,bshe->bhde', kc * k_dec, vc)
        return h, y

    h, ys = lax.scan(step, h0, (to_chunks(q, nc, c), to_chunks(k, nc, c), to_chunks(v, nc, c)))
    y = jnp.moveaxis(ys, 0, 1).reshape(Bsz, L, RET_HEADS, RET_V_DIM)
    return y, h


def rotary(t, pos):
    half = RET_QK_DIM // 2
    inv = ROPE_BASE ** (-jnp.arange(half, dtype=jnp.float32) / half)
    ang = pos[:, None] * inv
    cos = jnp.cos(ang)[None, :, None]
    sin = jnp.sin(ang)[None, :, None]
    t1, t2 = t[..., :half], t[..., half:]
    return jnp.concatenate([t1 * cos - t2 * sin, t1 * sin + t2 * cos], axis=-1)


def trunk_layer(x, pos0, ssm_h0, conv_buf, ret_h0, p):
    Bsz, L, _ = x.shape
    f32 = jnp.float32
    h = x + 0.5 * swiglu(rms_norm(x, p['norm_ffn1']), p['ffn1_w1'], p['ffn1_w3'], p['ffn1_w2'])
    u = rms_norm(h, p['norm_mix'])
    proj = u @ p['w_in']
    z, xbc, dt_raw, rq, rk, rv, rg, ga, gb = jnp.split(proj, IN_OFFSETS, axis=-1)
    xbc_c, conv_new = causal_conv(xbc, conv_buf.astype(xbc.dtype), p['conv_w'], p['conv_b'])
    xs, bm, cm = jnp.split(xbc_c.astype(f32), [SSM_D_INNER, SSM_D_INNER + SSM_GROUPS * SSM_STATE], axis=-1)
    dt = jax.nn.softplus(dt_raw.astype(f32) + p['dt_bias'].astype(f32))
    a = -jnp.exp(p['a_log'].astype(f32))
    xh = xs.reshape(Bsz, L, SSM_HEADS, SSM_HEAD_DIM)
    y_ssm, ssm_new = ssd_scan(xh, dt, a,
                              bm.reshape(Bsz, L, SSM_GROUPS, SSM_STATE),
                              cm.reshape(Bsz, L, SSM_GROUPS, SSM_STATE),
                              ssm_h0.astype(f32))
    y_ssm = y_ssm + p['ssm_d'].astype(f32)[:, None] * xh
    y_ssm = y_ssm.reshape(Bsz, L, SSM_D_INNER) * jax.nn.silu(z.astype(f32))
    y_ssm = rms_norm(y_ssm, p['ssm_norm'], GATED_NORM_EPS).astype(x.dtype)
    branch_ssm = y_ssm @ p['w_branch_ssm']
    pos = pos0 + jnp.arange(L, dtype=f32)
    q = rotary(rq.astype(f32).reshape(Bsz, L, RET_HEADS, RET_QK_DIM), pos)
    k = rotary(rk.astype(f32).reshape(Bsz, L, RET_HEADS, RET_QK_DIM), pos) * (RET_QK_DIM ** -0.5)
    v = rv.astype(f32).reshape(Bsz, L, RET_HEADS, RET_V_DIM)
    y_ret, ret_new = retention_scan(q, k, v, ret_h0.astype(f32))
    y_ret = rms_norm(y_ret, p['ret_norm'].reshape(RET_HEADS, RET_V_DIM), NORM_EPS).reshape(Bsz, L, RET_V)
    y_ret = (jax.nn.silu(rg.astype(f32)) * y_ret).astype(x.dtype)
    branch_ret = y_ret @ p['w_branch_ret']
    merged = jax.nn.sigmoid(ga) * branch_ssm + jax.nn.sigmoid(gb) * branch_ret
    h = h + merged @ p['w_out']
    h = h + 0.5 * swiglu(rms_norm(h, p['norm_ffn2']), p['ffn2_w1'], p['ffn2_w3'], p['ffn2_w2'])
    return h, ssm_new, conv_new, ret_new


def setup_inputs(seed: int = 0) -> dict:
    key = jax.random.key(seed)
    ks = iter(jax.random.split(key, 32))

    def nrm(shape, scale):
        return jax.random.normal(next(ks), shape, jnp.float32) * scale

    def gain(shape):
        return 1.0 + nrm(shape, 0.02)

    dt0 = jnp.exp(jax.random.uniform(next(ks), (DEPTH, SSM_HEADS), jnp.float32,
                                     minval=math.log(DT_MIN), maxval=math.log(DT_MAX)))
    dt_bias = dt0 + jnp.log(-jnp.expm1(-dt0))
    a_log = jnp.log(jax.random.uniform(next(ks), (DEPTH, SSM_HEADS), jnp.float32, minval=1.0, maxval=16.0))
    return {
        'x_prompt': nrm((BATCH, SEQ, D_MODEL), 1.0),
        'x_sample': nrm((DEC_BATCH, DEC_SEQ, D_MODEL), 1.0),
        'state_ssm': nrm((DEPTH, DEC_BATCH, SSM_HEADS, SSM_HEAD_DIM, SSM_STATE), 0.1),
        'state_conv': nrm((DEPTH, DEC_BATCH, SSM_CONV - 1, SSM_CONV_DIM), 1.0),
        'state_ret': nrm((DEPTH, DEC_BATCH, RET_HEADS, RET_QK_DIM, RET_V_DIM), 0.5),
        'norm_ffn1': gain((DEPTH, D_MODEL)),
        'ffn1_w1': nrm((DEPTH, D_MODEL, D_FF), D_MODEL ** -0.5),
        'ffn1_w3': nrm((DEPTH, D_MODEL, D_FF), D_MODEL ** -0.5),
        'ffn1_w2': nrm((DEPTH, D_FF, D_MODEL), D_FF ** -0.5),
        'norm_mix': gain((DEPTH, D_MODEL)),
        'w_in': nrm((DEPTH, D_MODEL, IN_DIM), D_MODEL ** -0.5),
        'conv_w': nrm((DEPTH, SSM_CONV, SSM_CONV_DIM), SSM_CONV ** -0.5),
        'conv_b': nrm((DEPTH, SSM_CONV_DIM), 0.02),
        'dt_bias': dt_bias,
        'a_log': a_log,
        'ssm_d': 1.0 + nrm((DEPTH, SSM_HEADS), 0.1),
        'ssm_norm': gain((DEPTH, SSM_D_INNER)),
        'ret_norm': gain((DEPTH, RET_V)),
        'w_branch_ssm': nrm((DEPTH, SSM_D_INNER, D_MODEL), SSM_D_INNER ** -0.5),
        'w_branch_ret': nrm((DEPTH, RET_V, D_MODEL), RET_V ** -0.5),
        'w_out': nrm((DEPTH, D_MODEL, D_MODEL), D_MODEL ** -0.5),
        'norm_ffn2': gain((DEPTH, D_MODEL)),
        'ffn2_w1': nrm((DEPTH, D_MODEL, D_FF), D_MODEL ** -0.5),
        'ffn2_w3': nrm((DEPTH, D_MODEL, D_FF), D_MODEL ** -0.5),
        'ffn2_w2': nrm((DEPTH, D_FF, D_MODEL), D_FF ** -0.5),
        'norm_final': gain((D_MODEL,)),
    }


def reference(x_prompt, x_sample, state_ssm, state_conv, state_ret,
              norm_ffn1, ffn1_w1, ffn1_w3, ffn1_w2, norm_mix, w_in, conv_w, conv_b,
              dt_bias, a_log, ssm_d, ssm_norm, ret_norm, w_branch_ssm, w_branch_ret, w_out,
              norm_ffn2, ffn2_w1, ffn2_w3, ffn2_w2, norm_final):
    bp = x_prompt.shape[0]
    hp, hs = x_prompt, x_sample
    ssm_p, conv_p, ret_p, ssm_s, conv_s, ret_s = [], [], [], [], [], []
    for l in range(DEPTH):
        p = dict(norm_ffn1=norm_ffn1[l], ffn1_w1=ffn1_w1[l], ffn1_w3=ffn1_w3[l], ffn1_w2=ffn1_w2[l],
                 norm_mix=norm_mix[l], w_in=w_in[l], conv_w=conv_w[l], conv_b=conv_b[l],
                 dt_bias=dt_bias[l], a_log=a_log[l], ssm_d=ssm_d[l], ssm_norm=ssm_norm[l],
                 ret_norm=ret_norm[l], w_branch_ssm=w_branch_ssm[l], w_branch_ret=w_branch_ret[l],
                 w_out=w_out[l], norm_ffn2=norm_ffn2[l], ffn2_w1=ffn2_w1[l], ffn2_w3=ffn2_w3[l],
                 ffn2_w2=ffn2_w2[l])
        hp, a1, b1, c1 = trunk_layer(
            hp, 0.0,
            jnp.zeros((bp, SSM_HEADS, SSM_HEAD_DIM, SSM_STATE), jnp.float32),
            jnp.zeros((bp, SSM_CONV - 1, SSM_CONV_DIM), x_prompt.dtype),
            jnp.zeros((bp, RET_HEADS, RET_QK_DIM, RET_V_DIM), jnp.float32), p)
        hs, a2, b2, c2 = trunk_layer(hs, float(PAST_LEN), state_ssm[l], state_conv[l], state_ret[l], p)
        ssm_p.append(a1); conv_p.append(b1); ret_p.append(c1)
        ssm_s.append(a2); conv_s.append(b2); ret_s.append(c2)
    y_prompt = rms_norm(hp, norm_final)
    y_sample = rms_norm(hs, norm_final)
    return (y_prompt, y_sample,
            jnp.stack(ssm_p).astype(state_ssm.dtype), jnp.stack(conv_p).astype(state_conv.dtype),
            jnp.stack(ret_p).astype(state_ret.dtype),
            jnp.stack(ssm_s).astype(state_ssm.dtype), jnp.stack(conv_s).astype(state_conv.dtype),
            jnp.stack(ret_s).astype(state_ret.dtype))
```

```python
import math
import contextlib
import numpy as np
import concourse.bass as bass
import concourse.mybir as mybir
from concourse.bass_utils import run_bass_kernel_spmd

F32 = mybir.dt.float32
BF16 = mybir.dt.bfloat16
AF = mybir.ActivationFunctionType
ALU = mybir.AluOpType

NCORES = 8
D = 1024
KD = 8
SEQ = 2048
DEC_B = 128
DEC_SEQ = 4
PAST = 16384
DFF = 2816
IN_DIM = 13344
OFF_Z, OFF_X, OFF_B, OFF_C, OFF_DT = 0, 2048, 4096, 4608, 5120
OFF_Q, OFF_K, OFF_V, OFF_G, OFF_GA, OFF_GB = 5152, 6176, 7200, 9248, 11296, 12320
NPASS = 4
PCH = (SEQ // 128) // NPASS
NS = (DEC_B // NCORES) // NPASS
NPT = PCH * 128
NTP = NPT + NS * DEC_SEQ
NCH = PCH + 1
EPS = 1e-6
EPS_G = 1e-5
GAMMAS = [1.0 - 2.0 ** (-5.0 - h) for h in range(4)]

SAME_ENG_DIST = 4
DBG = {"passes": 4, "stop": 99, "ssm_tiles": 99, "cut": 99, "groups": 4, "jcut": 99, "sub": 99}
STRICT_SAME_ENGINE = True
SEM_GEN_LIMIT = 12000


class Prog:
    ENGS = ("pe", "act", "dve", "pool", "sp")

    def __init__(self, nc):
        self.nc = nc
        self.ops = []
        self.marks = []

    def mark(self, label):
        self.marks.append((label, sum(1 for o in self.ops if o['eng'] == 'pe')))

    def op(self, eng, fn, rd=(), wr=(), dma=False, semkey=None):
        rd, wr = tuple(rd), tuple(wr)
        self.ops.append(dict(eng=eng, fn=fn, rd=rd, wr=wr, dma=dma, semkey=semkey))

    def emit(self, final_wait_eng="sp"):
        nc = self.nc
        ops = self.ops
        n = len(ops)
        last_w = {}
        readers = {}
        bank_last = {}
        eng_pos = {e: 0 for e in self.ENGS}
        for o in ops:
            o["pos"] = eng_pos[o["eng"]]
            eng_pos[o["eng"]] += 1
        deps = [None] * n
        for i, o in enumerate(ops):
            d = {}
            for k in o["rd"]:
                if k in last_w:
                    d[last_w[k]] = "raw"
            for k in o["wr"]:
                if k in last_w:
                    d.setdefault(last_w[k], "waw")
                for r in readers.get(k, ()):
                    if r != i:
                        d.setdefault(r, "war")
            for k in set(o["rd"]) | set(o["wr"]):
                if isinstance(k, tuple) and k and k[0] == "ps":
                    la = bank_last.setdefault(k, {})
                    for e2, j2 in la.items():
                        if e2 != o["eng"]:
                            d.setdefault(j2, "xeng")
                    la[o["eng"]] = i
            for k in o["rd"]:
                readers.setdefault(k, []).append(i)
            for k in o["wr"]:
                last_w[k] = i
                readers[k] = []
            keep = []
            for j, kind in d.items():
                pj = ops[j]
                if pj["eng"] == o["eng"] and not pj["dma"]:
                    if o["dma"]:
                        keep.append(j)
                    elif o["eng"] == "pe":
                        pass
                    elif STRICT_SAME_ENGINE:
                        keep.append(j)
                    elif kind == "raw" and (o["pos"] - pj["pos"]) <= SAME_ENG_DIST:
                        keep.append(j)
                else:
                    keep.append(j)
            deps[i] = keep
        signaling = [False] * n
        for i in range(n):
            for j in deps[i]:
                signaling[j] = True
        sems = []

        def new_sem(name):
            s = nc.alloc_semaphore(name)
            sems.append(s)
            return s

        eng_sem, eng_cnt, dma_sem, dma_cnt = {}, {}, {}, {}
        sig = [None] * n
        for i, o in enumerate(ops):
            if o["dma"]:
                k = o["semkey"] if o["semkey"] is not None else (o["wr"][0] if o["wr"] else ("dma", i))
                if k not in dma_sem:
                    dma_sem[k] = new_sem("d%d" % len(sems))
                    dma_cnt[k] = 0
                dma_cnt[k] += 16
                sig[i] = (dma_sem[k], dma_cnt[k])
            elif signaling[i]:
                e = o["eng"]
                if e not in eng_sem or eng_cnt[e] >= SEM_GEN_LIMIT:
                    eng_sem[e] = new_sem("e%s%d" % (e, len(sems)))
                    eng_cnt[e] = 0
                eng_cnt[e] += 1
                sig[i] = (eng_sem[e], eng_cnt[e])
        waited = {e: {} for e in self.ENGS}
        waits = [None] * n
        for i, o in enumerate(ops):
            need = {}
            for j in deps[i]:
                s, v = sig[j]
                key = id(s)
                if key not in need or need[key][1] < v:
                    need[key] = (s, v)
            w = []
            for key, (s, v) in need.items():
                if waited[o["eng"]].get(key, 0) >= v:
                    continue
                waited[o["eng"]][key] = v
                w.append((s, v))
            waits[i] = w
        finals = [(dma_sem[k], dma_cnt[k]) for k in dma_sem]
        by_eng = {e: [i for i, o in enumerate(ops) if o["eng"] == e] for e in self.ENGS}
        self.stats = {e: len(by_eng[e]) for e in self.ENGS}
        self.stats["waits"] = sum(len(w) for w in waits)
        self.stats["sems"] = len(sems)

        def run_engine(eng_name, eng):
            for i in by_eng[eng_name]:
                o = ops[i]
                for s, v in waits[i]:
                    eng.wait_ge(s, v)
                ins = o["fn"](eng)
                if sig[i] is not None:
                    ins.then_inc(sig[i][0], 16 if o["dma"] else 1)
            if eng_name == final_wait_eng:
                for s, v in finals:
                    eng.wait_ge(s, v)

        with nc.Block() as block:
            @block.tensor
            def _(e):
                run_engine("pe", e)

            @block.scalar
            def _(e):
                run_engine("act", e)

            @block.vector
            def _(e):
                run_engine("dve", e)

            @block.gpsimd
            def _(e):
                run_engine("pool", e)

            @block.sync
            def _(e):
                run_engine("sp", e)


def token_tiles():
    tiles = []
    t = 0
    while t < NPT:
        n = min(512, NPT - t)
        tiles.append((t, n, False))
        t += n
    tiles.append((NPT, NS * DEC_SEQ, True))
    return tiles


def build_nc():
    nc = bass.Bass("TRN2", target_bir_lowering=False)
    P = Prog(nc)

    def din(name, shape):
        return nc.dram_tensor(name, list(shape), F32, kind="ExternalInput").ap()

    def dout(name, shape):
        return nc.dram_tensor(name, list(shape), F32, kind="ExternalOutput").ap()

    xT_d = din("xT", [NPASS, 128, KD, NTP])
    w1_d = [din("f1w1", [D, DFF]), din("f2w1", [D, DFF])]
    w3_d = [din("f1w3", [D, DFF]), din("f2w3", [D, DFF])]
    w2_d = [din("f1w2", [DFF, D]), din("f2w2", [DFF, D])]
    win_d = din("w_in", [D, IN_DIM])
    wbs_d = din("w_bs", [2048, D])
    wbr_d = din("w_br", [2048, D])
    wout_d = din("w_out", [D, D])
    gains_d = din("gains", [128, 4, KD])
    gssm_d = din("g_ssm", [128, 16])
    gret_d = din("g_ret", [128, 16])
    convw_d = din("convw", [128, 24, 4])
    convb_d = din("convb", [128, 24])
    dtb_d = din("dtb", [128, 32])
    alog_d = din("alog", [128, 32])
    dpp_d = din("dpp", [128, 16])
    cos_d = din("cosT", [NPASS, 128, NTP])
    sin_d = din("sinT", [NPASS, 128, NTP])
    qdec_d = din("qdecT", [4, 128, NTP])
    kdec_d = din("kdec", [128, 8])
    dmat_d = din("dmatT", [128, 4, 128])
    tri_d = din("tri", [128, 128])
    ustr_d = din("ustrict", [128, 128])
    identf_d = din("identf", [128, 128])
    tris_d = din("tri_s", [128, 128])
    ustrs_d = din("ustr_s", [128, 128])
    blks_d = din("blk_s", [128, 128])
    selB_d = din("selB", [128, NS, 128])
    seqsel_d = din("seqsel", [128, NS])
    csel_d = din("csel", [128, NS, 32])
    dmatS_d = din("dmatS", [128, 4, 128])
    ssm_in_d = din("ssmT_in", [NS * NPASS, 128, 2048])
    conv_in_d = din("conv_in", [NS * NPASS, 128, 24, 3])
    ret_in_d = din("ret_in", [NS * NPASS, 4, 256, 512])

    yT_d = dout("yT", [NPASS, 128, KD, NTP])
    ssm_p_d = dout("ssmT_p", [128, 2048])
    conv_p_d = dout("conv_p", [128, 24, 3])
    ret_p_d = dout("ret_p", [4, 256, 512])
    ssm_s_d = dout("ssmT_s", [NS * NPASS, 128, 2048])
    conv_s_d = dout("conv_s", [NS * NPASS, 128, 24, 3])
    ret_s_d = dout("ret_s", [NS * NPASS, 4, 256, 512])

    TT = token_tiles()
    es = contextlib.ExitStack()
    with es:
        def S(name, shape, dt):
            return es.enter_context(nc.sbuf_tensor("s_" + name, list(shape), dt))

        ps = [es.enter_context(nc.psum_tensor("ps%d" % i, [128, 512], F32)) for i in range(8)]
        PK = [("ps", i) for i in range(8)]

        hT = S("hT", [128, KD, NTP], F32)
        uT = S("uT", [128, KD, NTP], BF16)
        yg = S("yg", [128, 16, NTP], BF16)
        NSLOT = 2
        WA = S("WA", [128, NSLOT, 12288], BF16)
        hst = S("hst", [128, 4, 512], F32)
        rst = S("rst", [128, 4, 2, 512], F32)
        hbf = S("hbf", [128, 512], BF16)
        rbf = S("rbf", [128, 2, 512], BF16)
        hist = S("hist", [128, 24, 3], F32)
        convp_st = S("convp_st", [128, 24, 3], F32)
        cvin = S("cvin", [128, NS, 24, 3], F32)
        cvout = S("cvout", [128, NS, 24, 3], F32)
        gains = S("gains", [128, 4, KD], F32)
        gssm = S("gssm", [128, 16], F32)
        gret = S("gret", [128, 16], F32)
        convw = S("convw", [128, 24, 4], F32)
        convb = S("convb", [128, 24], F32)
        dtb = S("dtb", [128, 32], F32)
        nega = S("nega", [128, 32], F32)
        dpp = S("dpp", [128, 16], F32)
        kdec = S("kdec", [128, 8], F32)
        dmatT = S("dmatT", [128, 4, 128], F32)
        tri = S("tri", [128, 128], F32)
        ustr = S("ustr", [128, 128], F32)
        identf = S("identf", [128, 128], F32)
        tri_s = S("tri_s", [128, 128], F32)
        ustr_s = S("ustr_s", [128, 128], F32)
        blk_s = S("blk_s", [128, 128], F32)
        selB = S("selB", [128, NS, 128], F32)
        seqsel = S("seqsel", [128, NS], F32)
        csel = S("csel", [128, NS, 32], F32)
        dmatS = S("dmatS", [128, 4, 128], F32)
        identb = S("identb", [128, 128], BF16)
        onesb = S("onesb", [128, 128], BF16)
        onesf = S("onesf", [128, 128], F32)
        epsc = S("epsc", [128, 4], F32)
        wdt = S("wdt", [128, KD, 32], BF16)
        dt_tm = S("dt_tm", [128, NCH, 32], F32)
        dta_tm = S("dta_tm", [128, NCH, 32], F32)
        ssq_b = S("ssq_b", [128, NTP], F32)
        NAR = 18432
        AR = S("AR", [128, NAR], F32)
        K = {}
        RG = 8
        ar_off = [0]

        ar_pos = {}

        def A(name, shape, dt, alias=None):
            n = 1
            for s_ in shape[1:]:
                n *= s_
            nf = n if dt == F32 else (n + 1) // 2
            nf = (nf + 7) // 8 * 8
            if alias is None:
                o = ar_off[0]
                assert o + nf <= NAR, (name, o, nf)
                ar_off[0] = o + nf
            else:
                o = ar_pos[alias]
                assert o + nf <= NAR, (name, o, nf)
            ar_pos[name] = o
            v = AR[0:shape[0], o:o + nf]
            if dt != F32:
                v = v.bitcast(BF16)
            v = v[:, 0:n]
            if len(shape) == 3:
                v = v.rearrange("p (a b) -> p a b", a=shape[1])
            elif len(shape) == 4:
                v = v.rearrange("p (a b c) -> p a b c", a=shape[1], b=shape[2])
            K[name] = [("AR", r) for r in range(o // RG, (o + nf - 1) // RG + 1)]
            return v

        def KR(name, lo, n):
            o = ar_pos[name] + lo
            return [("AR", r) for r in range(o // RG, (o + n - 1) // RG + 1)]

        def stage_begin():
            ar_off[0] = 0
            T = {}
            T["rsn"] = A("rsn", [128, 512], F32)
            T["sgt0"] = A("sgt0", [128, 512], F32)
            T["sgt1"] = A("sgt1", [128, 512], F32)
            return T

        def MM(out, lhsT, rhs, start, stop, rd, wr):
            P.op("pe", lambda e: e.matmul(out, lhsT=lhsT, rhs=rhs, start=start, stop=stop), rd, wr)

        def TR(out, in_, ident, rd, wr):
            P.op("pe", lambda e: e.transpose(out, in_, ident), rd, wr)

        def ACT(out, in_, func, rd, wr, bias=None, scale=None):
            kw = {}
            if bias is not None:
                kw["bias"] = bias
            if scale is not None:
                kw["scale"] = scale
            P.op("act", lambda e: e.activation(out=out, in_=in_, func=func, **kw), rd, wr)

        def TT_(out, in0, in1, op, rd, wr, eng="dve"):
            P.op(eng, lambda e: e.tensor_tensor(out=out, in0=in0, in1=in1, op=op), rd, wr)

        def TS(out, in0, s1, s2, op0, op1, rd, wr, eng="dve"):
            if s2 is None:
                P.op(eng, lambda e: e.tensor_scalar(out=out, in0=in0, scalar1=s1, scalar2=None, op0=op0), rd, wr)
            else:
                P.op(eng, lambda e: e.tensor_scalar(out=out, in0=in0, scalar1=s1, scalar2=s2, op0=op0, op1=op1), rd, wr)

        def STT(out, in0, scalar, in1, op0, op1, rd, wr, eng="dve"):
            P.op(eng, lambda e: e.scalar_tensor_tensor(out=out, in0=in0, scalar=scalar, in1=in1, op0=op0, op1=op1), rd, wr)

        def CP(out, in_, rd, wr, eng="dve"):
            P.op(eng, lambda e: e.tensor_copy(out=out, in_=in_), rd, wr)

        def RECIP(out, in_, rd, wr):
            P.op("dve", lambda e: e.reciprocal(out=out, in_=in_), rd, wr)

        def MEMSET(ap, val, wr, eng="dve"):
            P.op(eng, lambda e: e.memset(ap, val), (), wr)

        def DMA(out, in_, rd, wr, semkey=None, cast=False):
            P.op("pool" if cast else "sp", lambda e: e.dma_start(out=out, in_=in_), rd, wr, dma=True, semkey=semkey)

        def par2(fa, fb):
            n0 = len(P.ops)
            fa()
            n1 = len(P.ops)
            fb()
            n2 = len(P.ops)
            A_, B_ = P.ops[n0:n1], P.ops[n1:n2]
            merged = []
            ia = ib = 0
            while ia < len(A_) or ib < len(B_):
                if ib >= len(B_) or (ia < len(A_) and ia * len(B_) <= ib * len(A_)):
                    merged.append(A_[ia])
                    ia += 1
                else:
                    merged.append(B_[ib])
                    ib += 1
            P.ops[n0:n2] = merged

        wslot_ctr = [0]

        def next_slot():
            s = wslot_ctr[0] % NSLOT
            wslot_ctr[0] += 1
            return s

        def wreg(s, lo, hi):
            return [("W", s, r) for r in range(lo // 2048, (hi - 1) // 2048 + 1)]

        def wcols(dram, c0, cn):
            return dram.rearrange("(k p) n -> p k n", p=128)[:, :, c0:c0 + cn]

        def wload(s, lo, n, view, src, tag):
            ks = wreg(s, lo, lo + n)
            DMA(view, src, (), ks, cast=True, semkey=("Wsem", s, tag))
            return ks

        for t_, d_, kn in ((gains, gains_d, "gains"), (gssm, gssm_d, "gssm"), (gret, gret_d, "gret"), (convw, convw_d, "convw"),
                           (convb, convb_d, "convb"), (dtb, dtb_d, "dtb"), (dpp, dpp_d, "dpp"), (kdec, kdec_d, "kdec"),
                           (dmatT, dmat_d, "dmatT"), (tri, tri_d, "tri"), (ustr, ustr_d, "ustr"), (identf, identf_d, "identf"),
                           (tri_s, tris_d, "tri_s"), (ustr_s, ustrs_d, "ustr_s"), (blk_s, blks_d, "blk_s"), (selB, selB_d, "selB"),
                           (seqsel, seqsel_d, "seqsel"), (csel, csel_d, "csel"), (dmatS, dmatS_d, "dmatS")):
            sl = tuple(slice(None) for _ in t_.shape)
            DMA(t_[sl], d_[sl], (), [kn])
        DMA(nega[:, :], alog_d[:, :], (), ["nega"])
        ACT(nega[:, :], nega[:, :], AF.Exp, ["nega"], ["nega"])
        TS(nega[:, :], nega[:, :], -1.0, None, ALU.mult, None, ["nega"], ["nega"])
        CP(identb[:, :], identf[:, :], ["identf"], ["identb"])
        MEMSET(onesb[:, :], 1.0, ["onesb"])
        MEMSET(onesf[:, :], 1.0, ["onesf"])
        MEMSET(epsc[:, 0:1], EPS, ["epsc"])
        MEMSET(epsc[:, 1:2], EPS_G, ["epsc"])
        MEMSET(epsc[:, 2:3], 1.0, ["epsc"])
        MEMSET(hst[:, :, :], 0.0, [("hst", g) for g in range(4)])
        MEMSET(rst[:, :, :, :], 0.0, [("rst", r) for r in range(4)])
        MEMSET(hist[:, :, :], 0.0, [("hist", g) for g in range(4)])
        DMA(wdt[:, :, :], wcols(win_d, OFF_DT, 32), (), ["wdt"], cast=True)

        HK = lambda ti, m: ("h", ti, m)
        UK = lambda ti, k: ("u", ti, k)

        def norm_stage(T, gidx, out_dma=None):
            sq, rsn = T["sq"], T["rsn"]
            for ti, (t0, tn, _) in enumerate(TT):
                hk = [HK(ti, m) for m in range(KD)]
                ACT(sq[:, :, 0:tn], hT[:, :, t0:t0 + tn], AF.Square, hk, K["sq"])
                for k in range(KD):
                    MM(ps[7][:, 0:tn], onesb[:, :], sq[:, k, 0:tn], k == 0, k == KD - 1, K["sq"] + ["onesb"], [PK[7]])
                ACT(rsn[:, 0:tn], ps[7][:, 0:tn], AF.Sqrt, [PK[7], "epsc"], K["rsn"], bias=epsc[:, 0:1], scale=1.0 / D)
                RECIP(rsn[:, 0:tn], rsn[:, 0:tn], K["rsn"], K["rsn"])
                if out_dma is None:
                    for k in range(KD):
                        STT(uT[:, k, t0:t0 + tn], hT[:, k, t0:t0 + tn], gains[:, gidx, k:k + 1], rsn[:, 0:tn], ALU.mult, ALU.mult,
                            [HK(ti, k), "gains"] + K["rsn"], [UK(ti, k)])
                else:
                    outT = T["outT"]
                    for k in range(KD):
                        STT(outT[:, k, 0:tn], hT[:, k, t0:t0 + tn], gains[:, gidx, k:k + 1], rsn[:, 0:tn], ALU.mult, ALU.mult,
                            [HK(ti, k), "gains"] + K["rsn"], K["outT"])
                    DMA(out_dma[:, :, t0:t0 + tn], outT[:, :, 0:tn], K["outT"], [("yout", ti)], semkey="outT")

        def ffn_stage(T, which):
            w1, w3, w2 = w1_d[which], w3_d[which], w2_d[which]
            aT = [T["aT0"], T["aT1"]]
            j0 = 0
            it = 0
            while j0 < DFF:
                jn = min(512, DFF - j0)
                nj = jn // 128
                s = next_slot()
                w1b = WA[:, s, 0:4096].rearrange("p (k n) -> p k n", k=KD)
                w3b = WA[:, s, 4096:8192].rearrange("p (k n) -> p k n", k=KD)
                w2b = WA[:, s, 8192:12288].rearrange("p (c n) -> p c n", c=4)
                k1 = wload(s, 0, 4096, w1b[:, :, 0:jn], wcols(w1, j0, jn), 0)
                k3 = wload(s, 4096, 4096, w3b[:, :, 0:jn], wcols(w3, j0, jn), 1)
                k2 = wload(s, 8192, 4096, w2b[:, 0:nj, :], w2[j0:j0 + jn, :].rearrange("(c p) n -> p c n", p=128), 2)
                for ti, (t0, tn, _) in enumerate(TT):
                    a = aT[it % 2]
                    akj = [KR("aT%d" % (it % 2), 256 * jc_, 256) for jc_ in range(4)]
                    it += 1
                    for jc in range(nj):
                        gb, ub = jc % 2, 2 + jc % 2
                        for k in range(KD):
                            MM(ps[gb][:, 0:tn], w1b[:, k, jc * 128:(jc + 1) * 128], uT[:, k, t0:t0 + tn], k == 0, k == KD - 1,
                               k1 + [UK(ti, k)], [PK[gb]])
                        for k in range(KD):
                            MM(ps[ub][:, 0:tn], w3b[:, k, jc * 128:(jc + 1) * 128], uT[:, k, t0:t0 + tn], k == 0, k == KD - 1,
                               k3 + [UK(ti, k)], [PK[ub]])
                        sg = T["sgt%d" % (jc % 2)]
                        sgk = K["sgt%d" % (jc % 2)]
                        ACT(sg[:, 0:tn], ps[gb][:, 0:tn], AF.Silu, [PK[gb]], sgk)
                        TT_(a[:, jc, 0:tn], sg[:, 0:tn], ps[ub][:, 0:tn], ALU.mult, sgk + [PK[ub]], akj[jc])
                    for m in range(KD):
                        ob = 4 + m % 2
                        for jc in range(nj):
                            MM(ps[ob][:, 0:tn], w2b[:, jc, m * 128:(m + 1) * 128], a[:, jc, 0:tn], jc == 0, jc == nj - 1,
                               k2 + akj[jc], [PK[ob]])
                        STT(hT[:, m, t0:t0 + tn], ps[ob][:, 0:tn], 0.5, hT[:, m, t0:t0 + tn], ALU.mult, ALU.add,
                            [PK[ob], HK(ti, m)], [HK(ti, m)])
                j0 += jn

        def chunks_of_tile(ti):
            t0, tn, is_s = TT[ti]
            if not is_s:
                return [(c * 128, 128, (t0 // 128) + c, t0 + c * 128, None) for c in range(tn // 128)]
            return [(0, NS * DEC_SEQ, PCH, t0, "batch")]

        def dt_stage(T):
            smallt = T["smallt"]
            for ti in range(len(TT)):
                for (c0, Tc, ci, tok0, si) in chunks_of_tile(ti):
                    for k in range(KD):
                        MM(ps[1][0:Tc, 0:32], uT[:, k, tok0:tok0 + Tc], wdt[:, k, :], k == 0, k == KD - 1, [UK(ti, k), "wdt"], [PK[1]])
                    xb, ax, ee, ll = smallt[0:Tc, 0, :], smallt[0:Tc, 1, :], smallt[0:Tc, 2, :], smallt[0:Tc, 3, :]
                    kk = K["smallt"]
                    TT_(xb, ps[1][0:Tc, 0:32], dtb[0:Tc, :], ALU.add, [PK[1], "dtb"], kk)
                    ACT(ax, xb, AF.Abs, kk, kk)
                    ACT(ee, ax, AF.Exp, kk, kk, scale=-1.0)
                    ACT(ll, ee, AF.Ln, kk + ["epsc"], kk, bias=epsc[0:Tc, 2:3])
                    STT(dt_tm[0:Tc, ci, :], xb, 0.0, ll, ALU.max, ALU.add, kk, [("dt", ci)])
                    TT_(dta_tm[0:Tc, ci, :], dt_tm[0:Tc, ci, :], nega[0:Tc, :], ALU.mult, [("dt", ci), "nega"], [("dta", ci)])

        def ssm_group(T, g, p, last_pass):
            s = next_slot()
            wz = WA[:, s, 0:4096].rearrange("p (k n) -> p k n", k=KD)
            wx = WA[:, s, 4096:8192].rearrange("p (k n) -> p k n", k=KD)
            wbc = WA[:, s, 8192:10240].rearrange("p (k n) -> p k n", k=KD)
            kz = wload(s, 0, 4096, wz, wcols(win_d, OFF_Z + 512 * g, 512), 0)
            kx = wload(s, 4096, 4096, wx, wcols(win_d, OFF_X + 512 * g, 512), 1)
            kb1 = wload(s, 8192, 2048, wbc[:, :, 0:128], wcols(win_d, OFF_B + 128 * g, 128), 2)
            kb2 = wload(s, 8192, 2048, wbc[:, :, 128:256], wcols(win_d, OFF_C + 128 * g, 128), 3)
            kbc = kb1
            gch = [4 * g + j for j in range(4)] + [16 + g, 20 + g]
            hk_ = ("hist", g)
            raw2, rawS, acc, xc, sz = T["raw2"], T["rawS"], [T["acc0"], T["acc1"]], T["xc"], T["sz"]
            Rt, Et, cbm = T["Rt"], T["Et"], T["cbm"]
            ytmp, yt2, ytm, t4, t3, ygf, sqs, hs = T["ytmp"], T["yt2"], T["ytm"], T["t4"], T["t3"], T["ygf"], T["sqs"], T["hs"]
            hbs, ctm, btmm = T["hbs"], T["ctm"], T["btmm"]
            for ti, (t0, tn, is_s) in enumerate(TT[:DBG["ssm_tiles"]]):
                for j in range(6):
                    cj = gch[j]
                    wv_ = wx[:, :, j * 128:(j + 1) * 128] if j < 4 else wbc[:, :, (j - 4) * 128:(j - 3) * 128]
                    wkeys = kx if j < 4 else kbc
                    b = 6 + j % 2
                    rw = raw2[:, j % 2, :]
                    rwk = KR("raw2", 520 * (j % 2), 520)
                    for k in range(KD):
                        MM(ps[b][:, 0:tn], wv_[:, k, :], uT[:, k, t0:t0 + tn], k == 0, k == KD - 1, wkeys + [UK(ti, k)], [PK[b]])
                    a_ = acc[j % 2]
                    akk = K["acc%d" % (j % 2)]
                    if not is_s:
                        CP(rw[:, 0:3], hist[:, 6 * g + j, :], [hk_], rwk)
                        ACT(rw[:, 3:3 + tn], ps[b][:, 0:tn], AF.Copy, [PK[b]], rwk)
                        CP(hist[:, 6 * g + j, :], rw[:, tn:tn + 3], rwk, [hk_])
                        if last_pass:
                            CP(convp_st[:, cj, :], rw[:, tn:tn + 3], rwk, ["convp_st"])
                        win = lambda kk_: rw[:, kk_:kk_ + tn]
                        av = a_[:, 0:tn]
                        xo = xc[:, j, 0:tn]
                        rdk = rwk
                    else:
                        CP(rawS[:, j, :, 0:3], cvin[:, :, cj, :], ["cvin"], K["rawS"])
                        ACT(rawS[:, j, :, 3:7], ps[b][:, 0:tn].rearrange("p (s t) -> p s t", t=DEC_SEQ), AF.Copy, [PK[b]], K["rawS"])
                        CP(cvout[:, :, cj, :], rawS[:, j, :, 4:7], K["rawS"], ["cvout"])
                        win = lambda kk_: rawS[:, j, :, kk_:kk_ + DEC_SEQ]
                        av = a_[:, 0:tn].rearrange("p (s t) -> p s t", t=DEC_SEQ)
                        xo = xc[:, j, 0:tn].rearrange("p (s t) -> p s t", t=DEC_SEQ)
                        rdk = K["rawS"]
                    if DBG["jcut"] <= 1:
                        continue
                    TS(av, win(0), convw[:, cj, 0:1], convb[:, cj:cj + 1], ALU.mult, ALU.add, rdk + ["convw", "convb"], akk)
                    if DBG["jcut"] <= 2:
                        continue
                    for kk_ in range(1, 4):
                        STT(av, win(kk_), convw[:, cj, kk_:kk_ + 1], av, ALU.mult, ALU.add, rdk + ["convw"] + akk, akk)
                    if DBG["jcut"] <= 3:
                        continue
                    ACT(xo, av, AF.Silu, akk, KR("xc", 256 * j, 256))
                for j in range(4):
                    b = 6 + j % 2
                    for k in range(KD):
                        MM(ps[b][:, 0:tn], wz[:, k, j * 128:(j + 1) * 128], uT[:, k, t0:t0 + tn], k == 0, k == KD - 1, kz + [UK(ti, k)], [PK[b]])
                    ACT(sz[:, j, 0:tn], ps[b][:, 0:tn], AF.Silu, [PK[b]], KR("sz", 256 * j, 256))
                hsl = slice(8 * g, 8 * g + 8)

                def phaseA(ch, par):
                    (c0, Tc, ci, tok0, si) = ch
                    bat = si is not None
                    TRI, USTR = (tri_s, ustr_s) if bat else (tri, ustr)
                    trk, usk = ("tri_s", "ustr_s") if bat else ("tri", "ustr")
                    xdt, xdtt, btm, wT, sm8, decs = (T["xdt%d" % par], T["xdtt%d" % par], T["btm%d" % par], T["wT%d" % par],
                                                     T["sm8%d" % par], T["decs%d" % par])
                    kxdt, kxdtt, kbtm, kwT, k8, kdecs = (K["xdt%d" % par], K["xdtt%d" % par], K["btm%d" % par], K["wT%d" % par],
                                                         K["sm8%d" % par], K["decs%d" % par])
                    la_sb, expla, lal_sb, dec_b, tail = sm8[:, 0, :], sm8[:, 1, :], sm8[:, 2, :], sm8[:, 3, :], sm8[:, 4, :]
                    k_la, k_ex, k_lal, k_dec, k_tl = [KR("sm8%d" % par, 8 * i_, 8) for i_ in range(5)]
                    tp = ps[0][:, :].bitcast(BF16)
                    for j in range(5):
                        TR(tp[0:Tc, j * 128:(j + 1) * 128], xc[:, j, c0:c0 + Tc], identb[:, :], KR("xc", 256 * j, 256) + ["identb"], [PK[0]])
                    dts = dt_tm[0:Tc, ci, hsl]
                    TT_(xdt[0:Tc, :].rearrange("p (h d) -> p h d", h=8), tp[0:Tc, 0:512].rearrange("p (h d) -> p h d", h=8),
                        dts.unsqueeze(2).to_broadcast([Tc, 8, 64]), ALU.mult, [PK[0], ("dt", ci)], kxdt)
                    CP(btm[0:Tc, :], tp[0:Tc, 512:640], [PK[0]], kbtm)
                    dtas = dta_tm[0:Tc, ci, hsl]
                    MM(ps[1][0:Tc, 0:8], TRI[0:Tc, 0:Tc], dtas, True, True, [trk, ("dta", ci)], [PK[1]])
                    if not bat:
                        MM(ps[1][:, 8:16], onesf[0:Tc, :], dtas, True, True, ["onesf", ("dta", ci)], [PK[1]])
                    else:
                        MM(ps[1][0:Tc, 8:16], blk_s[0:Tc, 0:Tc], dtas, True, True, ["blk_s", ("dta", ci)], [PK[1]])
                        for i in range(NS):
                            MM(ps[1][:, 16 + 8 * i:24 + 8 * i], selB[0:Tc, i, :], dtas, True, True, ["selB", ("dta", ci)], [PK[1]])
                    MM(ps[1][0:Tc, 128:128 + Tc], xc[:, 4, c0:c0 + Tc], xc[:, 5, c0:c0 + Tc], True, True, KR("xc", 1024, 512), [PK[1]])
                    ACT(la_sb[0:Tc, :], ps[1][0:Tc, 0:8], AF.Copy, [PK[1]], k_la)
                    ACT(expla[0:Tc, :], ps[1][0:Tc, 0:8], AF.Exp, [PK[1]], k_ex)
                    if not bat:
                        ACT(lal_sb[:, :], ps[1][:, 8:16], AF.Copy, [PK[1]], k_lal)
                        ACT(dec_b[:, :], ps[1][:, 8:16], AF.Exp, [PK[1]], k_dec)
                    else:
                        ACT(lal_sb[0:Tc, :], ps[1][0:Tc, 8:16], AF.Copy, [PK[1]], k_lal)
                        ACT(decs[:, :], ps[1][:, 16:16 + 8 * NS], AF.Exp, [PK[1]], kdecs)
                    TT_(cbm[0:Tc, 0:Tc], ps[1][0:Tc, 128:128 + Tc], TRI[0:Tc, 0:Tc], ALU.mult, [PK[1], trk], K["cbm"])
                    TT_(tail[0:Tc, :], lal_sb[0:Tc, :], la_sb[0:Tc, :], ALU.subtract, k_lal + k_la, k_tl)
                    ACT(tail[0:Tc, :], tail[0:Tc, :], AF.Exp, k_tl, k_tl)
                    TT_(xdtt[0:Tc, :].rearrange("p (h d) -> p h d", h=8), xdt[0:Tc, :].rearrange("p (h d) -> p h d", h=8),
                        tail[0:Tc, :].unsqueeze(2).to_broadcast([Tc, 8, 64]), ALU.mult, kxdt + k_tl, kxdtt)
                    for hf in range(2):
                        TT_(Rt[0:Tc, :, 0:Tc], dta_tm[0:Tc, ci, 8 * g + 4 * hf:8 * g + 4 * hf + 4].unsqueeze(2).to_broadcast([Tc, 4, Tc]),
                            TRI[0:Tc, 0:Tc].unsqueeze(1).to_broadcast([Tc, 4, Tc]), ALU.mult, [("dta", ci), trk], K["Rt"])
                        segv = ps[2 + hf][0:Tc, 0:4 * Tc].rearrange("p (h t) -> p h t", h=4)
                        MM(segv, USTR[0:Tc, 0:Tc], Rt[0:Tc, :, 0:Tc], True, True, [usk] + K["Rt"], [PK[2 + hf]])
                        ACT(Et[0:Tc, :, 0:Tc], segv, AF.Exp, [PK[2 + hf]], K["Et"])
                        TT_(wT[0:Tc, 4 * hf:4 * hf + 4, 0:Tc], Et[0:Tc, :, 0:Tc], cbm[0:Tc, 0:Tc].unsqueeze(1).to_broadcast([Tc, 4, Tc]), ALU.mult,
                            K["Et"] + K["cbm"], kwT)

                def phaseB(ch, par):
                    (c0, Tc, ci, tok0, si) = ch
                    bat = si is not None
                    xdt, xdtt, btm, wT, sm8, decs = (T["xdt%d" % par], T["xdtt%d" % par], T["btm%d" % par], T["wT%d" % par],
                                                     T["sm8%d" % par], T["decs%d" % par])
                    kxdt, kxdtt, kbtm, kwT, k8, kdecs = (K["xdt%d" % par], K["xdtt%d" % par], K["btm%d" % par], K["wT%d" % par],
                                                         K["sm8%d" % par], K["decs%d" % par])
                    expla, dec_b = sm8[:, 1, :], sm8[:, 3, :]
                    k_la, k_ex, k_lal, k_dec, k_tl = [KR("sm8%d" % par, 8 * i_, 8) for i_ in range(5)]
                    if not bat:
                        st, stk = hst[:, g, :], [("hst", g)]
                        ACT(hbf[:, :], st, AF.Copy, stk, ["hbf"])
                    else:
                        s0 = p * NS
                        stk = K["hs"]
                        DMA(hs[:, :, :], ssm_in_d[s0:s0 + NS, :, 512 * g:512 * g + 512].rearrange("s n f -> n s f"), (), stk, semkey="hsld")
                        ACT(hbs[:, :, :], hs[:, :, :], AF.Copy, stk, K["hbs"])
                    for hh in range(8):
                        MM(ps[4][0:Tc, hh * 64:(hh + 1) * 64], wT[0:Tc, hh, 0:Tc], xdt[0:Tc, hh * 64:(hh + 1) * 64], True, True,
                           kwT + kxdt, [PK[4]])
                    if not bat:
                        MM(ps[5][0:Tc, 0:512], xc[:, 5, c0:c0 + Tc], hbf[:, :], True, True, KR("xc", 1280, 256) + ["hbf"], [PK[5]])
                    else:
                        TT_(ctm[:, :, 0:Tc], xc[:, 5, c0:c0 + Tc].unsqueeze(1).to_broadcast([128, NS, Tc]), csel[:, :, 0:Tc], ALU.mult,
                            KR("xc", 1280, 256) + ["csel"], K["ctm"])
                        for i in range(NS):
                            MM(ps[5][0:Tc, 0:512], ctm[:, i, 0:Tc], hbs[:, i, :], i == 0, i == NS - 1, K["ctm"] + K["hbs"], [PK[5]])
                    ACT(ytmp[0:Tc, :], ps[4][0:Tc, :], AF.Copy, [PK[4]], K["ytmp"])
                    TT_(yt2[0:Tc, :].rearrange("p (h d) -> p h d", h=8), ps[5][0:Tc, :].rearrange("p (h d) -> p h d", h=8),
                        expla[0:Tc, :].unsqueeze(2).to_broadcast([Tc, 8, 64]), ALU.mult, [PK[5]] + k_ex, K["yt2"])
                    TT_(ytm[0:Tc, :], ytmp[0:Tc, :], yt2[0:Tc, :], ALU.add, K["ytmp"] + K["yt2"], K["ytm"])
                    if not bat:
                        MM(ps[7][:, 0:512], btm[0:Tc, :], xdtt[0:Tc, :], True, True, kbtm + kxdtt, [PK[7]])
                        TT_(t4[:, :].rearrange("p (h d) -> p h d", h=8), st.rearrange("p (h d) -> p h d", h=8),
                            dec_b[:, :].unsqueeze(2).to_broadcast([128, 8, 64]), ALU.mult, stk + k_dec, K["t4"])
                        TT_(st, t4[:, :], ps[7][:, 0:512], ALU.add, K["t4"] + [PK[7]], stk)
                    else:
                        TT_(btmm[0:Tc, :, :], btm[0:Tc, :].unsqueeze(1).to_broadcast([Tc, NS, 128]),
                            seqsel[0:Tc, :].unsqueeze(2).to_broadcast([Tc, NS, 128]), ALU.mult, kbtm + ["seqsel"], K["btmm"])
                        for i in range(NS):
                            b = 7 if i % 2 == 0 else 5
                            MM(ps[b][:, 0:512], btmm[0:Tc, i, :], xdtt[0:Tc, :], True, True, K["btmm"] + kxdtt, [PK[b]])
                            TT_(t4[:, :].rearrange("p (h d) -> p h d", h=8), hs[:, i, :].rearrange("p (h d) -> p h d", h=8),
                                decs[:, 8 * i:8 * i + 8].unsqueeze(2).to_broadcast([128, 8, 64]), ALU.mult, stk + kdecs, K["t4"])
                            TT_(hs[:, i, :], t4[:, :], ps[b][:, 0:512], ALU.add, K["t4"] + [PK[b]], stk)
                        DMA(ssm_s_d[s0:s0 + NS, :, 512 * g:512 * g + 512].rearrange("s n f -> n s f"), hs[:, :, :], stk, [("ssms", p, g)], semkey="hsst")
                    for j in range(4):
                        TR(ps[6][:, j * Tc:(j + 1) * Tc], ytm[0:Tc, j * 128:(j + 1) * 128], identf[0:Tc, 0:Tc], K["ytm"] + ["identf"], [PK[6]])
                    TT_(t3[:, :, 0:Tc], xc[:, 0:4, c0:c0 + Tc], dpp[:, 4 * g:4 * g + 4].unsqueeze(2).to_broadcast([128, 4, Tc]), ALU.mult,
                        KR("xc", 0, 1024) + ["dpp"], K["t3"])
                    TT_(t3[:, :, 0:Tc], t3[:, :, 0:Tc], ps[6][:, 0:4 * Tc].rearrange("p (j t) -> p j t", j=4), ALU.add, K["t3"] + [PK[6]], K["t3"])
                    TT_(ygf[:, :, 0:Tc], t3[:, :, 0:Tc], sz[:, :, c0:c0 + Tc], ALU.mult, K["t3"] + K["sz"], K["ygf"])
                    ACT(sqs[:, :, 0:Tc], ygf[:, :, 0:Tc], AF.Square, K["ygf"], K["sqs"])
                    TT_(yg[:, 4 * g:4 * g + 4, tok0:tok0 + Tc], ygf[:, :, 0:Tc],
                        gssm[:, 4 * g:4 * g + 4].unsqueeze(2).to_broadcast([128, 4, Tc]), ALU.mult, K["ygf"] + ["gssm"], [("yg", ti)])
                    for j in range(4):
                        MM(ps[4][:, 0:Tc], onesb[:, :], sqs[:, j, 0:Tc], j == 0, j == 3, ["onesb"] + K["sqs"], [PK[4]])
                    if g == 0:
                        CP(ssq_b[:, tok0:tok0 + Tc], ps[4][:, 0:Tc], [PK[4]], [("ssq", ti)])
                    else:
                        TT_(ssq_b[:, tok0:tok0 + Tc], ssq_b[:, tok0:tok0 + Tc], ps[4][:, 0:Tc], ALU.add, [PK[4], ("ssq", ti)], [("ssq", ti)])

                chs = chunks_of_tile(ti)
                phaseA(chs[0], 0)
                for idx in range(1, len(chs)):
                    par2(lambda: phaseA(chs[idx], idx % 2), lambda: phaseB(chs[idx - 1], (idx - 1) % 2))
                phaseB(chs[-1], (len(chs) - 1) % 2)
            if last_pass:
                DMA(ssm_p_d[:, 512 * g:512 * g + 512], hst[:, g, :], [("hst", g)], [("ssmp", g)], semkey=("ssmp", g))

        def finalize(T, wb_d, gate_off, use_rs):
            mbf = T["mbf"]
            mk = K["mbf"]
            for ti, (t0, tn, _) in enumerate(TT):
                if use_rs:
                    ACT(ssq_b[:, t0:t0 + tn], ssq_b[:, t0:t0 + tn], AF.Sqrt, [("ssq", ti), "epsc"], [("ssq", ti)], bias=epsc[:, 1:2], scale=1.0 / 2048)
                    RECIP(ssq_b[:, t0:t0 + tn], ssq_b[:, t0:t0 + tn], [("ssq", ti)], [("ssq", ti)])
            for half in range(2):
                s = next_slot()
                wb = WA[:, s, 0:8192].rearrange("p (c n) -> p c n", c=16)
                wg = WA[:, s, 8192:12288].rearrange("p (k n) -> p k n", k=KD)
                kb_ = wload(s, 0, 8192, wb, wb_d.rearrange("(c p) n -> p c n", p=128)[:, :, half * 512:(half + 1) * 512], 0)
                kg_ = wload(s, 8192, 4096, wg, wcols(win_d, gate_off + half * 512, 512), 1)
                for mi in range(4):
                    m = 4 * half + mi
                    for ti, (t0, tn, _) in enumerate(TT):
                        pa, pg = (0, 1) if (m + ti) % 2 == 0 else (4, 5)
                        for c in range(16):
                            MM(ps[pa][:, 0:tn], wb[:, c, mi * 128:(mi + 1) * 128], yg[:, c, t0:t0 + tn], c == 0, c == 15, kb_ + [("yg", ti)], [PK[pa]])
                        for k in range(KD):
                            MM(ps[pg][:, 0:tn], wg[:, k, mi * 128:(mi + 1) * 128], uT[:, k, t0:t0 + tn], k == 0, k == KD - 1, kg_ + [UK(ti, k)], [PK[pg]])
                        sg = T["sgt%d" % ((m + ti) % 2)]
                        sgk = K["sgt%d" % ((m + ti) % 2)]
                        ACT(sg[:, 0:tn], ps[pg][:, 0:tn], AF.Sigmoid, [PK[pg]], sgk)
                        if use_rs:
                            TT_(sg[:, 0:tn], sg[:, 0:tn], ssq_b[:, t0:t0 + tn], ALU.mult, sgk + [("ssq", ti)], sgk)
                        TT_(mbf[:, m, t0:t0 + tn], ps[pa][:, 0:tn], sg[:, 0:tn], ALU.mult, [PK[pa]] + sgk, mk)
            s = next_slot()
            wo = WA[:, s, 0:8192].rearrange("p (k n) -> p k n", k=KD)
            ko_ = wload(s, 0, 8192, wo, wcols(wout_d, 0, 1024), 0)
            for m in range(KD):
                for ti, (t0, tn, _) in enumerate(TT):
                    b = 2 + (m + ti) % 2
                    for k in range(KD):
                        MM(ps[b][:, 0:tn], wo[:, k, m * 128:(m + 1) * 128], mbf[:, k, t0:t0 + tn], k == 0, k == KD - 1, ko_ + mk, [PK[b]])
                    TT_(hT[:, m, t0:t0 + tn], hT[:, m, t0:t0 + tn], ps[b][:, 0:tn], ALU.add, [HK(ti, m), PK[b]], [HK(ti, m)])

        def ret_head(T, r, p, last_pass):
            s = next_slot()
            wq = WA[:, s, 0:2048].rearrange("p (k n) -> p k n", k=KD)
            wkk = WA[:, s, 2048:4096].rearrange("p (k n) -> p k n", k=KD)
            wv = WA[:, s, 4096:8192].rearrange("p (k n) -> p k n", k=KD)
            wg = WA[:, s, 8192:12288].rearrange("p (k n) -> p k n", k=KD)
            kq = wload(s, 0, 2048, wq, wcols(win_d, OFF_Q + 256 * r, 256), 0)
            kk_ = wload(s, 2048, 2048, wkk, wcols(win_d, OFF_K + 256 * r, 256), 1)
            kv = wload(s, 4096, 4096, wv, wcols(win_d, OFF_V + 512 * r, 512), 2)
            kg = wload(s, 8192, 4096, wg, wcols(win_d, OFF_G + 512 * r, 512), 3)
            gam = GAMMAS[r]
            qkf, rot, qT, qsT, kT, gsg = T["qkf"], T["rot"], T["qT"], T["qsT"], T["kT"], T["gsg"]
            yrt, sqs, rs, rsn, rbs = T["yrt"], T["sqs"], T["rs"], T["rsn"], T["rbs"]
            cosT, sinT, qdecT = T["cosT"], T["sinT"], T["qdecT"]
            DMA(qdecT[:, :], qdec_d[r, :, :], (), K["qdecT"], semkey="qdld")
            for ti, (t0, tn, is_s) in enumerate(TT):
                for qi, (wv_, wks) in enumerate(((wq, kq), (wkk, kk_))):
                    for dc in range(2):
                        b = 6 + dc
                        for k in range(KD):
                            MM(ps[b][:, 0:tn], wv_[:, k, dc * 128:(dc + 1) * 128], uT[:, k, t0:t0 + tn], k == 0, k == KD - 1, wks + [UK(ti, k)], [PK[b]])
                        ACT(qkf[:, qi, dc, 0:tn], ps[b][:, 0:tn], AF.Copy, [PK[b]], KR("qkf", 1024 * qi + 512 * dc, 512), scale=(1.0 if qi == 0 else 1.0 / 16.0))
                    t1, t2 = qkf[:, qi, 0, 0:tn], qkf[:, qi, 1, 0:tn]
                    cs, sn = cosT[:, t0:t0 + tn], sinT[:, t0:t0 + tn]
                    dst = qT if qi == 0 else kT
                    dk = K["qT"] if qi == 0 else K["kT"]
                    kq1, kq2 = KR("qkf", 1024 * qi, 512), KR("qkf", 1024 * qi + 512, 512)
                    kr = [KR("rot", 512 * i_, 512) for i_ in range(2)]
                    dk0, dk1 = KR("qT" if qi == 0 else "kT", 0, 256), KR("qT" if qi == 0 else "kT", 256, 256)
                    TT_(rot[:, 0, 0:tn], t1, cs, ALU.mult, kq1 + K["cosT"], kr[0])
                    TT_(rot[:, 1, 0:tn], t2, sn, ALU.mult, kq2 + K["sinT"], kr[1])
                    TT_(dst[:, 0, 0:tn], rot[:, 0, 0:tn], rot[:, 1, 0:tn], ALU.subtract, kr[0] + kr[1], dk0)
                    TT_(rot[:, 0, 0:tn], t1, sn, ALU.mult, kq1 + K["sinT"], kr[0])
                    TT_(rot[:, 1, 0:tn], t2, cs, ALU.mult, kq2 + K["cosT"], kr[1])
                    TT_(dst[:, 1, 0:tn], rot[:, 0, 0:tn], rot[:, 1, 0:tn], ALU.add, kr[0] + kr[1], dk1)
                TT_(qsT[:, :, 0:tn], qT[:, :, 0:tn], qdecT[:, t0:t0 + tn].unsqueeze(1).to_broadcast([128, 2, tn]), ALU.mult, K["qT"] + K["qdecT"], K["qsT"])
                for j in range(4):
                    b = 6 + j % 2
                    for k in range(KD):
                        MM(ps[b][:, 0:tn], wg[:, k, j * 128:(j + 1) * 128], uT[:, k, t0:t0 + tn], k == 0, k == KD - 1, kg + [UK(ti, k)], [PK[b]])
                    sg = T["sgt%d" % (j % 2)]
                    sgk = K["sgt%d" % (j % 2)]
                    ACT(sg[:, 0:tn], ps[b][:, 0:tn], AF.Silu, [PK[b]], sgk)
                    TS(gsg[:, j, 0:tn], sg[:, 0:tn], gret[:, 4 * r + j:4 * r + j + 1], None, ALU.mult, None, sgk + ["gret"], KR("gsg", 256 * j, 256))
                def rphaseA(ch, par):
                    (c0, Tc, ci, tok0, si) = ch
                    bat = si is not None
                    vtm, kd, sm, qsm, kdm = T["vtm%d" % par], T["kd%d" % par], T["sm%d" % par], T["qsm%d" % par], T["kdm%d" % par]
                    kvtm, kkd, ksm, kqsm, kkdm = K["vtm%d" % par], K["kd%d" % par], K["sm%d" % par], K["qsm%d" % par], K["kdm%d" % par]
                    kdc = kdec[0:Tc, 4 + r:5 + r] if bat else kdec[0:Tc, r:r + 1]
                    DM, dmk = (dmatS, "dmatS") if bat else (dmatT, "dmatT")
                    for k in range(KD):
                        MM(ps[4][0:Tc, 0:512], uT[:, k, tok0:tok0 + Tc], wv[:, k, :], k == 0, k == KD - 1, [UK(ti, k)] + kv, [PK[4]])
                    ACT(vtm[0:Tc, :], ps[4][0:Tc, :], AF.Copy, [PK[4]], kvtm)
                    tp = ps[0][:, :].bitcast(BF16)
                    for dc in range(2):
                        TR(tp[0:Tc, dc * 128:(dc + 1) * 128], kT[:, dc, c0:c0 + Tc], identb[:, :], K["kT"] + ["identb"], [PK[0]])
                    TS(kd[0:Tc, :], tp[0:Tc, 0:256], kdc, None, ALU.mult, None, [PK[0], "kdec"], kkd)
                    for dc in range(2):
                        MM(ps[1][0:Tc, 0:Tc], kT[:, dc, c0:c0 + Tc], qT[:, dc, c0:c0 + Tc], dc == 0, dc == 1, K["kT"] + K["qT"], [PK[1]])
                    TT_(sm[0:Tc, 0:Tc], ps[1][0:Tc, 0:Tc], DM[0:Tc, r, 0:Tc], ALU.mult, [PK[1], dmk], ksm)
                    if bat:
                        TT_(qsm[:, :, :, 0:Tc], qsT[:, :, c0:c0 + Tc].unsqueeze(2).to_broadcast([128, 2, NS, Tc]),
                            csel[:, :, 0:Tc].unsqueeze(1).to_broadcast([128, 2, NS, Tc]), ALU.mult, K["qsT"] + ["csel"], kqsm)
                        TT_(kdm[0:Tc, :, :], kd[0:Tc, :].unsqueeze(1).to_broadcast([Tc, NS, 256]),
                            seqsel[0:Tc, :].unsqueeze(2).to_broadcast([Tc, NS, 256]), ALU.mult, kkd + ["seqsel"], kkdm)

                def rphaseB(ch, par):
                    (c0, Tc, ci, tok0, si) = ch
                    bat = si is not None
                    vtm, kd, sm, qsm, kdm = T["vtm%d" % par], T["kd%d" % par], T["sm%d" % par], T["qsm%d" % par], T["kdm%d" % par]
                    kvtm, kkd, ksm, kqsm, kkdm = K["vtm%d" % par], K["kd%d" % par], K["sm%d" % par], K["qsm%d" % par], K["kdm%d" % par]
                    if not bat:
                        st, stk = rst[:, r, :, :], [("rst", r)]
                        cdec = gam ** 128
                        ACT(rbf[:, :, :], st, AF.Copy, stk, ["rbf"])
                    else:
                        s0 = p * NS
                        stk = K["rs"]
                        for i in range(NS):
                            DMA(rs[:, i, :, :], ret_in_d[s0 + i, r, :, :].rearrange("(c p) e -> p c e", p=128), (), stk, semkey=("rsld", i))
                        cdec = gam ** DEC_SEQ
                        ACT(rbs[:, :, :, :], rs[:, :, :, :], AF.Copy, stk, K["rbs"])
                    for ec in range(4):
                        o_ = ps[5][:, ec * Tc:(ec + 1) * Tc]
                        MM(o_, vtm[0:Tc, ec * 128:(ec + 1) * 128], sm[0:Tc, 0:Tc], True, False, kvtm + ksm, [PK[5]])
                        if not bat:
                            MM(o_, rbf[:, 0, ec * 128:(ec + 1) * 128], qsT[:, 0, c0:c0 + Tc], False, False, ["rbf"] + K["qsT"], [PK[5]])
                            MM(o_, rbf[:, 1, ec * 128:(ec + 1) * 128], qsT[:, 1, c0:c0 + Tc], False, True, ["rbf"] + K["qsT"], [PK[5]])
                        else:
                            for i in range(NS):
                                for dc in range(2):
                                    MM(o_, rbs[:, i, dc, ec * 128:(ec + 1) * 128], qsm[:, dc, i, 0:Tc], False, (i == NS - 1 and dc == 1),
                                       K["rbs"] + kqsm, [PK[5]])
                    if not bat:
                        for dc in range(2):
                            b = 2 + dc
                            MM(ps[b][:, 0:512], kd[0:Tc, dc * 128:(dc + 1) * 128], vtm[0:Tc, :], True, True, kkd + kvtm, [PK[b]])
                            STT(st[:, dc, :], st[:, dc, :], float(cdec), ps[b][:, 0:512], ALU.mult, ALU.add, stk + [PK[b]], stk)
                    else:
                        for i in range(NS):
                            for dc in range(2):
                                b = 2 + (2 * i + dc) % 2
                                MM(ps[b][:, 0:512], kdm[0:Tc, i, dc * 128:(dc + 1) * 128], vtm[0:Tc, :], True, True, kkdm + kvtm, [PK[b]])
                                STT(rs[:, i, dc, :], rs[:, i, dc, :], float(cdec), ps[b][:, 0:512], ALU.mult, ALU.add, stk + [PK[b]], stk)
                        for i in range(NS):
                            DMA(ret_s_d[s0 + i, r, :, :].rearrange("(c p) e -> p c e", p=128), rs[:, i, :, :], stk, [("rets", p, r, i)], semkey=("rsst", i))
                    yv = ps[5][:, 0:4 * Tc].rearrange("p (j t) -> p j t", j=4)
                    ACT(sqs[:, :, 0:Tc], yv, AF.Square, [PK[5]], K["sqs"])
                    for j in range(4):
                        MM(ps[6][:, 0:Tc], onesb[:, :], sqs[:, j, 0:Tc], j == 0, j == 3, ["onesb"] + K["sqs"], [PK[6]])
                    ACT(rsn[:, 0:Tc], ps[6][:, 0:Tc], AF.Sqrt, [PK[6], "epsc"], K["rsn"], bias=epsc[:, 0:1], scale=1.0 / 512)
                    RECIP(rsn[:, 0:Tc], rsn[:, 0:Tc], K["rsn"], K["rsn"])
                    TT_(yrt[:, :, 0:Tc], yv, rsn[:, 0:Tc].unsqueeze(1).to_broadcast([128, 4, Tc]), ALU.mult, [PK[5]] + K["rsn"], K["yrt"])
                    TT_(yg[:, 4 * r:4 * r + 4, tok0:tok0 + Tc], yrt[:, :, 0:Tc], gsg[:, :, c0:c0 + Tc], ALU.mult, K["yrt"] + K["gsg"], [("yg", ti)])

                chs = chunks_of_tile(ti)
                rphaseA(chs[0], 0)
                for idx in range(1, len(chs)):
                    par2(lambda: rphaseA(chs[idx], idx % 2), lambda: rphaseB(chs[idx - 1], (idx - 1) % 2))
                rphaseB(chs[-1], (len(chs) - 1) % 2)
            if last_pass:
                DMA(ret_p_d[r, :, :].rearrange("(c p) e -> p c e", p=128), rst[:, r, :, :], [("rst", r)], [("retp", r)], semkey=("retp", r))

        for p in range(DBG["passes"]):
            last = p == NPASS - 1
            for ti, (t0, tn, _) in enumerate(TT):
                DMA(hT[:, :, t0:t0 + tn], xT_d[p, :, :, t0:t0 + tn], (), [HK(ti, m) for m in range(KD)], semkey=("xin", ti))
            for si in range(NS):
                DMA(cvin[:, si, :, :], conv_in_d[p * NS + si, :, :, :], (), ["cvin"], semkey=("cvin", si))
            P.mark('p%d ffn1' % p)
            T = stage_begin()
            T["sq"] = A("sq", [128, KD, 512], BF16)
            T["aT0"] = A("aT0", [128, 4, 512], BF16)
            T["aT1"] = A("aT1", [128, 4, 512], BF16)
            norm_stage(T, 0)
            ffn_stage(T, 0)
            if DBG["stop"] <= 1:
                continue
            P.mark('p%d mixnorm' % p)
            T = stage_begin()
            T["sq"] = A("sq", [128, KD, 512], BF16)
            T["smallt"] = A("smallt", [128, 4, 32], F32)
            norm_stage(T, 1)
            dt_stage(T)
            if DBG["stop"] <= 2:
                continue
            P.mark('p%d ssm' % p)
            T = stage_begin()
            for nm, shp, dt_ in (("raw2", [128, 2, 520], F32), ("rawS", [128, 6, NS, 7], F32), ("acc0", [128, 512], F32),
                                 ("acc1", [128, 512], F32), ("xc", [128, 6, 512], BF16), ("sz", [128, 4, 512], BF16),
                                 ("xdt0", [128, 512], BF16), ("xdtt0", [128, 512], BF16), ("btm0", [128, 128], BF16),
                                 ("xdt1", [128, 512], BF16), ("xdtt1", [128, 512], BF16), ("btm1", [128, 128], BF16),
                                 ("sm80", [128, 5, 8], F32), ("sm81", [128, 5, 8], F32), ("wT0", [128, 8, 128], BF16),
                                 ("wT1", [128, 8, 128], BF16), ("decs0", [128, 8 * NS], F32), ("decs1", [128, 8 * NS], F32),
                                 ("Rt", [128, 4, 128], F32), ("Et", [128, 4, 128], F32),
                                 ("cbm", [128, 128], F32), ("ytmp", [128, 512], F32),
                                 ("yt2", [128, 512], F32), ("ytm", [128, 512], F32), ("t4", [128, 512], F32),
                                 ("t3", [128, 4, 128], F32), ("ygf", [128, 4, 128], F32), ("sqs", [128, 4, 128], BF16),
                                 ("hs", [128, NS, 512], F32), ("hbs", [128, NS, 512], BF16),
                                 ("ctm", [128, NS, 32], BF16), ("btmm", [128, NS, 128], BF16)):
                T[nm] = A(nm, shp, dt_)
            T["mbf"] = A("mbf", [128, KD, NTP], BF16, alias="hs")
            for g in range(DBG["groups"]):
                ssm_group(T, g, p, last)
            if last:
                DMA(conv_p_d[:, :, :], convp_st[:, :, :], ["convp_st"], ["convp_out"], semkey="convp_out")
            if DBG["stop"] <= 3:
                continue
            for si in range(NS):
                DMA(conv_s_d[p * NS + si, :, :, :], cvout[:, si, :, :], ["cvout"], [("convs", p, si)], semkey=("cvout", si))
            P.mark('p%d fin_ssm' % p)
            finalize(T, wbs_d, OFF_GA, True)
            if DBG["stop"] <= 4:
                continue
            P.mark('p%d ret' % p)
            T = stage_begin()
            for nm, shp, dt_ in (("qkf", [128, 2, 2, 512], F32), ("rot", [128, 2, 512], F32), ("qT", [128, 2, 512], BF16),
                                 ("qsT", [128, 2, 512], BF16), ("kT", [128, 2, 512], BF16), ("gsg", [128, 4, 512], BF16),
                                 ("vtm0", [128, 512], BF16), ("kd0", [128, 256], BF16), ("sm0", [128, 128], BF16),
                                 ("vtm1", [128, 512], BF16), ("kd1", [128, 256], BF16), ("sm1", [128, 128], BF16),
                                 ("qsm0", [128, 2, NS, 32], BF16), ("kdm0", [128, NS, 256], BF16),
                                 ("qsm1", [128, 2, NS, 32], BF16), ("kdm1", [128, NS, 256], BF16),
                                 ("yrt", [128, 4, 128], F32), ("sqs", [128, 4, 128], BF16), ("rs", [128, NS, 2, 512], F32),
                                 ("rbs", [128, NS, 2, 512], BF16),
                                 ("cosT", [128, NTP], F32), ("sinT", [128, NTP], F32), ("qdecT", [128, NTP], F32)):
                T[nm] = A(nm, shp, dt_)
            T["mbf"] = A("mbf", [128, KD, NTP], BF16, alias="qkf")
            DMA(T["cosT"][:, :], cos_d[p, :, :], (), K["cosT"], semkey="cosld")
            DMA(T["sinT"][:, :], sin_d[p, :, :], (), K["sinT"], semkey="sinld")
            for r in range(4):
                ret_head(T, r, p, last)
            if DBG["stop"] <= 5:
                continue
            P.mark('p%d fin_ret' % p)
            finalize(T, wbr_d, OFF_GB, False)
            if DBG["stop"] <= 6:
                continue
            P.mark('p%d ffn2' % p)
            T = stage_begin()
            T["sq"] = A("sq", [128, KD, 512], BF16)
            T["aT0"] = A("aT0", [128, 4, 512], BF16)
            T["aT1"] = A("aT1", [128, 4, 512], BF16)
            T["outT"] = A("outT", [128, KD, 512], F32)
            norm_stage(T, 2)
            ffn_stage(T, 1)
            norm_stage(T, 3, out_dma=yT_d[p])
        P.emit()
        build_nc.stats = P.stats
        build_nc.marks = P.marks
    return nc


def _host_consts():
    half = 128
    inv = (10000.0 ** (-np.arange(half, dtype=np.float32) / np.float32(half))).astype(np.float32)
    cosT = np.zeros((NPASS, 128, NTP), np.float32)
    sinT = np.zeros((NPASS, 128, NTP), np.float32)
    for p in range(NPASS):
        pos = np.concatenate([np.arange(p * NPT, (p + 1) * NPT, dtype=np.float32),
                              np.tile(PAST + np.arange(DEC_SEQ, dtype=np.float32), NS)]).astype(np.float32)
        ang = (pos[None, :] * inv[:, None]).astype(np.float32)
        cosT[p] = np.cos(ang.astype(np.float64)).astype(np.float32)
        sinT[p] = np.sin(ang.astype(np.float64)).astype(np.float32)
    lg = np.log1p(-np.exp2(-5.0 - np.arange(4, dtype=np.float64)))
    inchunk = np.concatenate([np.arange(NPT) % 128, np.tile(np.arange(DEC_SEQ), NS)]).astype(np.float64)
    qdecT = np.zeros((4, 128, NTP), np.float32)
    for h in range(4):
        qdecT[h] = np.exp((inchunk + 1.0) * lg[h])[None, :]
    kdec = np.zeros((128, 8), np.float32)
    for h in range(4):
        kdec[:, h] = np.exp((127.0 - np.arange(128)) * lg[h])
        kdec[0:DEC_SEQ, 4 + h] = np.exp((DEC_SEQ - 1.0 - np.arange(DEC_SEQ)) * lg[h])
    dmatT = np.zeros((128, 4, 128), np.float32)
    s_ = np.arange(128)[:, None]
    t_ = np.arange(128)[None, :]
    for h in range(4):
        dmatT[:, h, :] = np.where(t_ >= s_, np.exp(np.maximum(t_ - s_, 0) * lg[h]), 0.0)
    tri = (s_ <= t_).astype(np.float32)
    ustrict = (s_ > t_).astype(np.float32)
    identf = np.eye(128, dtype=np.float32)
    seq_of = np.arange(128) // DEC_SEQ
    same = (seq_of[:, None] == seq_of[None, :])
    tri_s = (same & (s_ <= t_)).astype(np.float32)
    ustr_s = (same & (s_ > t_)).astype(np.float32)
    blk_s = same.astype(np.float32)
    seqsel = (seq_of[:, None] == np.arange(NS)[None, :]).astype(np.float32)
    selB = np.ascontiguousarray(np.broadcast_to(seqsel[:, :, None], (128, NS, 128))).astype(np.float32)
    csel = np.ascontiguousarray(np.broadcast_to((np.arange(NS)[:, None] == seq_of[None, :32])[None], (128, NS, 32))).astype(np.float32)
    dmatS = np.zeros((128, 4, 128), np.float32)
    for h in range(4):
        dmatS[:, h, :] = np.where(same & (t_ >= s_), np.exp(np.maximum(t_ - s_, 0) * lg[h]), 0.0)
        kdec[:, 4 + h] = np.exp((DEC_SEQ - 1.0 - (np.arange(128) % DEC_SEQ)) * lg[h])
    return dict(cosT=cosT, sinT=sinT, qdecT=qdecT, kdec=kdec, dmatT=dmatT, tri=tri, ustrict=ustrict, identf=identf,
                tri_s=tri_s, ustr_s=ustr_s, blk_s=blk_s, seqsel=seqsel, selB=selB, csel=csel, dmatS=dmatS)


_NC_CACHE = {}
_PREP_ONLY = [False]


def kernel(x_prompt, x_sample, state_ssm, state_conv, state_ret,
           norm_ffn1, ffn1_w1, ffn1_w3, ffn1_w2, norm_mix, w_in, conv_w, conv_b,
           dt_bias, a_log, ssm_d, ssm_norm, ret_norm, w_branch_ssm, w_branch_ret, w_out,
           norm_ffn2, ffn2_w1, ffn2_w3, ffn2_w2, norm_final):
    f = lambda a: np.ascontiguousarray(np.asarray(a, dtype=np.float32))
    x_prompt, x_sample = f(x_prompt), f(x_sample)
    state_ssm, state_conv, state_ret = f(state_ssm)[0], f(state_conv)[0], f(state_ret)[0]
    SPC = DEC_B // NCORES
    consts = _host_consts()
    pp = lambda v: f(np.asarray(v).reshape(-1, 128).T)
    bc = lambda v: f(np.broadcast_to(np.asarray(v).reshape(1, -1), (128, np.asarray(v).size)))
    shared = dict(
        f1w1=f(ffn1_w1)[0], f1w3=f(ffn1_w3)[0], f1w2=f(ffn1_w2)[0],
        f2w1=f(ffn2_w1)[0], f2w3=f(ffn2_w3)[0], f2w2=f(ffn2_w2)[0],
        w_in=f(w_in)[0], w_bs=f(w_branch_ssm)[0], w_br=f(w_branch_ret)[0], w_out=f(w_out)[0],
        gains=f(np.stack([pp(norm_ffn1), pp(norm_mix), pp(norm_ffn2), pp(norm_final)], axis=1)),
        g_ssm=pp(ssm_norm), g_ret=pp(ret_norm),
        convw=f(np.asarray(conv_w)[0].reshape(4, 24, 128).transpose(2, 1, 0)),
        convb=pp(conv_b), dtb=bc(dt_bias), alog=bc(a_log),
        dpp=f(np.repeat(np.asarray(ssm_d).reshape(-1), 64).reshape(16, 128).T),
        **consts,
    )
    in_maps = []
    for c in range(NCORES):
        xs = x_sample[c * SPC:(c + 1) * SPC]
        xT = np.zeros((NPASS, 128, KD, NTP), np.float32)
        for p in range(NPASS):
            tok = np.concatenate([x_prompt[c, p * NPT:(p + 1) * NPT], xs[p * NS:(p + 1) * NS].reshape(NS * DEC_SEQ, D)], axis=0)
            xT[p] = tok.T.reshape(KD, 128, NTP).transpose(1, 0, 2)
        m = dict(shared)
        m["xT"] = xT
        m["ssmT_in"] = f(state_ssm[c * SPC:(c + 1) * SPC].reshape(SPC, 2048, 128).transpose(0, 2, 1))
        m["conv_in"] = f(state_conv[c * SPC:(c + 1) * SPC].reshape(SPC, 3, 24, 128).transpose(0, 3, 2, 1))
        m["ret_in"] = f(state_ret[c * SPC:(c + 1) * SPC])
        in_maps.append(m)
    if _PREP_ONLY[0]:
        return in_maps
    if "nc" not in _NC_CACHE:
        _NC_CACHE["nc"] = build_nc()
    nc = _NC_CACHE["nc"]
    res = run_bass_kernel_spmd(nc, in_maps, core_ids=list(range(NCORES)))
    return _post(res.results)


def _post(R):
    SPC = DEC_B // NCORES
    y_prompt = np.zeros((NCORES, SEQ, D), np.float32)
    y_sample = np.zeros((DEC_B, DEC_SEQ, D), np.float32)
    ssm_p = np.zeros((1, NCORES, 32, 64, 128), np.float32)
    conv_p = np.zeros((1, NCORES, 3, 3072), np.float32)
    ret_p = np.zeros((1, NCORES, 4, 256, 512), np.float32)
    ssm_s = np.zeros((1, DEC_B, 32, 64, 128), np.float32)
    conv_s = np.zeros((1, DEC_B, 3, 3072), np.float32)
    ret_s = np.zeros((1, DEC_B, 4, 256, 512), np.float32)
    for c in range(len(R)):
        r = R[c]
        yT = r["yT"]
        for p in range(NPASS):
            tok = yT[p].transpose(1, 0, 2).reshape(D, NTP).T
            y_prompt[c, p * NPT:(p + 1) * NPT] = tok[:NPT]
            y_sample[c * SPC + p * NS:c * SPC + (p + 1) * NS] = tok[NPT:].reshape(NS, DEC_SEQ, D)
        ssm_p[0, c] = r["ssmT_p"].T.reshape(32, 64, 128)
        conv_p[0, c] = r["conv_p"].transpose(2, 1, 0).reshape(3, 3072)
        ret_p[0, c] = r["ret_p"]
        ssm_s[0, c * SPC:(c + 1) * SPC] = r["ssmT_s"].transpose(0, 2, 1).reshape(SPC, 32, 64, 128)
        conv_s[0, c * SPC:(c + 1) * SPC] = r["conv_s"].transpose(0, 3, 2, 1).reshape(SPC, 3, 3072)
        ret_s[0, c * SPC:(c + 1) * SPC] = r["ret_s"]
    return (y_prompt, y_sample, ssm_p, conv_p, ret_p, ssm_s, conv_s, ret_s)
```

```python
import math
import contextlib
import numpy as np
import concourse.bass as bass
import concourse.mybir as mybir
from concourse.bass_utils import run_bass_kernel_spmd

F32 = mybir.dt.float32
BF16 = mybir.dt.bfloat16
AF = mybir.ActivationFunctionType
ALU = mybir.AluOpType

NCORES = 8
D = 1024
KD = 8
SEQ = 2048
DEC_B = 128
DEC_SEQ = 4
PAST = 16384
DFF = 2816
IN_DIM = 13344
OFF_Z, OFF_X, OFF_B, OFF_C, OFF_DT = 0, 2048, 4096, 4608, 5120
OFF_Q, OFF_K, OFF_V, OFF_G, OFF_GA, OFF_GB = 5152, 6176, 7200, 9248, 11296, 12320
NPASS = 4
PCH = (SEQ // 128) // NPASS
NS = (DEC_B // NCORES) // NPASS
NPT = PCH * 128
NTP = NPT + NS * DEC_SEQ
NCH = PCH + 1
EPS = 1e-6
EPS_G = 1e-5
GAMMAS = [1.0 - 2.0 ** (-5.0 - h) for h in range(4)]

SAME_ENG_DIST = 4
DBG = {"passes": 4, "stop": 99, "ssm_tiles": 99, "cut": 99, "groups": 4, "jcut": 99, "sub": 99}
STRICT_SAME_ENGINE = True
SEM_GEN_LIMIT = 12000


class Prog:
    ENGS = ("pe", "act", "dve", "pool", "sp")

    def __init__(self, nc):
        self.nc = nc
        self.ops = []
        self.marks = []

    def mark(self, label):
        self.marks.append((label, sum(1 for o in self.ops if o['eng'] == 'pe')))

    def op(self, eng, fn, rd=(), wr=(), dma=False, semkey=None):
        def _flat(ks):
            out = []
            for k_ in ks:
                if isinstance(k_, list):
                    out.extend(_flat(k_))
                else:
                    out.append(k_)
            return tuple(out)
        rd, wr = _flat(rd), _flat(wr)
        self.ops.append(dict(eng=eng, fn=fn, rd=rd, wr=wr, dma=dma, semkey=semkey))

    def emit(self, final_wait_eng="sp"):
        nc = self.nc
        ops = self.ops
        n = len(ops)
        last_w = {}
        readers = {}
        bank_last = {}
        eng_pos = {e: 0 for e in self.ENGS}
        for o in ops:
            o["pos"] = eng_pos[o["eng"]]
            eng_pos[o["eng"]] += 1
        deps = [None] * n
        for i, o in enumerate(ops):
            d = {}
            for k in o["rd"]:
                if k in last_w:
                    d[last_w[k]] = "raw"
            for k in o["wr"]:
                if k in last_w:
                    d.setdefault(last_w[k], "waw")
                for r in readers.get(k, ()):
                    if r != i:
                        d.setdefault(r, "war")
            for k in set(o["rd"]) | set(o["wr"]):
                if isinstance(k, tuple) and k and k[0] == "ps":
                    la = bank_last.setdefault(k, {})
                    for e2, j2 in la.items():
                        if e2 != o["eng"]:
                            d.setdefault(j2, "xeng")
                    la[o["eng"]] = i
            for k in o["rd"]:
                readers.setdefault(k, []).append(i)
            for k in o["wr"]:
                last_w[k] = i
                readers[k] = []
            keep = []
            for j, kind in d.items():
                pj = ops[j]
                if pj["eng"] == o["eng"] and not pj["dma"]:
                    if o["dma"]:
                        keep.append(j)
                    elif o["eng"] == "pe":
                        pass
                    elif STRICT_SAME_ENGINE:
                        keep.append(j)
                    elif kind == "raw" and (o["pos"] - pj["pos"]) <= SAME_ENG_DIST:
                        keep.append(j)
                else:
                    keep.append(j)
            deps[i] = keep
        signaling = [False] * n
        for i in range(n):
            for j in deps[i]:
                signaling[j] = True
        sems = []

        def new_sem(name):
            s = nc.alloc_semaphore(name)
            sems.append(s)
            return s

        eng_sem, eng_cnt, dma_sem, dma_cnt = {}, {}, {}, {}
        sig = [None] * n
        for i, o in enumerate(ops):
            if o["dma"]:
                k = o["semkey"] if o["semkey"] is not None else (o["wr"][0] if o["wr"] else ("dma", i))
                if k not in dma_sem:
                    dma_sem[k] = new_sem("d%d" % len(sems))
                    dma_cnt[k] = 0
                dma_cnt[k] += 16
                sig[i] = (dma_sem[k], dma_cnt[k])
            elif signaling[i]:
                e = o["eng"]
                if e not in eng_sem or eng_cnt[e] >= SEM_GEN_LIMIT:
                    eng_sem[e] = new_sem("e%s%d" % (e, len(sems)))
                    eng_cnt[e] = 0
                eng_cnt[e] += 1
                sig[i] = (eng_sem[e], eng_cnt[e])
        waited = {e: {} for e in self.ENGS}
        waits = [None] * n
        for i, o in enumerate(ops):
            need = {}
            for j in deps[i]:
                s, v = sig[j]
                key = id(s)
                if key not in need or need[key][1] < v:
                    need[key] = (s, v)
            w = []
            for key, (s, v) in need.items():
                if waited[o["eng"]].get(key, 0) >= v:
                    continue
                waited[o["eng"]][key] = v
                w.append((s, v))
            waits[i] = w
        finals = [(dma_sem[k], dma_cnt[k]) for k in dma_sem]
        by_eng = {e: [i for i, o in enumerate(ops) if o["eng"] == e] for e in self.ENGS}
        self.stats = {e: len(by_eng[e]) for e in self.ENGS}
        self.stats["waits"] = sum(len(w) for w in waits)
        self.stats["sems"] = len(sems)

        def run_engine(eng_name, eng):
            for i in by_eng[eng_name]:
                o = ops[i]
                for s, v in waits[i]:
                    eng.wait_ge(s, v)
                ins = o["fn"](eng)
                if sig[i] is not None:
                    ins.then_inc(sig[i][0], 16 if o["dma"] else 1)
            if eng_name == final_wait_eng:
                for s, v in finals:
                    eng.wait_ge(s, v)

        with nc.Block() as block:
            @block.tensor
            def _(e):
                run_engine("pe", e)

            @block.scalar
            def _(e):
                run_engine("act", e)

            @block.vector
            def _(e):
                run_engine("dve", e)

            @block.gpsimd
            def _(e):
                run_engine("pool", e)

            @block.sync
            def _(e):
                run_engine("sp", e)


def token_tiles():
    tiles = []
    t = 0
    while t < NPT:
        n = min(512, NPT - t)
        tiles.append((t, n, False))
        t += n
    tiles.append((NPT, NS * DEC_SEQ, True))
    return tiles


def build_nc():
    nc = bass.Bass("TRN2", target_bir_lowering=False)
    P = Prog(nc)

    def din(name, shape):
        return nc.dram_tensor(name, list(shape), F32, kind="ExternalInput").ap()

    def dout(name, shape):
        return nc.dram_tensor(name, list(shape), F32, kind="ExternalOutput").ap()

    xT_d = din("xT", [NPASS, 128, KD, NTP])
    w1_d = [din("f1w1", [D, DFF]), din("f2w1", [D, DFF])]
    w3_d = [din("f1w3", [D, DFF]), din("f2w3", [D, DFF])]
    w2_d = [din("f1w2", [DFF, D]), din("f2w2", [DFF, D])]
    win_d = din("w_in", [D, IN_DIM])
    wbs_d = din("w_bs", [2048, D])
    wbr_d = din("w_br", [2048, D])
    wout_d = din("w_out", [D, D])
    gains_d = din("gains", [128, 4, KD])
    gssm_d = din("g_ssm", [128, 16])
    gret_d = din("g_ret", [128, 16])
    convw_d = din("convw", [128, 24, 4])
    convb_d = din("convb", [128, 24])
    dtb_d = din("dtb", [128, 32])
    alog_d = din("alog", [128, 32])
    dpp_d = din("dpp", [128, 16])
    cos_d = din("cosT", [NPASS, 128, NTP])
    sin_d = din("sinT", [NPASS, 128, NTP])
    qdec_d = din("qdecT", [4, 128, NTP])
    kdec_d = din("kdec", [128, 8])
    dmat_d = din("dmatT", [128, 4, 128])
    tri_d = din("tri", [128, 128])
    ustr_d = din("ustrict", [128, 128])
    identf_d = din("identf", [128, 128])
    tris_d = din("tri_s", [128, 128])
    ustrs_d = din("ustr_s", [128, 128])
    blks_d = din("blk_s", [128, 128])
    selB_d = din("selB", [128, NS, 128])
    seqsel_d = din("seqsel", [128, NS])
    csel_d = din("csel", [128, NS, 32])
    dmatS_d = din("dmatS", [128, 4, 128])
    ssm_in_d = din("ssmT_in", [NS * NPASS, 128, 2048])
    conv_in_d = din("conv_in", [NS * NPASS, 128, 24, 3])
    ret_in_d = din("ret_in", [NS * NPASS, 4, 256, 512])

    yT_d = dout("yT", [NPASS, 128, KD, NTP])
    ssm_p_d = dout("ssmT_p", [128, 2048])
    conv_p_d = dout("conv_p", [128, 24, 3])
    ret_p_d = dout("ret_p", [4, 256, 512])
    ssm_s_d = dout("ssmT_s", [NS * NPASS, 128, 2048])
    conv_s_d = dout("conv_s", [NS * NPASS, 128, 24, 3])
    ret_s_d = dout("ret_s", [NS * NPASS, 4, 256, 512])

    TT = token_tiles()
    es = contextlib.ExitStack()
    with es:
        def S(name, shape, dt):
            return es.enter_context(nc.sbuf_tensor("s_" + name, list(shape), dt))

        ps = [es.enter_context(nc.psum_tensor("ps%d" % i, [128, 512], F32)) for i in range(8)]
        PK = [("ps", i) for i in range(8)]

        hT = S("hT", [128, KD, NTP], F32)
        uT = S("uT", [128, KD, NTP], BF16)
        yg = S("yg", [128, 16, NTP], BF16)
        NSLOT = 2
        WA = S("WA", [128, NSLOT, 12288], BF16)
        hst = S("hst", [128, 4, 512], F32)
        rst = S("rst", [128, 4, 2, 512], F32)
        hbf = S("hbf", [128, 512], BF16)
        rbf = S("rbf", [128, 2, 512], BF16)
        hist = S("hist", [128, 24, 3], F32)
        convp_st = S("convp_st", [128, 24, 3], F32)
        cvin = S("cvin", [128, NS, 24, 3], F32)
        cvout = S("cvout", [128, NS, 24, 3], F32)
        gains = S("gains", [128, 4, KD], F32)
        gssm = S("gssm", [128, 16], F32)
        gret = S("gret", [128, 16], F32)
        convw = S("convw", [128, 24, 4], F32)
        convb = S("convb", [128, 24], F32)
        dtb = S("dtb", [128, 32], F32)
        nega = S("nega", [128, 32], F32)
        dpp = S("dpp", [128, 16], F32)
        kdec = S("kdec", [128, 8], F32)
        dmatT = S("dmatT", [128, 4, 128], F32)
        tri = S("tri", [128, 128], F32)
        ustr = S("ustr", [128, 128], F32)
        identf = S("identf", [128, 128], F32)
        tri_s = S("tri_s", [128, 128], F32)
        ustr_s = S("ustr_s", [128, 128], F32)
        blk_s = S("blk_s", [128, 128], F32)
        selB = S("selB", [128, NS, 128], F32)
        seqsel = S("seqsel", [128, NS], F32)
        csel = S("csel", [128, NS, 32], F32)
        dmatS = S("dmatS", [128, 4, 128], F32)
        identb = S("identb", [128, 128], BF16)
        onesb = S("onesb", [128, 128], BF16)
        onesf = S("onesf", [128, 128], F32)
        epsc = S("epsc", [128, 4], F32)
        wdt = S("wdt", [128, KD, 32], BF16)
        dt_tm = S("dt_tm", [128, NCH, 32], F32)
        dta_tm = S("dta_tm", [128, NCH, 32], F32)
        ssq_b = S("ssq_b", [128, NTP], F32)
        NAR = 18432
        AR = S("AR", [128, NAR], F32)
        K = {}
        RG = 8
        ar_off = [0]

        ar_pos = {}

        def A(name, shape, dt, alias=None):
            n = 1
            for s_ in shape[1:]:
                n *= s_
            nf = n if dt == F32 else (n + 1) // 2
            nf = (nf + 7) // 8 * 8
            if alias is None:
                o = ar_off[0]
                assert o + nf <= NAR, (name, o, nf)
                ar_off[0] = o + nf
            else:
                o = ar_pos[alias]
                assert o + nf <= NAR, (name, o, nf)
            ar_pos[name] = o
            v = AR[0:shape[0], o:o + nf]
            if dt != F32:
                v = v.bitcast(BF16)
            v = v[:, 0:n]
            if len(shape) == 3:
                v = v.rearrange("p (a b) -> p a b", a=shape[1])
            elif len(shape) == 4:
                v = v.rearrange("p (a b c) -> p a b c", a=shape[1], b=shape[2])
            K[name] = [("AR", r) for r in range(o // RG, (o + nf - 1) // RG + 1)]
            return v

        def KR(name, lo, n):
            o = ar_pos[name] + lo
            return [("AR", r) for r in range(o // RG, (o + n - 1) // RG + 1)]

        def stage_begin():
            ar_off[0] = 0
            T = {}
            T["rsn"] = A("rsn", [128, 512], F32)
            T["sgt0"] = A("sgt0", [128, 512], F32)
            T["sgt1"] = A("sgt1", [128, 512], F32)
            return T

        def MM(out, lhsT, rhs, start, stop, rd, wr):
            P.op("pe", lambda e: e.matmul(out, lhsT=lhsT, rhs=rhs, start=start, stop=stop), rd, wr)

        def TR(out, in_, ident, rd, wr):
            P.op("pe", lambda e: e.transpose(out, in_, ident), rd, wr)

        def ACT(out, in_, func, rd, wr, bias=None, scale=None):
            kw = {}
            if bias is not None:
                kw["bias"] = bias
            if scale is not None:
                kw["scale"] = scale
            P.op("act", lambda e: e.activation(out=out, in_=in_, func=func, **kw), rd, wr)

        def TT_(out, in0, in1, op, rd, wr, eng="dve"):
            P.op(eng, lambda e: e.tensor_tensor(out=out, in0=in0, in1=in1, op=op), rd, wr)

        def TS(out, in0, s1, s2, op0, op1, rd, wr, eng="dve"):
            if s2 is None:
                P.op(eng, lambda e: e.tensor_scalar(out=out, in0=in0, scalar1=s1, scalar2=None, op0=op0), rd, wr)
            else:
                P.op(eng, lambda e: e.tensor_scalar(out=out, in0=in0, scalar1=s1, scalar2=s2, op0=op0, op1=op1), rd, wr)

        def STT(out, in0, scalar, in1, op0, op1, rd, wr, eng="dve"):
            P.op(eng, lambda e: e.scalar_tensor_tensor(out=out, in0=in0, scalar=scalar, in1=in1, op0=op0, op1=op1), rd, wr)

        def CP(out, in_, rd, wr, eng="dve"):
            P.op(eng, lambda e: e.tensor_copy(out=out, in_=in_), rd, wr)

        def RECIP(out, in_, rd, wr):
            P.op("dve", lambda e: e.reciprocal(out=out, in_=in_), rd, wr)

        def MEMSET(ap, val, wr, eng="dve"):
            P.op(eng, lambda e: e.memset(ap, val), (), wr)

        def DMA(out, in_, rd, wr, semkey=None, cast=False):
            P.op("pool" if cast else "sp", lambda e: e.dma_start(out=out, in_=in_), rd, wr, dma=True, semkey=semkey)

        def par2(fa, fb):
            n0 = len(P.ops)
            fa()
            n1 = len(P.ops)
            fb()
            n2 = len(P.ops)
            A_, B_ = P.ops[n0:n1], P.ops[n1:n2]
            merged = []
            ia = ib = 0
            while ia < len(A_) or ib < len(B_):
                if ib >= len(B_) or (ia < len(A_) and ia * len(B_) <= ib * len(A_)):
                    merged.append(A_[ia])
                    ia += 1
                else:
                    merged.append(B_[ib])
                    ib += 1
            P.ops[n0:n2] = merged

        wslot_ctr = [0]

        def next_slot():
            s = wslot_ctr[0] % NSLOT
            wslot_ctr[0] += 1
            return s

        def wreg(s, lo, hi):
            return [("W", s, r) for r in range(lo // 2048, (hi - 1) // 2048 + 1)]

        def wcols(dram, c0, cn):
            return dram.rearrange("(k p) n -> p k n", p=128)[:, :, c0:c0 + cn]

        def wload(s, lo, n, view, src, tag):
            ks = wreg(s, lo, lo + n)
            DMA(view, src, (), ks, cast=True, semkey=("Wsem", s, tag))
            return ks

        for t_, d_, kn in ((gains, gains_d, "gains"), (gssm, gssm_d, "gssm"), (gret, gret_d, "gret"), (convw, convw_d, "convw"),
                           (convb, convb_d, "convb"), (dtb, dtb_d, "dtb"), (dpp, dpp_d, "dpp"), (kdec, kdec_d, "kdec"),
                           (dmatT, dmat_d, "dmatT"), (tri, tri_d, "tri"), (ustr, ustr_d, "ustr"), (identf, identf_d, "identf"),
                           (tri_s, tris_d, "tri_s"), (ustr_s, ustrs_d, "ustr_s"), (blk_s, blks_d, "blk_s"), (selB, selB_d, "selB"),
                           (seqsel, seqsel_d, "seqsel"), (csel, csel_d, "csel"), (dmatS, dmatS_d, "dmatS")):
            sl = tuple(slice(None) for _ in t_.shape)
            DMA(t_[sl], d_[sl], (), [kn])
        DMA(nega[:, :], alog_d[:, :], (), ["nega"])
        ACT(nega[:, :], nega[:, :], AF.Exp, ["nega"], ["nega"])
        TS(nega[:, :], nega[:, :], -1.0, None, ALU.mult, None, ["nega"], ["nega"])
        CP(identb[:, :], identf[:, :], ["identf"], ["identb"])
        MEMSET(onesb[:, :], 1.0, ["onesb"])
        MEMSET(onesf[:, :], 1.0, ["onesf"])
        MEMSET(epsc[:, 0:1], EPS, ["epsc"])
        MEMSET(epsc[:, 1:2], EPS_G, ["epsc"])
        MEMSET(epsc[:, 2:3], 1.0, ["epsc"])
        MEMSET(hst[:, :, :], 0.0, [("hst", g) for g in range(4)])
        MEMSET(rst[:, :, :, :], 0.0, [("rst", r) for r in range(4)])
        MEMSET(hist[:, :, :], 0.0, [("hist", g) for g in range(4)])
        DMA(wdt[:, :, :], wcols(win_d, OFF_DT, 32), (), ["wdt"], cast=True)

        TTD = [(0, NTP // 2, False), (NTP // 2, NTP - NTP // 2, False)]
        SEGB = sorted(set([0, NTP] + [t_[0] for t_ in TT] + [t_[0] for t_ in TTD]))
        SEGS = [(SEGB[i_], SEGB[i_ + 1]) for i_ in range(len(SEGB) - 1)]
        CUR = [TT]

        def segs(ti):
            t0_, tn_ = CUR[0][ti][0], CUR[0][ti][1]
            return [s_ for s_, (a_, b_) in enumerate(SEGS) if a_ < t0_ + tn_ and b_ > t0_]

        HK = lambda ti, m: [("h", s_, m) for s_ in segs(ti)]
        UK = lambda ti, k: [("u", s_, k) for s_ in segs(ti)]

        def norm_stage(T, gidx, out_dma=None, tiling=None):
            CUR[0] = tiling if tiling is not None else TT
            sq, rsn = T["sq"], T["rsn"]
            for ti, (t0, tn, _) in enumerate(CUR[0]):
                hk = [HK(ti, m) for m in range(KD)]
                ACT(sq[:, :, 0:tn], hT[:, :, t0:t0 + tn], AF.Square, hk, K["sq"])
                for k in range(KD):
                    MM(ps[7][:, 0:tn], onesb[:, :], sq[:, k, 0:tn], k == 0, k == KD - 1, K["sq"] + ["onesb"], [PK[7]])
                ACT(rsn[:, 0:tn], ps[7][:, 0:tn], AF.Sqrt, [PK[7], "epsc"], K["rsn"], bias=epsc[:, 0:1], scale=1.0 / D)
                RECIP(rsn[:, 0:tn], rsn[:, 0:tn], K["rsn"], K["rsn"])
                if out_dma is None:
                    for k in range(KD):
                        STT(uT[:, k, t0:t0 + tn], hT[:, k, t0:t0 + tn], gains[:, gidx, k:k + 1], rsn[:, 0:tn], ALU.mult, ALU.mult,
                            [HK(ti, k), "gains"] + K["rsn"], [UK(ti, k)])
                else:
                    outT = T["outT"]
                    for k in range(KD):
                        STT(outT[:, k, 0:tn], hT[:, k, t0:t0 + tn], gains[:, gidx, k:k + 1], rsn[:, 0:tn], ALU.mult, ALU.mult,
                            [HK(ti, k), "gains"] + K["rsn"], K["outT"])
                    DMA(out_dma[:, :, t0:t0 + tn], outT[:, :, 0:tn], K["outT"], [("yout", t0)], semkey="outT")
            CUR[0] = TT

        def ffn_stage(T, which):
            CUR[0] = TTD
            w1, w3, w2 = w1_d[which], w3_d[which], w2_d[which]
            aT = [T["aT0"], T["aT1"]]
            j0 = 0
            it = 0
            while j0 < DFF:
                jn = min(512, DFF - j0)
                nj = jn // 128
                s = next_slot()
                w1b = WA[:, s, 0:4096].rearrange("p (k n) -> p k n", k=KD)
                w3b = WA[:, s, 4096:8192].rearrange("p (k n) -> p k n", k=KD)
                w2b = WA[:, s, 8192:12288].rearrange("p (c n) -> p c n", c=4)
                k1 = wload(s, 0, 4096, w1b[:, :, 0:jn], wcols(w1, j0, jn), 0)
                k3 = wload(s, 4096, 4096, w3b[:, :, 0:jn], wcols(w3, j0, jn), 1)
                k2 = wload(s, 8192, 4096, w2b[:, 0:nj, :], w2[j0:j0 + jn, :].rearrange("(c p) n -> p c n", p=128), 2)
                for ti, (t0, tn, _) in enumerate(TTD):
                    a = aT[it % 2]
                    akj = [KR("aT%d" % (it % 2), 256 * jc_, 256) for jc_ in range(4)]
                    it += 1
                    for jc in range(nj):
                        gb, ub = jc % 2, 2 + jc % 2
                        for k in range(KD):
                            MM(ps[gb][:, 0:tn], w1b[:, k, jc * 128:(jc + 1) * 128], uT[:, k, t0:t0 + tn], k == 0, k == KD - 1,
                               k1 + [UK(ti, k)], [PK[gb]])
                        for k in range(KD):
                            MM(ps[ub][:, 0:tn], w3b[:, k, jc * 128:(jc + 1) * 128], uT[:, k, t0:t0 + tn], k == 0, k == KD - 1,
                               k3 + [UK(ti, k)], [PK[ub]])
                        sg = T["sgt%d" % (jc % 2)]
                        sgk = K["sgt%d" % (jc % 2)]
                        ACT(sg[:, 0:tn], ps[gb][:, 0:tn], AF.Silu, [PK[gb]], sgk)
                        TT_(a[:, jc, 0:tn], sg[:, 0:tn], ps[ub][:, 0:tn], ALU.mult, sgk + [PK[ub]], akj[jc])
                    for m in range(KD):
                        ob = 4 + m % 2
                        for jc in range(nj):
                            MM(ps[ob][:, 0:tn], w2b[:, jc, m * 128:(m + 1) * 128], a[:, jc, 0:tn], jc == 0, jc == nj - 1,
                               k2 + akj[jc], [PK[ob]])
                        STT(hT[:, m, t0:t0 + tn], ps[ob][:, 0:tn], 0.5, hT[:, m, t0:t0 + tn], ALU.mult, ALU.add,
                            [PK[ob], HK(ti, m)], [HK(ti, m)])
                j0 += jn
            CUR[0] = TT

        def chunks_of_tile(ti):
            t0, tn, is_s = TT[ti]
            if not is_s:
                return [(c * 128, 128, (t0 // 128) + c, t0 + c * 128, None) for c in range(tn // 128)]
            return [(0, NS * DEC_SEQ, PCH, t0, "batch")]

        def dt_stage(T):
            smallt = T["smallt"]
            for ti in range(len(TT)):
                for (c0, Tc, ci, tok0, si) in chunks_of_tile(ti):
                    for k in range(KD):
                        MM(ps[1][0:Tc, 0:32], uT[:, k, tok0:tok0 + Tc], wdt[:, k, :], k == 0, k == KD - 1, [UK(ti, k), "wdt"], [PK[1]])
                    xb, ax, ee, ll = smallt[0:Tc, 0, :], smallt[0:Tc, 1, :], smallt[0:Tc, 2, :], smallt[0:Tc, 3, :]
                    kk = K["smallt"]
                    TT_(xb, ps[1][0:Tc, 0:32], dtb[0:Tc, :], ALU.add, [PK[1], "dtb"], kk)
                    ACT(ax, xb, AF.Abs, kk, kk)
                    ACT(ee, ax, AF.Exp, kk, kk, scale=-1.0)
                    ACT(ll, ee, AF.Ln, kk + ["epsc"], kk, bias=epsc[0:Tc, 2:3])
                    STT(dt_tm[0:Tc, ci, :], xb, 0.0, ll, ALU.max, ALU.add, kk, [("dt", ci)])
                    TT_(dta_tm[0:Tc, ci, :], dt_tm[0:Tc, ci, :], nega[0:Tc, :], ALU.mult, [("dt", ci), "nega"], [("dta", ci)])

        def ssm_group(T, g, p, last_pass):
            s = next_slot()
            wz = WA[:, s, 0:4096].rearrange("p (k n) -> p k n", k=KD)
            wx = WA[:, s, 4096:8192].rearrange("p (k n) -> p k n", k=KD)
            wbc = WA[:, s, 8192:10240].rearrange("p (k n) -> p k n", k=KD)
            kz = wload(s, 0, 4096, wz, wcols(win_d, OFF_Z + 512 * g, 512), 0)
            kx = wload(s, 4096, 4096, wx, wcols(win_d, OFF_X + 512 * g, 512), 1)
            kb1 = wload(s, 8192, 2048, wbc[:, :, 0:128], wcols(win_d, OFF_B + 128 * g, 128), 2)
            kb2 = wload(s, 8192, 2048, wbc[:, :, 128:256], wcols(win_d, OFF_C + 128 * g, 128), 3)
            kbc = kb1
            gch = [4 * g + j for j in range(4)] + [16 + g, 20 + g]
            hk_ = ("hist", g)
            raw2, rawS, acc, xc, sz = T["raw2"], T["rawS"], [T["acc0"], T["acc1"]], T["xc"], T["sz"]
            Rt, Et, cbm = T["Rt"], T["Et"], T["cbm"]
            ytmp, yt2, ytm, t4, t3, ygf, sqs, hs = T["ytmp"], T["yt2"], T["ytm"], T["t4"], T["t3"], T["ygf"], T["sqs"], T["hs"]
            hbs, ctm, btmm = T["hbs"], T["ctm"], T["btmm"]
            for ti, (t0, tn, is_s) in enumerate(TT[:DBG["ssm_tiles"]]):
                for j in range(6):
                    cj = gch[j]
                    wv_ = wx[:, :, j * 128:(j + 1) * 128] if j < 4 else wbc[:, :, (j - 4) * 128:(j - 3) * 128]
                    wkeys = kx if j < 4 else kbc
                    b = 6 + j % 2
                    rw = raw2[:, j % 2, :]
                    rwk = KR("raw2", 520 * (j % 2), 520)
                    for k in range(KD):
                        MM(ps[b][:, 0:tn], wv_[:, k, :], uT[:, k, t0:t0 + tn], k == 0, k == KD - 1, wkeys + [UK(ti, k)], [PK[b]])
                    a_ = acc[j % 2]
                    akk = K["acc%d" % (j % 2)]
                    if not is_s:
                        CP(rw[:, 0:3], hist[:, 6 * g + j, :], [hk_], rwk)
                        ACT(rw[:, 3:3 + tn], ps[b][:, 0:tn], AF.Copy, [PK[b]], rwk)
                        CP(hist[:, 6 * g + j, :], rw[:, tn:tn + 3], rwk, [hk_])
                        if last_pass:
                            CP(convp_st[:, cj, :], rw[:, tn:tn + 3], rwk, ["convp_st"])
                        win = lambda kk_: rw[:, kk_:kk_ + tn]
                        av = a_[:, 0:tn]
                        xo = xc[:, j, 0:tn]
                        rdk = rwk
                    else:
                        CP(rawS[:, j, :, 0:3], cvin[:, :, cj, :], ["cvin"], K["rawS"])
                        ACT(rawS[:, j, :, 3:7], ps[b][:, 0:tn].rearrange("p (s t) -> p s t", t=DEC_SEQ), AF.Copy, [PK[b]], K["rawS"])
                        CP(cvout[:, :, cj, :], rawS[:, j, :, 4:7], K["rawS"], ["cvout"])
                        win = lambda kk_: rawS[:, j, :, kk_:kk_ + DEC_SEQ]
                        av = a_[:, 0:tn].rearrange("p (s t) -> p s t", t=DEC_SEQ)
                        xo = xc[:, j, 0:tn].rearrange("p (s t) -> p s t", t=DEC_SEQ)
                        rdk = K["rawS"]
                    if DBG["jcut"] <= 1:
                        continue
                    TS(av, win(0), convw[:, cj, 0:1], convb[:, cj:cj + 1], ALU.mult, ALU.add, rdk + ["convw", "convb"], akk)
                    if DBG["jcut"] <= 2:
                        continue
                    for kk_ in range(1, 4):
                        STT(av, win(kk_), convw[:, cj, kk_:kk_ + 1], av, ALU.mult, ALU.add, rdk + ["convw"] + akk, akk)
                    if DBG["jcut"] <= 3:
                        continue
                    ACT(xo, av, AF.Silu, akk, KR("xc", 256 * j, 256))
                for j in range(4):
                    b = 6 + j % 2
                    for k in range(KD):
                        MM(ps[b][:, 0:tn], wz[:, k, j * 128:(j + 1) * 128], uT[:, k, t0:t0 + tn], k == 0, k == KD - 1, kz + [UK(ti, k)], [PK[b]])
                    ACT(sz[:, j, 0:tn], ps[b][:, 0:tn], AF.Silu, [PK[b]], KR("sz", 256 * j, 256))
                hsl = slice(8 * g, 8 * g + 8)

                def phaseA(ch, par):
                    (c0, Tc, ci, tok0, si) = ch
                    bat = si is not None
                    TRI, USTR = (tri_s, ustr_s) if bat else (tri, ustr)
                    trk, usk = ("tri_s", "ustr_s") if bat else ("tri", "ustr")
                    xdt, xdtt, btm, wT, sm8, decs = (T["xdt%d" % par], T["xdtt%d" % par], T["btm%d" % par], T["wT%d" % par],
                                                     T["sm8%d" % par], T["decs%d" % par])
                    kxdt, kxdtt, kbtm, kwT, k8, kdecs = (K["xdt%d" % par], K["xdtt%d" % par], K["btm%d" % par], K["wT%d" % par],
                                                         K["sm8%d" % par], K["decs%d" % par])
                    la_sb, expla, lal_sb, dec_b, tail = sm8[:, 0, :], sm8[:, 1, :], sm8[:, 2, :], sm8[:, 3, :], sm8[:, 4, :]
                    k_la, k_ex, k_lal, k_dec, k_tl = [KR("sm8%d" % par, 8 * i_, 8) for i_ in range(5)]
                    tp = ps[0][:, :].bitcast(BF16)
                    for j in range(5):
                        TR(tp[0:Tc, j * 128:(j + 1) * 128], xc[:, j, c0:c0 + Tc], identb[:, :], KR("xc", 256 * j, 256) + ["identb"], [PK[0]])
                    dts = dt_tm[0:Tc, ci, hsl]
                    TT_(xdt[0:Tc, :].rearrange("p (h d) -> p h d", h=8), tp[0:Tc, 0:512].rearrange("p (h d) -> p h d", h=8),
                        dts.unsqueeze(2).to_broadcast([Tc, 8, 64]), ALU.mult, [PK[0], ("dt", ci)], kxdt)
                    CP(btm[0:Tc, :], tp[0:Tc, 512:640], [PK[0]], kbtm)
                    dtas = dta_tm[0:Tc, ci, hsl]
                    MM(ps[1][0:Tc, 0:8], TRI[0:Tc, 0:Tc], dtas, True, True, [trk, ("dta", ci)], [PK[1]])
                    if not bat:
                        MM(ps[1][:, 8:16], onesf[0:Tc, :], dtas, True, True, ["onesf", ("dta", ci)], [PK[1]])
                    else:
                        MM(ps[1][0:Tc, 8:16], blk_s[0:Tc, 0:Tc], dtas, True, True, ["blk_s", ("dta", ci)], [PK[1]])
                        for i in range(NS):
                            MM(ps[1][:, 16 + 8 * i:24 + 8 * i], selB[0:Tc, i, :], dtas, True, True, ["selB", ("dta", ci)], [PK[1]])
                    MM(ps[1][0:Tc, 128:128 + Tc], xc[:, 4, c0:c0 + Tc], xc[:, 5, c0:c0 + Tc], True, True, KR("xc", 1024, 512), [PK[1]])
                    ACT(la_sb[0:Tc, :], ps[1][0:Tc, 0:8], AF.Copy, [PK[1]], k_la)
                    ACT(expla[0:Tc, :], ps[1][0:Tc, 0:8], AF.Exp, [PK[1]], k_ex)
                    if not bat:
                        ACT(lal_sb[:, :], ps[1][:, 8:16], AF.Copy, [PK[1]], k_lal)
                        ACT(dec_b[:, :], ps[1][:, 8:16], AF.Exp, [PK[1]], k_dec)
                    else:
                        ACT(lal_sb[0:Tc, :], ps[1][0:Tc, 8:16], AF.Copy, [PK[1]], k_lal)
                        ACT(decs[:, :], ps[1][:, 16:16 + 8 * NS], AF.Exp, [PK[1]], kdecs)
                    TT_(cbm[0:Tc, 0:Tc], ps[1][0:Tc, 128:128 + Tc], TRI[0:Tc, 0:Tc], ALU.mult, [PK[1], trk], K["cbm"])
                    TT_(tail[0:Tc, :], lal_sb[0:Tc, :], la_sb[0:Tc, :], ALU.subtract, k_lal + k_la, k_tl)
                    ACT(tail[0:Tc, :], tail[0:Tc, :], AF.Exp, k_tl, k_tl)
                    TT_(xdtt[0:Tc, :].rearrange("p (h d) -> p h d", h=8), xdt[0:Tc, :].rearrange("p (h d) -> p h d", h=8),
                        tail[0:Tc, :].unsqueeze(2).to_broadcast([Tc, 8, 64]), ALU.mult, kxdt + k_tl, kxdtt)
                    for hf in range(2):
                        TT_(Rt[0:Tc, :, 0:Tc], dta_tm[0:Tc, ci, 8 * g + 4 * hf:8 * g + 4 * hf + 4].unsqueeze(2).to_broadcast([Tc, 4, Tc]),
                            TRI[0:Tc, 0:Tc].unsqueeze(1).to_broadcast([Tc, 4, Tc]), ALU.mult, [("dta", ci), trk], K["Rt"])
                        segv = ps[2 + hf][0:Tc, 0:4 * Tc].rearrange("p (h t) -> p h t", h=4)
                        MM(segv, USTR[0:Tc, 0:Tc], Rt[0:Tc, :, 0:Tc], True, True, [usk] + K["Rt"], [PK[2 + hf]])
                        ACT(Et[0:Tc, :, 0:Tc], segv, AF.Exp, [PK[2 + hf]], K["Et"])
                        TT_(wT[0:Tc, 4 * hf:4 * hf + 4, 0:Tc], Et[0:Tc, :, 0:Tc], cbm[0:Tc, 0:Tc].unsqueeze(1).to_broadcast([Tc, 4, Tc]), ALU.mult,
                            K["Et"] + K["cbm"], kwT)

                def phaseB(ch, par):
                    (c0, Tc, ci, tok0, si) = ch
                    bat = si is not None
                    xdt, xdtt, btm, wT, sm8, decs = (T["xdt%d" % par], T["xdtt%d" % par], T["btm%d" % par], T["wT%d" % par],
                                                     T["sm8%d" % par], T["decs%d" % par])
                    kxdt, kxdtt, kbtm, kwT, k8, kdecs = (K["xdt%d" % par], K["xdtt%d" % par], K["btm%d" % par], K["wT%d" % par],
                                                         K["sm8%d" % par], K["decs%d" % par])
                    expla, dec_b = sm8[:, 1, :], sm8[:, 3, :]
                    k_la, k_ex, k_lal, k_dec, k_tl = [KR("sm8%d" % par, 8 * i_, 8) for i_ in range(5)]
                    if not bat:
                        st, stk = hst[:, g, :], [("hst", g)]
                        ACT(hbf[:, :], st, AF.Copy, stk, ["hbf"])
                    else:
                        s0 = p * NS
                        stk = K["hs"]
                        DMA(hs[:, :, :], ssm_in_d[s0:s0 + NS, :, 512 * g:512 * g + 512].rearrange("s n f -> n s f"), (), stk, semkey="hsld")
                        ACT(hbs[:, :, :], hs[:, :, :], AF.Copy, stk, K["hbs"])
                    for hh in range(8):
                        MM(ps[4][0:Tc, hh * 64:(hh + 1) * 64], wT[0:Tc, hh, 0:Tc], xdt[0:Tc, hh * 64:(hh + 1) * 64], True, True,
                           kwT + kxdt, [PK[4]])
                    if not bat:
                        MM(ps[5][0:Tc, 0:512], xc[:, 5, c0:c0 + Tc], hbf[:, :], True, True, KR("xc", 1280, 256) + ["hbf"], [PK[5]])
                    else:
                        TT_(ctm[:, :, 0:Tc], xc[:, 5, c0:c0 + Tc].unsqueeze(1).to_broadcast([128, NS, Tc]), csel[:, :, 0:Tc], ALU.mult,
                            KR("xc", 1280, 256) + ["csel"], K["ctm"])
                        for i in range(NS):
                            MM(ps[5][0:Tc, 0:512], ctm[:, i, 0:Tc], hbs[:, i, :], i == 0, i == NS - 1, K["ctm"] + K["hbs"], [PK[5]])
                    ACT(ytmp[0:Tc, :], ps[4][0:Tc, :], AF.Copy, [PK[4]], K["ytmp"])
                    TT_(yt2[0:Tc, :].rearrange("p (h d) -> p h d", h=8), ps[5][0:Tc, :].rearrange("p (h d) -> p h d", h=8),
                        expla[0:Tc, :].unsqueeze(2).to_broadcast([Tc, 8, 64]), ALU.mult, [PK[5]] + k_ex, K["yt2"])
                    TT_(ytm[0:Tc, :], ytmp[0:Tc, :], yt2[0:Tc, :], ALU.add, K["ytmp"] + K["yt2"], K["ytm"])
                    if not bat:
                        MM(ps[7][:, 0:512], btm[0:Tc, :], xdtt[0:Tc, :], True, True, kbtm + kxdtt, [PK[7]])
                        TT_(t4[:, :].rearrange("p (h d) -> p h d", h=8), st.rearrange("p (h d) -> p h d", h=8),
                            dec_b[:, :].unsqueeze(2).to_broadcast([128, 8, 64]), ALU.mult, stk + k_dec, K["t4"])
                        TT_(st, t4[:, :], ps[7][:, 0:512], ALU.add, K["t4"] + [PK[7]], stk)
                    else:
                        TT_(btmm[0:Tc, :, :], btm[0:Tc, :].unsqueeze(1).to_broadcast([Tc, NS, 128]),
                            seqsel[0:Tc, :].unsqueeze(2).to_broadcast([Tc, NS, 128]), ALU.mult, kbtm + ["seqsel"], K["btmm"])
                        for i in range(NS):
                            b = 7 if i % 2 == 0 else 5
                            MM(ps[b][:, 0:512], btmm[0:Tc, i, :], xdtt[0:Tc, :], True, True, K["btmm"] + kxdtt, [PK[b]])
                            TT_(t4[:, :].rearrange("p (h d) -> p h d", h=8), hs[:, i, :].rearrange("p (h d) -> p h d", h=8),
                                decs[:, 8 * i:8 * i + 8].unsqueeze(2).to_broadcast([128, 8, 64]), ALU.mult, stk + kdecs, K["t4"])
                            TT_(hs[:, i, :], t4[:, :], ps[b][:, 0:512], ALU.add, K["t4"] + [PK[b]], stk)
                        DMA(ssm_s_d[s0:s0 + NS, :, 512 * g:512 * g + 512].rearrange("s n f -> n s f"), hs[:, :, :], stk, [("ssms", p, g)], semkey="hsst")
                    for j in range(4):
                        TR(ps[6][:, j * Tc:(j + 1) * Tc], ytm[0:Tc, j * 128:(j + 1) * 128], identf[0:Tc, 0:Tc], K["ytm"] + ["identf"], [PK[6]])
                    TT_(t3[:, :, 0:Tc], xc[:, 0:4, c0:c0 + Tc], dpp[:, 4 * g:4 * g + 4].unsqueeze(2).to_broadcast([128, 4, Tc]), ALU.mult,
                        KR("xc", 0, 1024) + ["dpp"], K["t3"])
                    TT_(t3[:, :, 0:Tc], t3[:, :, 0:Tc], ps[6][:, 0:4 * Tc].rearrange("p (j t) -> p j t", j=4), ALU.add, K["t3"] + [PK[6]], K["t3"])
                    TT_(ygf[:, :, 0:Tc], t3[:, :, 0:Tc], sz[:, :, c0:c0 + Tc], ALU.mult, K["t3"] + K["sz"], K["ygf"])
                    ACT(sqs[:, :, 0:Tc], ygf[:, :, 0:Tc], AF.Square, K["ygf"], K["sqs"])
                    TT_(yg[:, 4 * g:4 * g + 4, tok0:tok0 + Tc], ygf[:, :, 0:Tc],
                        gssm[:, 4 * g:4 * g + 4].unsqueeze(2).to_broadcast([128, 4, Tc]), ALU.mult, K["ygf"] + ["gssm"], [("yg", ti)])
                    for j in range(4):
                        MM(ps[4][:, 0:Tc], onesb[:, :], sqs[:, j, 0:Tc], j == 0, j == 3, ["onesb"] + K["sqs"], [PK[4]])
                    if g == 0:
                        CP(ssq_b[:, tok0:tok0 + Tc], ps[4][:, 0:Tc], [PK[4]], [("ssq", ti)])
                    else:
                        TT_(ssq_b[:, tok0:tok0 + Tc], ssq_b[:, tok0:tok0 + Tc], ps[4][:, 0:Tc], ALU.add, [PK[4], ("ssq", ti)], [("ssq", ti)])

                chs = chunks_of_tile(ti)
                phaseA(chs[0], 0)
                for idx in range(1, len(chs)):
                    par2(lambda: phaseA(chs[idx], idx % 2), lambda: phaseB(chs[idx - 1], (idx - 1) % 2))
                phaseB(chs[-1], (len(chs) - 1) % 2)
            if last_pass:
                DMA(ssm_p_d[:, 512 * g:512 * g + 512], hst[:, g, :], [("hst", g)], [("ssmp", g)], semkey=("ssmp", g))

        def finalize(T, wb_d, gate_off, use_rs):
            mbf = T["mbf"]
            mk = K["mbf"]
            for ti, (t0, tn, _) in enumerate(TT):
                if use_rs:
                    ACT(ssq_b[:, t0:t0 + tn], ssq_b[:, t0:t0 + tn], AF.Sqrt, [("ssq", ti), "epsc"], [("ssq", ti)], bias=epsc[:, 1:2], scale=1.0 / 2048)
                    RECIP(ssq_b[:, t0:t0 + tn], ssq_b[:, t0:t0 + tn], [("ssq", ti)], [("ssq", ti)])
            for half in range(2):
                s = next_slot()
                wb = WA[:, s, 0:8192].rearrange("p (c n) -> p c n", c=16)
                wg = WA[:, s, 8192:12288].rearrange("p (k n) -> p k n", k=KD)
                kb_ = wload(s, 0, 8192, wb, wb_d.rearrange("(c p) n -> p c n", p=128)[:, :, half * 512:(half + 1) * 512], 0)
                kg_ = wload(s, 8192, 4096, wg, wcols(win_d, gate_off + half * 512, 512), 1)
                for mi in range(4):
                    m = 4 * half + mi
                    for ti, (t0, tn, _) in enumerate(TT):
                        pa, pg = (0, 1) if (m + ti) % 2 == 0 else (4, 5)
                        for c in range(16):
                            MM(ps[pa][:, 0:tn], wb[:, c, mi * 128:(mi + 1) * 128], yg[:, c, t0:t0 + tn], c == 0, c == 15, kb_ + [("yg", ti)], [PK[pa]])
                        for k in range(KD):
                            MM(ps[pg][:, 0:tn], wg[:, k, mi * 128:(mi + 1) * 128], uT[:, k, t0:t0 + tn], k == 0, k == KD - 1, kg_ + [UK(ti, k)], [PK[pg]])
                        sg = T["sgt%d" % ((m + ti) % 2)]
                        sgk = K["sgt%d" % ((m + ti) % 2)]
                        ACT(sg[:, 0:tn], ps[pg][:, 0:tn], AF.Sigmoid, [PK[pg]], sgk)
                        if use_rs:
                            TT_(sg[:, 0:tn], sg[:, 0:tn], ssq_b[:, t0:t0 + tn], ALU.mult, sgk + [("ssq", ti)], sgk)
                        TT_(mbf[:, m, t0:t0 + tn], ps[pa][:, 0:tn], sg[:, 0:tn], ALU.mult, [PK[pa]] + sgk, mk)
            s = next_slot()
            wo = WA[:, s, 0:8192].rearrange("p (k n) -> p k n", k=KD)
            ko_ = wload(s, 0, 8192, wo, wcols(wout_d, 0, 1024), 0)
            for m in range(KD):
                for ti, (t0, tn, _) in enumerate(TT):
                    b = 2 + (m + ti) % 2
                    for k in range(KD):
                        MM(ps[b][:, 0:tn], wo[:, k, m * 128:(m + 1) * 128], mbf[:, k, t0:t0 + tn], k == 0, k == KD - 1, ko_ + mk, [PK[b]])
                    TT_(hT[:, m, t0:t0 + tn], hT[:, m, t0:t0 + tn], ps[b][:, 0:tn], ALU.add, [HK(ti, m), PK[b]], [HK(ti, m)])

        def ret_head(T, r, p, last_pass):
            s = next_slot()
            wq = WA[:, s, 0:2048].rearrange("p (k n) -> p k n", k=KD)
            wkk = WA[:, s, 2048:4096].rearrange("p (k n) -> p k n", k=KD)
            wv = WA[:, s, 4096:8192].rearrange("p (k n) -> p k n", k=KD)
            wg = WA[:, s, 8192:12288].rearrange("p (k n) -> p k n", k=KD)
            kq = wload(s, 0, 2048, wq, wcols(win_d, OFF_Q + 256 * r, 256), 0)
            kk_ = wload(s, 2048, 2048, wkk, wcols(win_d, OFF_K + 256 * r, 256), 1)
            kv = wload(s, 4096, 4096, wv, wcols(win_d, OFF_V + 512 * r, 512), 2)
            kg = wload(s, 8192, 4096, wg, wcols(win_d, OFF_G + 512 * r, 512), 3)
            gam = GAMMAS[r]
            qkf, rot, qT, qsT, kT, gsg = T["qkf"], T["rot"], T["qT"], T["qsT"], T["kT"], T["gsg"]
            vtm, kd, sm, yrt, sqs, rs, rsn = T["vtm"], T["kd"], T["sm"], T["yrt"], T["sqs"], T["rs"], T["rsn"]
            rbs, qsm, kdm = T["rbs"], T["qsm"], T["kdm"]
            cosT, sinT, qdecT = T["cosT"], T["sinT"], T["qdecT"]
            DMA(qdecT[:, :], qdec_d[r, :, :], (), K["qdecT"], semkey="qdld")
            for ti, (t0, tn, is_s) in enumerate(TT):
                for qi, (wv_, wks) in enumerate(((wq, kq), (wkk, kk_))):
                    for dc in range(2):
                        b = 6 + dc
                        for k in range(KD):
                            MM(ps[b][:, 0:tn], wv_[:, k, dc * 128:(dc + 1) * 128], uT[:, k, t0:t0 + tn], k == 0, k == KD - 1, wks + [UK(ti, k)], [PK[b]])
                        ACT(qkf[:, qi, dc, 0:tn], ps[b][:, 0:tn], AF.Copy, [PK[b]], KR("qkf", 1024 * qi + 512 * dc, 512), scale=(1.0 if qi == 0 else 1.0 / 16.0))
                    t1, t2 = qkf[:, qi, 0, 0:tn], qkf[:, qi, 1, 0:tn]
                    cs, sn = cosT[:, t0:t0 + tn], sinT[:, t0:t0 + tn]
                    dst = qT if qi == 0 else kT
                    dk = K["qT"] if qi == 0 else K["kT"]
                    kq1, kq2 = KR("qkf", 1024 * qi, 512), KR("qkf", 1024 * qi + 512, 512)
                    kr = [KR("rot", 512 * i_, 512) for i_ in range(4)]
                    dk0, dk1 = KR("qT" if qi == 0 else "kT", 0, 256), KR("qT" if qi == 0 else "kT", 256, 256)
                    TT_(rot[:, 0, 0:tn], t1, cs, ALU.mult, kq1 + K["cosT"], kr[0])
                    TT_(rot[:, 1, 0:tn], t2, sn, ALU.mult, kq2 + K["sinT"], kr[1])
                    TT_(rot[:, 2, 0:tn], t1, sn, ALU.mult, kq1 + K["sinT"], kr[2])
                    TT_(rot[:, 3, 0:tn], t2, cs, ALU.mult, kq2 + K["cosT"], kr[3])
                    TT_(dst[:, 0, 0:tn], rot[:, 0, 0:tn], rot[:, 1, 0:tn], ALU.subtract, kr[0] + kr[1], dk0)
                    TT_(dst[:, 1, 0:tn], rot[:, 2, 0:tn], rot[:, 3, 0:tn], ALU.add, kr[2] + kr[3], dk1)
                TT_(qsT[:, :, 0:tn], qT[:, :, 0:tn], qdecT[:, t0:t0 + tn].unsqueeze(1).to_broadcast([128, 2, tn]), ALU.mult, K["qT"] + K["qdecT"], K["qsT"])
                for j in range(4):
                    b = 6 + j % 2
                    for k in range(KD):
                        MM(ps[b][:, 0:tn], wg[:, k, j * 128:(j + 1) * 128], uT[:, k, t0:t0 + tn], k == 0, k == KD - 1, kg + [UK(ti, k)], [PK[b]])
                    sg = T["sgt%d" % (j % 2)]
                    sgk = K["sgt%d" % (j % 2)]
                    ACT(sg[:, 0:tn], ps[b][:, 0:tn], AF.Silu, [PK[b]], sgk)
                    TS(gsg[:, j, 0:tn], sg[:, 0:tn], gret[:, 4 * r + j:4 * r + j + 1], None, ALU.mult, None, sgk + ["gret"], KR("gsg", 256 * j, 256))
                for (c0, Tc, ci, tok0, si) in chunks_of_tile(ti):
                    bat = si is not None
                    if not bat:
                        st, stk = rst[:, r, :, :], [("rst", r)]
                        kdc = kdec[0:Tc, r:r + 1]
                        cdec = gam ** 128
                        DM = dmatT
                        dmk = "dmatT"
                        ACT(rbf[:, :, :], st, AF.Copy, stk, ["rbf"])
                    else:
                        s0 = p * NS
                        stk = K["rs"]
                        for i in range(NS):
                            DMA(rs[:, i, :, :], ret_in_d[s0 + i, r, :, :].rearrange("(c p) e -> p c e", p=128), (), stk, semkey=("rsld", i))
                        kdc = kdec[0:Tc, 4 + r:5 + r]
                        cdec = gam ** DEC_SEQ
                        DM = dmatS
                        dmk = "dmatS"
                        ACT(rbs[:, :, :, :], rs[:, :, :, :], AF.Copy, stk, K["rbs"])
                    for k in range(KD):
                        MM(ps[4][0:Tc, 0:512], uT[:, k, tok0:tok0 + Tc], wv[:, k, :], k == 0, k == KD - 1, [UK(ti, k)] + kv, [PK[4]])
                    ACT(vtm[0:Tc, :], ps[4][0:Tc, :], AF.Copy, [PK[4]], K["vtm"])
                    tp = ps[0][:, :].bitcast(BF16)
                    for dc in range(2):
                        TR(tp[0:Tc, dc * 128:(dc + 1) * 128], kT[:, dc, c0:c0 + Tc], identb[:, :], K["kT"] + ["identb"], [PK[0]])
                    TS(kd[0:Tc, :], tp[0:Tc, 0:256], kdc, None, ALU.mult, None, [PK[0], "kdec"], K["kd"])
                    for dc in range(2):
                        MM(ps[1][0:Tc, 0:Tc], kT[:, dc, c0:c0 + Tc], qT[:, dc, c0:c0 + Tc], dc == 0, dc == 1, K["kT"] + K["qT"], [PK[1]])
                    TT_(sm[0:Tc, 0:Tc], ps[1][0:Tc, 0:Tc], DM[0:Tc, r, 0:Tc], ALU.mult, [PK[1], dmk], K["sm"])
                    if bat:
                        TT_(qsm[:, :, :, 0:Tc], qsT[:, :, c0:c0 + Tc].unsqueeze(2).to_broadcast([128, 2, NS, Tc]),
                            csel[:, :, 0:Tc].unsqueeze(1).to_broadcast([128, 2, NS, Tc]), ALU.mult, K["qsT"] + ["csel"], K["qsm"])
                    for ec in range(4):
                        o_ = ps[5][:, ec * Tc:(ec + 1) * Tc]
                        MM(o_, vtm[0:Tc, ec * 128:(ec + 1) * 128], sm[0:Tc, 0:Tc], True, False, K["vtm"] + K["sm"], [PK[5]])
                        if not bat:
                            MM(o_, rbf[:, 0, ec * 128:(ec + 1) * 128], qsT[:, 0, c0:c0 + Tc], False, False, ["rbf"] + K["qsT"], [PK[5]])
                            MM(o_, rbf[:, 1, ec * 128:(ec + 1) * 128], qsT[:, 1, c0:c0 + Tc], False, True, ["rbf"] + K["qsT"], [PK[5]])
                        else:
                            for i in range(NS):
                                for dc in range(2):
                                    MM(o_, rbs[:, i, dc, ec * 128:(ec + 1) * 128], qsm[:, dc, i, 0:Tc], False, (i == NS - 1 and dc == 1),
                                       K["rbs"] + K["qsm"], [PK[5]])
                    if not bat:
                        for dc in range(2):
                            b = 2 + dc
                            MM(ps[b][:, 0:512], kd[0:Tc, dc * 128:(dc + 1) * 128], vtm[0:Tc, :], True, True, K["kd"] + K["vtm"], [PK[b]])
                            STT(st[:, dc, :], st[:, dc, :], float(cdec), ps[b][:, 0:512], ALU.mult, ALU.add, stk + [PK[b]], stk)
                    else:
                        TT_(kdm[0:Tc, :, :], kd[0:Tc, :].unsqueeze(1).to_broadcast([Tc, NS, 256]),
                            seqsel[0:Tc, :].unsqueeze(2).to_broadcast([Tc, NS, 256]), ALU.mult, K["kd"] + ["seqsel"], K["kdm"])
                        for i in range(NS):
                            for dc in range(2):
                                b = 2 + (2 * i + dc) % 2 + (4 if (2 * i + dc) % 4 >= 2 else 0)
                                MM(ps[b][:, 0:512], kdm[0:Tc, i, dc * 128:(dc + 1) * 128], vtm[0:Tc, :], True, True, K["kdm"] + K["vtm"], [PK[b]])
                                STT(rs[:, i, dc, :], rs[:, i, dc, :], float(cdec), ps[b][:, 0:512], ALU.mult, ALU.add, stk + [PK[b]], stk)
                        for i in range(NS):
                            DMA(ret_s_d[s0 + i, r, :, :].rearrange("(c p) e -> p c e", p=128), rs[:, i, :, :], stk, [("rets", p, r, i)], semkey=("rsst", i))
                    yv = ps[5][:, 0:4 * Tc].rearrange("p (j t) -> p j t", j=4)
                    ACT(sqs[:, :, 0:Tc], yv, AF.Square, [PK[5]], K["sqs"])
                    for j in range(4):
                        MM(ps[1][:, 256:256 + Tc], onesb[:, :], sqs[:, j, 0:Tc], j == 0, j == 3, ["onesb"] + K["sqs"], [PK[1]])
                    ACT(rsn[:, 0:Tc], ps[1][:, 256:256 + Tc], AF.Sqrt, [PK[1], "epsc"], K["rsn"], bias=epsc[:, 0:1], scale=1.0 / 512)
                    RECIP(rsn[:, 0:Tc], rsn[:, 0:Tc], K["rsn"], K["rsn"])
                    TT_(yrt[:, :, 0:Tc], yv, rsn[:, 0:Tc].unsqueeze(1).to_broadcast([128, 4, Tc]), ALU.mult, [PK[5]] + K["rsn"], K["yrt"])
                    TT_(yg[:, 4 * r:4 * r + 4, tok0:tok0 + Tc], yrt[:, :, 0:Tc], gsg[:, :, c0:c0 + Tc], ALU.mult, K["yrt"] + K["gsg"], [("yg", ti)])
            if last_pass:
                DMA(ret_p_d[r, :, :].rearrange("(c p) e -> p c e", p=128), rst[:, r, :, :], [("rst", r)], [("retp", r)], semkey=("retp", r))

        for p in range(DBG["passes"]):
            last = p == NPASS - 1
            for ti, (t0, tn, _) in enumerate(TT):
                DMA(hT[:, :, t0:t0 + tn], xT_d[p, :, :, t0:t0 + tn], (), [HK(ti, m) for m in range(KD)], semkey=("xin", ti))
            for si in range(NS):
                DMA(cvin[:, si, :, :], conv_in_d[p * NS + si, :, :, :], (), ["cvin"], semkey=("cvin", si))
            P.mark('p%d ffn1' % p)
            T = stage_begin()
            T["sq"] = A("sq", [128, KD, 512], BF16)
            T["aT0"] = A("aT0", [128, 4, 512], BF16)
            T["aT1"] = A("aT1", [128, 4, 512], BF16)
            norm_stage(T, 0, tiling=TTD)
            ffn_stage(T, 0)
            if DBG["stop"] <= 1:
                continue
            P.mark('p%d mixnorm' % p)
            T = stage_begin()
            T["sq"] = A("sq", [128, KD, 512], BF16)
            T["smallt"] = A("smallt", [128, 4, 32], F32)
            norm_stage(T, 1)
            dt_stage(T)
            if DBG["stop"] <= 2:
                continue
            P.mark('p%d ssm' % p)
            T = stage_begin()
            for nm, shp, dt_ in (("raw2", [128, 2, 520], F32), ("rawS", [128, 6, NS, 7], F32), ("acc0", [128, 512], F32),
                                 ("acc1", [128, 512], F32), ("xc", [128, 6, 512], BF16), ("sz", [128, 4, 512], BF16),
                                 ("xdt0", [128, 512], BF16), ("xdtt0", [128, 512], BF16), ("btm0", [128, 128], BF16),
                                 ("xdt1", [128, 512], BF16), ("xdtt1", [128, 512], BF16), ("btm1", [128, 128], BF16),
                                 ("sm80", [128, 5, 8], F32), ("sm81", [128, 5, 8], F32), ("wT0", [128, 8, 128], BF16),
                                 ("wT1", [128, 8, 128], BF16), ("decs0", [128, 8 * NS], F32), ("decs1", [128, 8 * NS], F32),
                                 ("Rt", [128, 4, 128], F32), ("Et", [128, 4, 128], F32),
                                 ("cbm", [128, 128], F32), ("ytmp", [128, 512], F32),
                                 ("yt2", [128, 512], F32), ("ytm", [128, 512], F32), ("t4", [128, 512], F32),
                                 ("t3", [128, 4, 128], F32), ("ygf", [128, 4, 128], F32), ("sqs", [128, 4, 128], BF16),
                                 ("hs", [128, NS, 512], F32), ("hbs", [128, NS, 512], BF16),
                                 ("ctm", [128, NS, 32], BF16), ("btmm", [128, NS, 128], BF16)):
                T[nm] = A(nm, shp, dt_)
            T["mbf"] = A("mbf", [128, KD, NTP], BF16, alias="hs")
            for g in range(DBG["groups"]):
                ssm_group(T, g, p, last)
            if last:
                DMA(conv_p_d[:, :, :], convp_st[:, :, :], ["convp_st"], ["convp_out"], semkey="convp_out")
            if DBG["stop"] <= 3:
                continue
            for si in range(NS):
                DMA(conv_s_d[p * NS + si, :, :, :], cvout[:, si, :, :], ["cvout"], [("convs", p, si)], semkey=("cvout", si))
            P.mark('p%d fin_ssm' % p)
            finalize(T, wbs_d, OFF_GA, True)
            if DBG["stop"] <= 4:
                continue
            P.mark('p%d ret' % p)
            T = stage_begin()
            for nm, shp, dt_ in (("qkf", [128, 2, 2, 512], F32), ("rot", [128, 4, 512], F32), ("qT", [128, 2, 512], BF16),
                                 ("qsT", [128, 2, 512], BF16), ("kT", [128, 2, 512], BF16), ("gsg", [128, 4, 512], BF16),
                                 ("vtm", [128, 512], BF16), ("kd", [128, 256], BF16), ("sm", [128, 128], BF16),
                                 ("yrt", [128, 4, 128], F32), ("sqs", [128, 4, 128], BF16), ("rs", [128, NS, 2, 512], F32),
                                 ("rbs", [128, NS, 2, 512], BF16), ("qsm", [128, 2, NS, 32], BF16), ("kdm", [128, NS, 256], BF16),
                                 ("cosT", [128, NTP], F32), ("sinT", [128, NTP], F32), ("qdecT", [128, NTP], F32)):
                T[nm] = A(nm, shp, dt_)
            T["mbf"] = A("mbf", [128, KD, NTP], BF16, alias="qkf")
            DMA(T["cosT"][:, :], cos_d[p, :, :], (), K["cosT"], semkey="cosld")
            DMA(T["sinT"][:, :], sin_d[p, :, :], (), K["sinT"], semkey="sinld")
            for r in range(4):
                ret_head(T, r, p, last)
            if DBG["stop"] <= 5:
                continue
            P.mark('p%d fin_ret' % p)
            finalize(T, wbr_d, OFF_GB, False)
            if DBG["stop"] <= 6:
                continue
            P.mark('p%d ffn2' % p)
            T = stage_begin()
            T["sq"] = A("sq", [128, KD, 512], BF16)
            T["aT0"] = A("aT0", [128, 4, 512], BF16)
            T["aT1"] = A("aT1", [128, 4, 512], BF16)
            T["outT"] = A("outT", [128, KD, 512], F32)
            norm_stage(T, 2, tiling=TTD)
            ffn_stage(T, 1)
            norm_stage(T, 3, out_dma=yT_d[p], tiling=TTD)
        P.emit()
        build_nc.stats = P.stats
        build_nc.marks = P.marks
    return nc


def _host_consts():
    half = 128
    inv = (10000.0 ** (-np.arange(half, dtype=np.float32) / np.float32(half))).astype(np.float32)
    cosT = np.zeros((NPASS, 128, NTP), np.float32)
    sinT = np.zeros((NPASS, 128, NTP), np.float32)
    for p in range(NPASS):
        pos = np.concatenate([np.arange(p * NPT, (p + 1) * NPT, dtype=np.float32),
                              np.tile(PAST + np.arange(DEC_SEQ, dtype=np.float32), NS)]).astype(np.float32)
        ang = (pos[None, :] * inv[:, None]).astype(np.float32)
        cosT[p] = np.cos(ang.astype(np.float64)).astype(np.float32)
        sinT[p] = np.sin(ang.astype(np.float64)).astype(np.float32)
    lg = np.log1p(-np.exp2(-5.0 - np.arange(4, dtype=np.float64)))
    inchunk = np.concatenate([np.arange(NPT) % 128, np.tile(np.arange(DEC_SEQ), NS)]).astype(np.float64)
    qdecT = np.zeros((4, 128, NTP), np.float32)
    for h in range(4):
        qdecT[h] = np.exp((inchunk + 1.0) * lg[h])[None, :]
    kdec = np.zeros((128, 8), np.float32)
    for h in range(4):
        kdec[:, h] = np.exp((127.0 - np.arange(128)) * lg[h])
        kdec[0:DEC_SEQ, 4 + h] = np.exp((DEC_SEQ - 1.0 - np.arange(DEC_SEQ)) * lg[h])
    dmatT = np.zeros((128, 4, 128), np.float32)
    s_ = np.arange(128)[:, None]
    t_ = np.arange(128)[None, :]
    for h in range(4):
        dmatT[:, h, :] = np.where(t_ >= s_, np.exp(np.maximum(t_ - s_, 0) * lg[h]), 0.0)
    tri = (s_ <= t_).astype(np.float32)
    ustrict = (s_ > t_).astype(np.float32)
    identf = np.eye(128, dtype=np.float32)
    seq_of = np.arange(128) // DEC_SEQ
    same = (seq_of[:, None] == seq_of[None, :])
    tri_s = (same & (s_ <= t_)).astype(np.float32)
    ustr_s = (same & (s_ > t_)).astype(np.float32)
    blk_s = same.astype(np.float32)
    seqsel = (seq_of[:, None] == np.arange(NS)[None, :]).astype(np.float32)
    selB = np.ascontiguousarray(np.broadcast_to(seqsel[:, :, None], (128, NS, 128))).astype(np.float32)
    csel = np.ascontiguousarray(np.broadcast_to((np.arange(NS)[:, None] == seq_of[None, :32])[None], (128, NS, 32))).astype(np.float32)
    dmatS = np.zeros((128, 4, 128), np.float32)
    for h in range(4):
        dmatS[:, h, :] = np.where(same & (t_ >= s_), np.exp(np.maximum(t_ - s_, 0) * lg[h]), 0.0)
        kdec[:, 4 + h] = np.exp((DEC_SEQ - 1.0 - (np.arange(128) % DEC_SEQ)) * lg[h])
    return dict(cosT=cosT, sinT=sinT, qdecT=qdecT, kdec=kdec, dmatT=dmatT, tri=tri, ustrict=ustrict, identf=identf,
                tri_s=tri_s, ustr_s=ustr_s, blk_s=blk_s, seqsel=seqsel, selB=selB, csel=csel, dmatS=dmatS)


_NC_CACHE = {}
_PREP_ONLY = [False]


def kernel(x_prompt, x_sample, state_ssm, state_conv, state_ret,
           norm_ffn1, ffn1_w1, ffn1_w3, ffn1_w2, norm_mix, w_in, conv_w, conv_b,
           dt_bias, a_log, ssm_d, ssm_norm, ret_norm, w_branch_ssm, w_branch_ret, w_out,
           norm_ffn2, ffn2_w1, ffn2_w3, ffn2_w2, norm_final):
    f = lambda a: np.ascontiguousarray(np.asarray(a, dtype=np.float32))
    x_prompt, x_sample = f(x_prompt), f(x_sample)
    state_ssm, state_conv, state_ret = f(state_ssm)[0], f(state_conv)[0], f(state_ret)[0]
    SPC = DEC_B // NCORES
    consts = _host_consts()
    pp = lambda v: f(np.asarray(v).reshape(-1, 128).T)
    bc = lambda v: f(np.broadcast_to(np.asarray(v).reshape(1, -1), (128, np.asarray(v).size)))
    shared = dict(
        f1w1=f(ffn1_w1)[0], f1w3=f(ffn1_w3)[0], f1w2=f(ffn1_w2)[0],
        f2w1=f(ffn2_w1)[0], f2w3=f(ffn2_w3)[0], f2w2=f(ffn2_w2)[0],
        w_in=f(w_in)[0], w_bs=f(w_branch_ssm)[0], w_br=f(w_branch_ret)[0], w_out=f(w_out)[0],
        gains=f(np.stack([pp(norm_ffn1), pp(norm_mix), pp(norm_ffn2), pp(norm_final)], axis=1)),
        g_ssm=pp(ssm_norm), g_ret=pp(ret_norm),
        convw=f(np.asarray(conv_w)[0].reshape(4, 24, 128).transpose(2, 1, 0)),
        convb=pp(conv_b), dtb=bc(dt_bias), alog=bc(a_log),
        dpp=f(np.repeat(np.asarray(ssm_d).reshape(-1), 64).reshape(16, 128).T),
        **consts,
    )
    in_maps = []
    for c in range(NCORES):
        xs = x_sample[c * SPC:(c + 1) * SPC]
        xT = np.zeros((NPASS, 128, KD, NTP), np.float32)
        for p in range(NPASS):
            tok = np.concatenate([x_prompt[c, p * NPT:(p + 1) * NPT], xs[p * NS:(p + 1) * NS].reshape(NS * DEC_SEQ, D)], axis=0)
            xT[p] = tok.T.reshape(KD, 128, NTP).transpose(1, 0, 2)
        m = dict(shared)
        m["xT"] = xT
        m["ssmT_in"] = f(state_ssm[c * SPC:(c + 1) * SPC].reshape(SPC, 2048, 128).transpose(0, 2, 1))
        m["conv_in"] = f(state_conv[c * SPC:(c + 1) * SPC].reshape(SPC, 3, 24, 128).transpose(0, 3, 2, 1))
        m["ret_in"] = f(state_ret[c * SPC:(c + 1) * SPC])
        in_maps.append(m)
    if _PREP_ONLY[0]:
        return in_maps
    if "nc" not in _NC_CACHE:
        _NC_CACHE["nc"] = build_nc()
    nc = _NC_CACHE["nc"]
    res = run_bass_kernel_spmd(nc, in_maps, core_ids=list(range(NCORES)))
    return _post(res.results)


def _post(R):
    SPC = DEC_B // NCORES
    y_prompt = np.zeros((NCORES, SEQ, D), np.float32)
    y_sample = np.zeros((DEC_B, DEC_SEQ, D), np.float32)
    ssm_p = np.zeros((1, NCORES, 32, 64, 128), np.float32)
    conv_p = np.zeros((1, NCORES, 3, 3072), np.float32)
    ret_p = np.zeros((1, NCORES, 4, 256, 512), np.float32)
    ssm_s = np.zeros((1, DEC_B, 32, 64, 128), np.float32)
    conv_s = np.zeros((1, DEC_B, 3, 3072), np.float32)
    ret_s = np.zeros((1, DEC_B, 4, 256, 512), np.float32)
    for c in range(len(R)):
        r = R[c]
        yT = r["yT"]
        for p in range(NPASS):
            tok = yT[p].transpose(1, 0, 2).reshape(D, NTP).T
            y_prompt[c, p * NPT:(p + 1) * NPT] = tok[:NPT]
            y_sample[c * SPC + p * NS:c * SPC + (p + 1) * NS] = tok[NPT:].reshape(NS, DEC_SEQ, D)
        ssm_p[0, c] = r["ssmT_p"].T.reshape(32, 64, 128)
        conv_p[0, c] = r["conv_p"].transpose(2, 1, 0).reshape(3, 3072)
        ret_p[0, c] = r["ret_p"]
        ssm_s[0, c * SPC:(c + 1) * SPC] = r["ssmT_s"].transpose(0, 2, 1).reshape(SPC, 32, 64, 128)
        conv_s[0, c * SPC:(c + 1) * SPC] = r["conv_s"].transpose(0, 3, 2, 1).reshape(SPC, 3, 3072)
        ret_s[0, c * SPC:(c + 1) * SPC] = r["ret_s"]
    return (y_prompt, y_sample, ssm_p, conv_p, ret_p, ssm_s, conv_s, ret_s)
```

```python
import math
import contextlib
import numpy as np
import concourse.bass as bass
import concourse.mybir as mybir
from concourse.bass_utils import run_bass_kernel_spmd

F32 = mybir.dt.float32
BF16 = mybir.dt.bfloat16
AF = mybir.ActivationFunctionType
ALU = mybir.AluOpType

NCORES = 8
D = 1024
KD = 8
SEQ = 2048
DEC_B = 128
DEC_SEQ = 4
PAST = 16384
DFF = 2816
IN_DIM = 13344
OFF_Z, OFF_X, OFF_B, OFF_C, OFF_DT = 0, 2048, 4096, 4608, 5120
OFF_Q, OFF_K, OFF_V, OFF_G, OFF_GA, OFF_GB = 5152, 6176, 7200, 9248, 11296, 12320
NPASS = 4
PCH = (SEQ // 128) // NPASS
NS = (DEC_B // NCORES) // NPASS
NPT = PCH * 128
NTP = NPT + NS * DEC_SEQ
NCH = PCH + 1
EPS = 1e-6
EPS_G = 1e-5
GAMMAS = [1.0 - 2.0 ** (-5.0 - h) for h in range(4)]

SAME_ENG_DIST = 4
DBG = {"passes": 4, "stop": 99, "ssm_tiles": 99, "cut": 99, "groups": 4, "jcut": 99, "sub": 99}
STRICT_SAME_ENGINE = True
SEM_GEN_LIMIT = 12000


class Prog:
    ENGS = ("pe", "act", "dve", "pool", "sp")

    def __init__(self, nc):
        self.nc = nc
        self.ops = []
        self.marks = []

    def mark(self, label):
        self.marks.append((label, sum(1 for o in self.ops if o['eng'] == 'pe')))

    def op(self, eng, fn, rd=(), wr=(), dma=False, semkey=None):
        def _flat(ks):
            out = []
            for k_ in ks:
                if isinstance(k_, list):
                    out.extend(_flat(k_))
                else:
                    out.append(k_)
            return tuple(out)
        rd, wr = _flat(rd), _flat(wr)
        self.ops.append(dict(eng=eng, fn=fn, rd=rd, wr=wr, dma=dma, semkey=semkey))

    def emit(self, final_wait_eng="sp"):
        nc = self.nc
        ops = self.ops
        n = len(ops)
        last_w = {}
        readers = {}
        bank_last = {}
        eng_pos = {e: 0 for e in self.ENGS}
        for o in ops:
            o["pos"] = eng_pos[o["eng"]]
            eng_pos[o["eng"]] += 1
        deps = [None] * n
        for i, o in enumerate(ops):
            d = {}
            for k in o["rd"]:
                if k in last_w:
                    d[last_w[k]] = "raw"
            for k in o["wr"]:
                if k in last_w:
                    d.setdefault(last_w[k], "waw")
                for r in readers.get(k, ()):
                    if r != i:
                        d.setdefault(r, "war")
            for k in set(o["rd"]) | set(o["wr"]):
                if isinstance(k, tuple) and k and k[0] == "ps":
                    la = bank_last.setdefault(k, {})
                    for e2, j2 in la.items():
                        if e2 != o["eng"]:
                            d.setdefault(j2, "xeng")
                    la[o["eng"]] = i
            for k in o["rd"]:
                readers.setdefault(k, []).append(i)
            for k in o["wr"]:
                last_w[k] = i
                readers[k] = []
            keep = []
            for j, kind in d.items():
                pj = ops[j]
                if pj["eng"] == o["eng"] and not pj["dma"]:
                    if o["dma"]:
                        keep.append(j)
                    elif o["eng"] == "pe":
                        pass
                    elif STRICT_SAME_ENGINE:
                        keep.append(j)
                    elif kind == "raw" and (o["pos"] - pj["pos"]) <= SAME_ENG_DIST:
                        keep.append(j)
                else:
                    keep.append(j)
            deps[i] = keep
        signaling = [False] * n
        for i in range(n):
            for j in deps[i]:
                signaling[j] = True
        sems = []

        def new_sem(name):
            s = nc.alloc_semaphore(name)
            sems.append(s)
            return s

        eng_sem, eng_cnt, dma_sem, dma_cnt = {}, {}, {}, {}
        sig = [None] * n
        for i, o in enumerate(ops):
            if o["dma"]:
                k = o["semkey"] if o["semkey"] is not None else (o["wr"][0] if o["wr"] else ("dma", i))
                if k not in dma_sem:
                    dma_sem[k] = new_sem("d%d" % len(sems))
                    dma_cnt[k] = 0
                dma_cnt[k] += 16
                sig[i] = (dma_sem[k], dma_cnt[k])
            elif signaling[i]:
                e = o["eng"]
                if e not in eng_sem or eng_cnt[e] >= SEM_GEN_LIMIT:
                    eng_sem[e] = new_sem("e%s%d" % (e, len(sems)))
                    eng_cnt[e] = 0
                eng_cnt[e] += 1
                sig[i] = (eng_sem[e], eng_cnt[e])
        waited = {e: {} for e in self.ENGS}
        waits = [None] * n
        for i, o in enumerate(ops):
            need = {}
            for j in deps[i]:
                s, v = sig[j]
                key = id(s)
                if key not in need or need[key][1] < v:
                    need[key] = (s, v)
            w = []
            for key, (s, v) in need.items():
                if waited[o["eng"]].get(key, 0) >= v:
                    continue
                waited[o["eng"]][key] = v
                w.append((s, v))
            waits[i] = w
        finals = [(dma_sem[k], dma_cnt[k]) for k in dma_sem]
        by_eng = {e: [i for i, o in enumerate(ops) if o["eng"] == e] for e in self.ENGS}
        self.stats = {e: len(by_eng[e]) for e in self.ENGS}
        self.stats["waits"] = sum(len(w) for w in waits)
        self.stats["sems"] = len(sems)

        def run_engine(eng_name, eng):
            for i in by_eng[eng_name]:
                o = ops[i]
                for s, v in waits[i]:
                    eng.wait_ge(s, v)
                ins = o["fn"](eng)
                if sig[i] is not None:
                    ins.then_inc(sig[i][0], 16 if o["dma"] else 1)
            if eng_name == final_wait_eng:
                for s, v in finals:
                    eng.wait_ge(s, v)

        with nc.Block() as block:
            @block.tensor
            def _(e):
                run_engine("pe", e)

            @block.scalar
            def _(e):
                run_engine("act", e)

            @block.vector
            def _(e):
                run_engine("dve", e)

            @block.gpsimd
            def _(e):
                run_engine("pool", e)

            @block.sync
            def _(e):
                run_engine("sp", e)


def token_tiles():
    tiles = []
    t = 0
    while t < NPT:
        n = min(512, NPT - t)
        tiles.append((t, n, False))
        t += n
    tiles.append((NPT, NS * DEC_SEQ, True))
    return tiles


def build_nc():
    nc = bass.Bass("TRN2", target_bir_lowering=False)
    P = Prog(nc)

    def din(name, shape):
        return nc.dram_tensor(name, list(shape), F32, kind="ExternalInput").ap()

    def dout(name, shape):
        return nc.dram_tensor(name, list(shape), F32, kind="ExternalOutput").ap()

    xT_d = din("xT", [NPASS, 128, KD, NTP])
    w1_d = [din("f1w1", [D, DFF]), din("f2w1", [D, DFF])]
    w3_d = [din("f1w3", [D, DFF]), din("f2w3", [D, DFF])]
    w2_d = [din("f1w2", [DFF, D]), din("f2w2", [DFF, D])]
    win_d = din("w_in", [D, IN_DIM])
    wbs_d = din("w_bs", [2048, D])
    wbr_d = din("w_br", [2048, D])
    wout_d = din("w_out", [D, D])
    gains_d = din("gains", [128, 4, KD])
    gssm_d = din("g_ssm", [128, 16])
    gret_d = din("g_ret", [128, 16])
    convw_d = din("convw", [128, 24, 4])
    convb_d = din("convb", [128, 24])
    dtb_d = din("dtb", [128, 32])
    alog_d = din("alog", [128, 32])
    dpp_d = din("dpp", [128, 16])
    cos_d = din("cosT", [NPASS, 128, NTP])
    sin_d = din("sinT", [NPASS, 128, NTP])
    qdec_d = din("qdecT", [4, 128, NTP])
    kdec_d = din("kdec", [128, 8])
    dmat_d = din("dmatT", [128, 4, 128])
    tri_d = din("tri", [128, 128])
    ustr_d = din("ustrict", [128, 128])
    identf_d = din("identf", [128, 128])
    tris_d = din("tri_s", [128, 128])
    ustrs_d = din("ustr_s", [128, 128])
    blks_d = din("blk_s", [128, 128])
    selB_d = din("selB", [128, NS, 128])
    seqsel_d = din("seqsel", [128, NS])
    csel_d = din("csel", [128, NS, 32])
    dmatS_d = din("dmatS", [128, 4, 128])
    ssm_in_d = din("ssmT_in", [NS * NPASS, 128, 2048])
    conv_in_d = din("conv_in", [NS * NPASS, 128, 24, 3])
    ret_in_d = din("ret_in", [NS * NPASS, 4, 256, 512])

    yT_d = dout("yT", [NPASS, 128, KD, NTP])
    ssm_p_d = dout("ssmT_p", [128, 2048])
    conv_p_d = dout("conv_p", [128, 24, 3])
    ret_p_d = dout("ret_p", [4, 256, 512])
    ssm_s_d = dout("ssmT_s", [NS * NPASS, 128, 2048])
    conv_s_d = dout("conv_s", [NS * NPASS, 128, 24, 3])
    ret_s_d = dout("ret_s", [NS * NPASS, 4, 256, 512])

    TT = token_tiles()
    es = contextlib.ExitStack()
    with es:
        def S(name, shape, dt):
            return es.enter_context(nc.sbuf_tensor("s_" + name, list(shape), dt))

        ps = [es.enter_context(nc.psum_tensor("ps%d" % i, [128, 512], F32)) for i in range(8)]
        PK = [("ps", i) for i in range(8)]

        hT = S("hT", [128, KD, NTP], F32)
        uT = S("uT", [128, KD, NTP], BF16)
        yg = S("yg", [128, 16, NTP], BF16)
        NSLOT = 2
        WA = S("WA", [128, NSLOT, 12288], BF16)
        hst = S("hst", [128, 4, 512], F32)
        rst = S("rst", [128, 4, 2, 512], F32)
        hbf = S("hbf", [128, 512], BF16)
        rbf = S("rbf", [128, 2, 512], BF16)
        hist = S("hist", [128, 24, 3], F32)
        convp_st = S("convp_st", [128, 24, 3], F32)
        cvin = S("cvin", [128, NS, 24, 3], F32)
        cvout = S("cvout", [128, NS, 24, 3], F32)
        gains = S("gains", [128, 4, KD], F32)
        gssm = S("gssm", [128, 16], F32)
        gret = S("gret", [128, 16], F32)
        convw = S("convw", [128, 24, 4], F32)
        convb = S("convb", [128, 24], F32)
        dtb = S("dtb", [128, 32], F32)
        nega = S("nega", [128, 32], F32)
        dpp = S("dpp", [128, 16], F32)
        kdec = S("kdec", [128, 8], F32)
        dmatT = S("dmatT", [128, 4, 128], F32)
        tri = S("tri", [128, 128], F32)
        ustr = S("ustr", [128, 128], F32)
        identf = S("identf", [128, 128], F32)
        tri_s = S("tri_s", [128, 128], F32)
        ustr_s = S("ustr_s", [128, 128], F32)
        blk_s = S("blk_s", [128, 128], F32)
        selB = S("selB", [128, NS, 128], F32)
        seqsel = S("seqsel", [128, NS], F32)
        csel = S("csel", [128, NS, 32], F32)
        dmatS = S("dmatS", [128, 4, 128], F32)
        identb = S("identb", [128, 128], BF16)
        onesb = S("onesb", [128, 128], BF16)
        onesf = S("onesf", [128, 128], F32)
        epsc = S("epsc", [128, 4], F32)
        wdt = S("wdt", [128, KD, 32], BF16)
        dt_tm = S("dt_tm", [128, NCH, 32], F32)
        dta_tm = S("dta_tm", [128, NCH, 32], F32)
        ssq_b = S("ssq_b", [128, NTP], F32)
        NAR = 18432
        AR = S("AR", [128, NAR], F32)
        K = {}
        RG = 8
        ar_off = [0]

        ar_pos = {}

        def A(name, shape, dt, alias=None):
            n = 1
            for s_ in shape[1:]:
                n *= s_
            nf = n if dt == F32 else (n + 1) // 2
            nf = (nf + 7) // 8 * 8
            if alias is None:
                o = ar_off[0]
                assert o + nf <= NAR, (name, o, nf)
                ar_off[0] = o + nf
            else:
                o = ar_pos[alias]
                assert o + nf <= NAR, (name, o, nf)
            ar_pos[name] = o
            v = AR[0:shape[0], o:o + nf]
            if dt != F32:
                v = v.bitcast(BF16)
            v = v[:, 0:n]
            if len(shape) == 3:
                v = v.rearrange("p (a b) -> p a b", a=shape[1])
            elif len(shape) == 4:
                v = v.rearrange("p (a b c) -> p a b c", a=shape[1], b=shape[2])
            K[name] = [("AR", r) for r in range(o // RG, (o + nf - 1) // RG + 1)]
            return v

        def KR(name, lo, n):
            o = ar_pos[name] + lo
            return [("AR", r) for r in range(o // RG, (o + n - 1) // RG + 1)]

        def stage_begin():
            ar_off[0] = 0
            T = {}
            T["rsn"] = A("rsn", [128, 512], F32)
            T["sgt0"] = A("sgt0", [128, 512], F32)
            T["sgt1"] = A("sgt1", [128, 512], F32)
            return T

        def MM(out, lhsT, rhs, start, stop, rd, wr):
            P.op("pe", lambda e: e.matmul(out, lhsT=lhsT, rhs=rhs, start=start, stop=stop), rd, wr)

        def TR(out, in_, ident, rd, wr):
            P.op("pe", lambda e: e.transpose(out, in_, ident), rd, wr)

        def ACT(out, in_, func, rd, wr, bias=None, scale=None):
            kw = {}
            if bias is not None:
                kw["bias"] = bias
            if scale is not None:
                kw["scale"] = scale
            P.op("act", lambda e: e.activation(out=out, in_=in_, func=func, **kw), rd, wr)

        def TT_(out, in0, in1, op, rd, wr, eng="dve"):
            P.op(eng, lambda e: e.tensor_tensor(out=out, in0=in0, in1=in1, op=op), rd, wr)

        def TS(out, in0, s1, s2, op0, op1, rd, wr, eng="dve"):
            if s2 is None:
                P.op(eng, lambda e: e.tensor_scalar(out=out, in0=in0, scalar1=s1, scalar2=None, op0=op0), rd, wr)
            else:
                P.op(eng, lambda e: e.tensor_scalar(out=out, in0=in0, scalar1=s1, scalar2=s2, op0=op0, op1=op1), rd, wr)

        def STT(out, in0, scalar, in1, op0, op1, rd, wr, eng="dve"):
            P.op(eng, lambda e: e.scalar_tensor_tensor(out=out, in0=in0, scalar=scalar, in1=in1, op0=op0, op1=op1), rd, wr)

        def CP(out, in_, rd, wr, eng="dve"):
            P.op(eng, lambda e: e.tensor_copy(out=out, in_=in_), rd, wr)

        def RECIP(out, in_, rd, wr):
            P.op("dve", lambda e: e.reciprocal(out=out, in_=in_), rd, wr)

        def MEMSET(ap, val, wr, eng="dve"):
            P.op(eng, lambda e: e.memset(ap, val), (), wr)

        def DMA(out, in_, rd, wr, semkey=None, cast=False):
            P.op("pool" if cast else "sp", lambda e: e.dma_start(out=out, in_=in_), rd, wr, dma=True, semkey=semkey)

        def par2(fa, fb):
            n0 = len(P.ops)
            fa()
            n1 = len(P.ops)
            fb()
            n2 = len(P.ops)
            A_, B_ = P.ops[n0:n1], P.ops[n1:n2]
            merged = []
            ia = ib = 0
            while ia < len(A_) or ib < len(B_):
                if ib >= len(B_) or (ia < len(A_) and ia * len(B_) <= ib * len(A_)):
                    merged.append(A_[ia])
                    ia += 1
                else:
                    merged.append(B_[ib])
                    ib += 1
            P.ops[n0:n2] = merged

        wslot_ctr = [0]

        def next_slot():
            s = wslot_ctr[0] % NSLOT
            wslot_ctr[0] += 1
            return s

        def wreg(s, lo, hi):
            return [("W", s, r) for r in range(lo // 2048, (hi - 1) // 2048 + 1)]

        def wcols(dram, c0, cn):
            return dram.rearrange("(k p) n -> p k n", p=128)[:, :, c0:c0 + cn]

        def wload(s, lo, n, view, src, tag):
            ks = wreg(s, lo, lo + n)
            DMA(view, src, (), ks, cast=True, semkey=("Wsem", s, tag))
            return ks

        for t_, d_, kn in ((gains, gains_d, "gains"), (gssm, gssm_d, "gssm"), (gret, gret_d, "gret"), (convw, convw_d, "convw"),
                           (convb, convb_d, "convb"), (dtb, dtb_d, "dtb"), (dpp, dpp_d, "dpp"), (kdec, kdec_d, "kdec"),
                           (dmatT, dmat_d, "dmatT"), (tri, tri_d, "tri"), (ustr, ustr_d, "ustr"), (identf, identf_d, "identf"),
                           (tri_s, tris_d, "tri_s"), (ustr_s, ustrs_d, "ustr_s"), (blk_s, blks_d, "blk_s"), (selB, selB_d, "selB"),
                           (seqsel, seqsel_d, "seqsel"), (csel, csel_d, "csel"), (dmatS, dmatS_d, "dmatS")):
            sl = tuple(slice(None) for _ in t_.shape)
            DMA(t_[sl], d_[sl], (), [kn])
        DMA(nega[:, :], alog_d[:, :], (), ["nega"])
        ACT(nega[:, :], nega[:, :], AF.Exp, ["nega"], ["nega"])
        TS(nega[:, :], nega[:, :], -1.0, None, ALU.mult, None, ["nega"], ["nega"])
        CP(identb[:, :], identf[:, :], ["identf"], ["identb"])
        MEMSET(onesb[:, :], 1.0, ["onesb"])
        MEMSET(onesf[:, :], 1.0, ["onesf"])
        MEMSET(epsc[:, 0:1], EPS, ["epsc"])
        MEMSET(epsc[:, 1:2], EPS_G, ["epsc"])
        MEMSET(epsc[:, 2:3], 1.0, ["epsc"])
        MEMSET(hst[:, :, :], 0.0, [("hst", g) for g in range(4)])
        MEMSET(rst[:, :, :, :], 0.0, [("rst", r) for r in range(4)])
        MEMSET(hist[:, :, :], 0.0, [("hist", g) for g in range(4)])
        DMA(wdt[:, :, :], wcols(win_d, OFF_DT, 32), (), ["wdt"], cast=True)

        TTD = [(0, NTP // 2, False), (NTP // 2, NTP - NTP // 2, False)]
        SEGB = sorted(set([0, NTP] + [t_[0] for t_ in TT] + [t_[0] for t_ in TTD]))
        SEGS = [(SEGB[i_], SEGB[i_ + 1]) for i_ in range(len(SEGB) - 1)]
        CUR = [TT]

        def segs(ti):
            t0_, tn_ = CUR[0][ti][0], CUR[0][ti][1]
            return [s_ for s_, (a_, b_) in enumerate(SEGS) if a_ < t0_ + tn_ and b_ > t0_]

        HK = lambda ti, m: [("h", s_, m) for s_ in segs(ti)]
        YGK = lambda ti: [("yg", s_) for s_ in segs(ti)]
        SQK = lambda ti: [("ssq", s_) for s_ in segs(ti)]
        UK = lambda ti, k: [("u", s_, k) for s_ in segs(ti)]

        def norm_stage(T, gidx, out_dma=None, tiling=None):
            CUR[0] = tiling if tiling is not None else TT
            sq, rsn = T["sq"], T["rsn"]
            for ti, (t0, tn, _) in enumerate(CUR[0]):
                hk = [HK(ti, m) for m in range(KD)]
                ACT(sq[:, :, 0:tn], hT[:, :, t0:t0 + tn], AF.Square, hk, K["sq"])
                for k in range(KD):
                    MM(ps[7][:, 0:tn], onesb[:, :], sq[:, k, 0:tn], k == 0, k == KD - 1, K["sq"] + ["onesb"], [PK[7]])
                ACT(rsn[:, 0:tn], ps[7][:, 0:tn], AF.Sqrt, [PK[7], "epsc"], K["rsn"], bias=epsc[:, 0:1], scale=1.0 / D)
                RECIP(rsn[:, 0:tn], rsn[:, 0:tn], K["rsn"], K["rsn"])
                if out_dma is None:
                    for k in range(KD):
                        STT(uT[:, k, t0:t0 + tn], hT[:, k, t0:t0 + tn], gains[:, gidx, k:k + 1], rsn[:, 0:tn], ALU.mult, ALU.mult,
                            [HK(ti, k), "gains"] + K["rsn"], [UK(ti, k)])
                else:
                    outT = T["outT"]
                    for k in range(KD):
                        STT(outT[:, k, 0:tn], hT[:, k, t0:t0 + tn], gains[:, gidx, k:k + 1], rsn[:, 0:tn], ALU.mult, ALU.mult,
                            [HK(ti, k), "gains"] + K["rsn"], K["outT"])
                    DMA(out_dma[:, :, t0:t0 + tn], outT[:, :, 0:tn], K["outT"], [("yout", t0)], semkey="outT")
            CUR[0] = TT

        def ffn_stage(T, which):
            CUR[0] = TTD
            w1, w3, w2 = w1_d[which], w3_d[which], w2_d[which]
            aT = [T["aT0"], T["aT1"]]
            j0 = 0
            it = 0
            while j0 < DFF:
                jn = min(512, DFF - j0)
                nj = jn // 128
                s = next_slot()
                w1b = WA[:, s, 0:4096].rearrange("p (k n) -> p k n", k=KD)
                w3b = WA[:, s, 4096:8192].rearrange("p (k n) -> p k n", k=KD)
                w2b = WA[:, s, 8192:12288].rearrange("p (c n) -> p c n", c=4)
                k1 = wload(s, 0, 4096, w1b[:, :, 0:jn], wcols(w1, j0, jn), 0)
                k3 = wload(s, 4096, 4096, w3b[:, :, 0:jn], wcols(w3, j0, jn), 1)
                k2 = wload(s, 8192, 4096, w2b[:, 0:nj, :], w2[j0:j0 + jn, :].rearrange("(c p) n -> p c n", p=128), 2)
                for ti, (t0, tn, _) in enumerate(TTD):
                    a = aT[it % 2]
                    akj = [KR("aT%d" % (it % 2), 256 * jc_, 256) for jc_ in range(4)]
                    it += 1
                    for jc in range(nj):
                        gb, ub = jc % 2, 2 + jc % 2
                        for k in range(KD):
                            MM(ps[gb][:, 0:tn], w1b[:, k, jc * 128:(jc + 1) * 128], uT[:, k, t0:t0 + tn], k == 0, k == KD - 1,
                               k1 + [UK(ti, k)], [PK[gb]])
                        for k in range(KD):
                            MM(ps[ub][:, 0:tn], w3b[:, k, jc * 128:(jc + 1) * 128], uT[:, k, t0:t0 + tn], k == 0, k == KD - 1,
                               k3 + [UK(ti, k)], [PK[ub]])
                        sg = T["sgt%d" % (jc % 2)]
                        sgk = K["sgt%d" % (jc % 2)]
                        ACT(sg[:, 0:tn], ps[gb][:, 0:tn], AF.Silu, [PK[gb]], sgk)
                        TT_(a[:, jc, 0:tn], sg[:, 0:tn], ps[ub][:, 0:tn], ALU.mult, sgk + [PK[ub]], akj[jc])
                    for m in range(KD):
                        ob = 4 + m % 2
                        for jc in range(nj):
                            MM(ps[ob][:, 0:tn], w2b[:, jc, m * 128:(m + 1) * 128], a[:, jc, 0:tn], jc == 0, jc == nj - 1,
                               k2 + akj[jc], [PK[ob]])
                        STT(hT[:, m, t0:t0 + tn], ps[ob][:, 0:tn], 0.5, hT[:, m, t0:t0 + tn], ALU.mult, ALU.add,
                            [PK[ob], HK(ti, m)], [HK(ti, m)])
                j0 += jn
            CUR[0] = TT

        def chunks_of_tile(ti):
            t0, tn, is_s = TT[ti]
            if not is_s:
                return [(c * 128, 128, (t0 // 128) + c, t0 + c * 128, None) for c in range(tn // 128)]
            return [(0, NS * DEC_SEQ, PCH, t0, "batch")]

        def dt_stage(T):
            smallt = T["smallt"]
            for ti in range(len(TT)):
                for (c0, Tc, ci, tok0, si) in chunks_of_tile(ti):
                    for k in range(KD):
                        MM(ps[1][0:Tc, 0:32], uT[:, k, tok0:tok0 + Tc], wdt[:, k, :], k == 0, k == KD - 1, [UK(ti, k), "wdt"], [PK[1]])
                    xb, ax, ee, ll = smallt[0:Tc, 0, :], smallt[0:Tc, 1, :], smallt[0:Tc, 2, :], smallt[0:Tc, 3, :]
                    kk = K["smallt"]
                    TT_(xb, ps[1][0:Tc, 0:32], dtb[0:Tc, :], ALU.add, [PK[1], "dtb"], kk)
                    ACT(ax, xb, AF.Abs, kk, kk)
                    ACT(ee, ax, AF.Exp, kk, kk, scale=-1.0)
                    ACT(ll, ee, AF.Ln, kk + ["epsc"], kk, bias=epsc[0:Tc, 2:3])
                    STT(dt_tm[0:Tc, ci, :], xb, 0.0, ll, ALU.max, ALU.add, kk, [("dt", ci)])
                    TT_(dta_tm[0:Tc, ci, :], dt_tm[0:Tc, ci, :], nega[0:Tc, :], ALU.mult, [("dt", ci), "nega"], [("dta", ci)])

        def ssm_group(T, g, p, last_pass):
            s = next_slot()
            wz = WA[:, s, 0:4096].rearrange("p (k n) -> p k n", k=KD)
            wx = WA[:, s, 4096:8192].rearrange("p (k n) -> p k n", k=KD)
            wbc = WA[:, s, 8192:10240].rearrange("p (k n) -> p k n", k=KD)
            kz = wload(s, 0, 4096, wz, wcols(win_d, OFF_Z + 512 * g, 512), 0)
            kx = wload(s, 4096, 4096, wx, wcols(win_d, OFF_X + 512 * g, 512), 1)
            kb1 = wload(s, 8192, 2048, wbc[:, :, 0:128], wcols(win_d, OFF_B + 128 * g, 128), 2)
            kb2 = wload(s, 8192, 2048, wbc[:, :, 128:256], wcols(win_d, OFF_C + 128 * g, 128), 3)
            kbc = kb1
            gch = [4 * g + j for j in range(4)] + [16 + g, 20 + g]
            hk_ = ("hist", g)
            raw2, rawS, acc, xc, sz = T["raw2"], T["rawS"], [T["acc0"], T["acc1"]], T["xc"], T["sz"]
            Rt, Et, cbm = T["Rt"], T["Et"], T["cbm"]
            ytmp, yt2, ytm, t4, t3, ygf, sqs, hs = T["ytmp"], T["yt2"], T["ytm"], T["t4"], T["t3"], T["ygf"], T["sqs"], T["hs"]
            hbs, ctm, btmm = T["hbs"], T["ctm"], T["btmm"]
            for ti, (t0, tn, is_s) in enumerate(TT[:DBG["ssm_tiles"]]):
                for j in range(6):
                    cj = gch[j]
                    wv_ = wx[:, :, j * 128:(j + 1) * 128] if j < 4 else wbc[:, :, (j - 4) * 128:(j - 3) * 128]
                    wkeys = kx if j < 4 else kbc
                    b = 6 + j % 2
                    rw = raw2[:, j % 2, :]
                    rwk = KR("raw2", 520 * (j % 2), 520)
                    for k in range(KD):
                        MM(ps[b][:, 0:tn], wv_[:, k, :], uT[:, k, t0:t0 + tn], k == 0, k == KD - 1, wkeys + [UK(ti, k)], [PK[b]])
                    a_ = acc[j % 2]
                    akk = K["acc%d" % (j % 2)]
                    if not is_s:
                        CP(rw[:, 0:3], hist[:, 6 * g + j, :], [hk_], rwk)
                        ACT(rw[:, 3:3 + tn], ps[b][:, 0:tn], AF.Copy, [PK[b]], rwk)
                        CP(hist[:, 6 * g + j, :], rw[:, tn:tn + 3], rwk, [hk_])
                        if last_pass:
                            CP(convp_st[:, cj, :], rw[:, tn:tn + 3], rwk, ["convp_st"])
                        win = lambda kk_: rw[:, kk_:kk_ + tn]
                        av = a_[:, 0:tn]
                        xo = xc[:, j, 0:tn]
                        rdk = rwk
                    else:
                        CP(rawS[:, j, :, 0:3], cvin[:, :, cj, :], ["cvin"], K["rawS"])
                        ACT(rawS[:, j, :, 3:7], ps[b][:, 0:tn].rearrange("p (s t) -> p s t", t=DEC_SEQ), AF.Copy, [PK[b]], K["rawS"])
                        CP(cvout[:, :, cj, :], rawS[:, j, :, 4:7], K["rawS"], ["cvout"])
                        win = lambda kk_: rawS[:, j, :, kk_:kk_ + DEC_SEQ]
                        av = a_[:, 0:tn].rearrange("p (s t) -> p s t", t=DEC_SEQ)
                        xo = xc[:, j, 0:tn].rearrange("p (s t) -> p s t", t=DEC_SEQ)
                        rdk = K["rawS"]
                    if DBG["jcut"] <= 1:
                        continue
                    TS(av, win(0), convw[:, cj, 0:1], convb[:, cj:cj + 1], ALU.mult, ALU.add, rdk + ["convw", "convb"], akk)
                    if DBG["jcut"] <= 2:
                        continue
                    for kk_ in range(1, 4):
                        STT(av, win(kk_), convw[:, cj, kk_:kk_ + 1], av, ALU.mult, ALU.add, rdk + ["convw"] + akk, akk)
                    if DBG["jcut"] <= 3:
                        continue
                    ACT(xo, av, AF.Silu, akk, KR("xc", 256 * j, 256))
                for j in range(4):
                    b = 6 + j % 2
                    for k in range(KD):
                        MM(ps[b][:, 0:tn], wz[:, k, j * 128:(j + 1) * 128], uT[:, k, t0:t0 + tn], k == 0, k == KD - 1, kz + [UK(ti, k)], [PK[b]])
                    ACT(sz[:, j, 0:tn], ps[b][:, 0:tn], AF.Silu, [PK[b]], KR("sz", 256 * j, 256))
                hsl = slice(8 * g, 8 * g + 8)

                def phaseA(ch, par):
                    (c0, Tc, ci, tok0, si) = ch
                    bat = si is not None
                    TRI, USTR = (tri_s, ustr_s) if bat else (tri, ustr)
                    trk, usk = ("tri_s", "ustr_s") if bat else ("tri", "ustr")
                    xdt, xdtt, btm, wT, sm8, decs = (T["xdt%d" % par], T["xdtt%d" % par], T["btm%d" % par], T["wT%d" % par],
                                                     T["sm8%d" % par], T["decs%d" % par])
                    kxdt, kxdtt, kbtm, kwT, k8, kdecs = (K["xdt%d" % par], K["xdtt%d" % par], K["btm%d" % par], K["wT%d" % par],
                                                         K["sm8%d" % par], K["decs%d" % par])
                    la_sb, expla, lal_sb, dec_b, tail = sm8[:, 0, :], sm8[:, 1, :], sm8[:, 2, :], sm8[:, 3, :], sm8[:, 4, :]
                    k_la, k_ex, k_lal, k_dec, k_tl = [KR("sm8%d" % par, 8 * i_, 8) for i_ in range(5)]
                    tp = ps[0][:, :].bitcast(BF16)
                    for j in range(5):
                        TR(tp[0:Tc, j * 128:(j + 1) * 128], xc[:, j, c0:c0 + Tc], identb[:, :], KR("xc", 256 * j, 256) + ["identb"], [PK[0]])
                    dts = dt_tm[0:Tc, ci, hsl]
                    TT_(xdt[0:Tc, :].rearrange("p (h d) -> p h d", h=8), tp[0:Tc, 0:512].rearrange("p (h d) -> p h d", h=8),
                        dts.unsqueeze(2).to_broadcast([Tc, 8, 64]), ALU.mult, [PK[0], ("dt", ci)], kxdt)
                    CP(btm[0:Tc, :], tp[0:Tc, 512:640], [PK[0]], kbtm)
                    dtas = dta_tm[0:Tc, ci, hsl]
                    MM(ps[1][0:Tc, 0:8], TRI[0:Tc, 0:Tc], dtas, True, True, [trk, ("dta", ci)], [PK[1]])
                    if not bat:
                        MM(ps[1][:, 8:16], onesf[0:Tc, :], dtas, True, True, ["onesf", ("dta", ci)], [PK[1]])
                    else:
                        MM(ps[1][0:Tc, 8:16], blk_s[0:Tc, 0:Tc], dtas, True, True, ["blk_s", ("dta", ci)], [PK[1]])
                        for i in range(NS):
                            MM(ps[1][:, 16 + 8 * i:24 + 8 * i], selB[0:Tc, i, :], dtas, True, True, ["selB", ("dta", ci)], [PK[1]])
                    MM(ps[1][0:Tc, 128:128 + Tc], xc[:, 4, c0:c0 + Tc], xc[:, 5, c0:c0 + Tc], True, True, KR("xc", 1024, 512), [PK[1]])
                    ACT(la_sb[0:Tc, :], ps[1][0:Tc, 0:8], AF.Copy, [PK[1]], k_la)
                    ACT(expla[0:Tc, :], ps[1][0:Tc, 0:8], AF.Exp, [PK[1]], k_ex)
                    if not bat:
                        ACT(lal_sb[:, :], ps[1][:, 8:16], AF.Copy, [PK[1]], k_lal)
                        ACT(dec_b[:, :], ps[1][:, 8:16], AF.Exp, [PK[1]], k_dec)
                    else:
                        ACT(lal_sb[0:Tc, :], ps[1][0:Tc, 8:16], AF.Copy, [PK[1]], k_lal)
                        ACT(decs[:, :], ps[1][:, 16:16 + 8 * NS], AF.Exp, [PK[1]], kdecs)
                    TT_(cbm[0:Tc, 0:Tc], ps[1][0:Tc, 128:128 + Tc], TRI[0:Tc, 0:Tc], ALU.mult, [PK[1], trk], K["cbm"])
                    TT_(tail[0:Tc, :], lal_sb[0:Tc, :], la_sb[0:Tc, :], ALU.subtract, k_lal + k_la, k_tl)
                    ACT(tail[0:Tc, :], tail[0:Tc, :], AF.Exp, k_tl, k_tl)
                    TT_(xdtt[0:Tc, :].rearrange("p (h d) -> p h d", h=8), xdt[0:Tc, :].rearrange("p (h d) -> p h d", h=8),
                        tail[0:Tc, :].unsqueeze(2).to_broadcast([Tc, 8, 64]), ALU.mult, kxdt + k_tl, kxdtt)
                    for hf in range(2):
                        TT_(Rt[0:Tc, :, 0:Tc], dta_tm[0:Tc, ci, 8 * g + 4 * hf:8 * g + 4 * hf + 4].unsqueeze(2).to_broadcast([Tc, 4, Tc]),
                            TRI[0:Tc, 0:Tc].unsqueeze(1).to_broadcast([Tc, 4, Tc]), ALU.mult, [("dta", ci), trk], K["Rt"])
                        segv = ps[2 + hf][0:Tc, 0:4 * Tc].rearrange("p (h t) -> p h t", h=4)
                        MM(segv, USTR[0:Tc, 0:Tc], Rt[0:Tc, :, 0:Tc], True, True, [usk] + K["Rt"], [PK[2 + hf]])
                        ACT(Et[0:Tc, :, 0:Tc], segv, AF.Exp, [PK[2 + hf]], K["Et"])
                        TT_(wT[0:Tc, 4 * hf:4 * hf + 4, 0:Tc], Et[0:Tc, :, 0:Tc], cbm[0:Tc, 0:Tc].unsqueeze(1).to_broadcast([Tc, 4, Tc]), ALU.mult,
                            K["Et"] + K["cbm"], kwT)

                def phaseB(ch, par):
                    (c0, Tc, ci, tok0, si) = ch
                    bat = si is not None
                    xdt, xdtt, btm, wT, sm8, decs = (T["xdt%d" % par], T["xdtt%d" % par], T["btm%d" % par], T["wT%d" % par],
                                                     T["sm8%d" % par], T["decs%d" % par])
                    kxdt, kxdtt, kbtm, kwT, k8, kdecs = (K["xdt%d" % par], K["xdtt%d" % par], K["btm%d" % par], K["wT%d" % par],
                                                         K["sm8%d" % par], K["decs%d" % par])
                    expla, dec_b = sm8[:, 1, :], sm8[:, 3, :]
                    k_la, k_ex, k_lal, k_dec, k_tl = [KR("sm8%d" % par, 8 * i_, 8) for i_ in range(5)]
                    if not bat:
                        st, stk = hst[:, g, :], [("hst", g)]
                        ACT(hbf[:, :], st, AF.Copy, stk, ["hbf"])
                    else:
                        s0 = p * NS
                        stk = K["hs"]
                        DMA(hs[:, :, :], ssm_in_d[s0:s0 + NS, :, 512 * g:512 * g + 512].rearrange("s n f -> n s f"), (), stk, semkey="hsld")
                        ACT(hbs[:, :, :], hs[:, :, :], AF.Copy, stk, K["hbs"])
                    for hh in range(8):
                        MM(ps[4][0:Tc, hh * 64:(hh + 1) * 64], wT[0:Tc, hh, 0:Tc], xdt[0:Tc, hh * 64:(hh + 1) * 64], True, True,
                           kwT + kxdt, [PK[4]])
                    if not bat:
                        MM(ps[5][0:Tc, 0:512], xc[:, 5, c0:c0 + Tc], hbf[:, :], True, True, KR("xc", 1280, 256) + ["hbf"], [PK[5]])
                    else:
                        TT_(ctm[:, :, 0:Tc], xc[:, 5, c0:c0 + Tc].unsqueeze(1).to_broadcast([128, NS, Tc]), csel[:, :, 0:Tc], ALU.mult,
                            KR("xc", 1280, 256) + ["csel"], K["ctm"])
                        for i in range(NS):
                            MM(ps[5][0:Tc, 0:512], ctm[:, i, 0:Tc], hbs[:, i, :], i == 0, i == NS - 1, K["ctm"] + K["hbs"], [PK[5]])
                    ACT(ytmp[0:Tc, :], ps[4][0:Tc, :], AF.Copy, [PK[4]], K["ytmp"])
                    TT_(yt2[0:Tc, :].rearrange("p (h d) -> p h d", h=8), ps[5][0:Tc, :].rearrange("p (h d) -> p h d", h=8),
                        expla[0:Tc, :].unsqueeze(2).to_broadcast([Tc, 8, 64]), ALU.mult, [PK[5]] + k_ex, K["yt2"])
                    TT_(ytm[0:Tc, :], ytmp[0:Tc, :], yt2[0:Tc, :], ALU.add, K["ytmp"] + K["yt2"], K["ytm"])
                    if not bat:
                        MM(ps[7][:, 0:512], btm[0:Tc, :], xdtt[0:Tc, :], True, True, kbtm + kxdtt, [PK[7]])
                        TT_(t4[:, :].rearrange("p (h d) -> p h d", h=8), st.rearrange("p (h d) -> p h d", h=8),
                            dec_b[:, :].unsqueeze(2).to_broadcast([128, 8, 64]), ALU.mult, stk + k_dec, K["t4"])
                        TT_(st, t4[:, :], ps[7][:, 0:512], ALU.add, K["t4"] + [PK[7]], stk)
                    else:
                        TT_(btmm[0:Tc, :, :], btm[0:Tc, :].unsqueeze(1).to_broadcast([Tc, NS, 128]),
                            seqsel[0:Tc, :].unsqueeze(2).to_broadcast([Tc, NS, 128]), ALU.mult, kbtm + ["seqsel"], K["btmm"])
                        for i in range(NS):
                            b = 7 if i % 2 == 0 else 5
                            MM(ps[b][:, 0:512], btmm[0:Tc, i, :], xdtt[0:Tc, :], True, True, K["btmm"] + kxdtt, [PK[b]])
                            TT_(t4[:, :].rearrange("p (h d) -> p h d", h=8), hs[:, i, :].rearrange("p (h d) -> p h d", h=8),
                                decs[:, 8 * i:8 * i + 8].unsqueeze(2).to_broadcast([128, 8, 64]), ALU.mult, stk + kdecs, K["t4"])
                            TT_(hs[:, i, :], t4[:, :], ps[b][:, 0:512], ALU.add, K["t4"] + [PK[b]], stk)
                        DMA(ssm_s_d[s0:s0 + NS, :, 512 * g:512 * g + 512].rearrange("s n f -> n s f"), hs[:, :, :], stk, [("ssms", p, g)], semkey="hsst")
                    for j in range(4):
                        TR(ps[6][:, j * Tc:(j + 1) * Tc], ytm[0:Tc, j * 128:(j + 1) * 128], identf[0:Tc, 0:Tc], K["ytm"] + ["identf"], [PK[6]])
                    TT_(t3[:, :, 0:Tc], xc[:, 0:4, c0:c0 + Tc], dpp[:, 4 * g:4 * g + 4].unsqueeze(2).to_broadcast([128, 4, Tc]), ALU.mult,
                        KR("xc", 0, 1024) + ["dpp"], K["t3"])
                    TT_(t3[:, :, 0:Tc], t3[:, :, 0:Tc], ps[6][:, 0:4 * Tc].rearrange("p (j t) -> p j t", j=4), ALU.add, K["t3"] + [PK[6]], K["t3"])
                    TT_(ygf[:, :, 0:Tc], t3[:, :, 0:Tc], sz[:, :, c0:c0 + Tc], ALU.mult, K["t3"] + K["sz"], K["ygf"])
                    ACT(sqs[:, :, 0:Tc], ygf[:, :, 0:Tc], AF.Square, K["ygf"], K["sqs"])
                    TT_(yg[:, 4 * g:4 * g + 4, tok0:tok0 + Tc], ygf[:, :, 0:Tc],
                        gssm[:, 4 * g:4 * g + 4].unsqueeze(2).to_broadcast([128, 4, Tc]), ALU.mult, K["ygf"] + ["gssm"], [YGK(ti)])
                    for j in range(4):
                        MM(ps[4][:, 0:Tc], onesb[:, :], sqs[:, j, 0:Tc], j == 0, j == 3, ["onesb"] + K["sqs"], [PK[4]])
                    if g == 0:
                        CP(ssq_b[:, tok0:tok0 + Tc], ps[4][:, 0:Tc], [PK[4]], [SQK(ti)])
                    else:
                        TT_(ssq_b[:, tok0:tok0 + Tc], ssq_b[:, tok0:tok0 + Tc], ps[4][:, 0:Tc], ALU.add, [PK[4], SQK(ti)], [SQK(ti)])

                chs = chunks_of_tile(ti)
                phaseA(chs[0], 0)
                for idx in range(1, len(chs)):
                    par2(lambda: phaseA(chs[idx], idx % 2), lambda: phaseB(chs[idx - 1], (idx - 1) % 2))
                phaseB(chs[-1], (len(chs) - 1) % 2)
            if last_pass:
                DMA(ssm_p_d[:, 512 * g:512 * g + 512], hst[:, g, :], [("hst", g)], [("ssmp", g)], semkey=("ssmp", g))

        def finalize(T, wb_d, gate_off, use_rs):
            CUR[0] = TTD
            mbf = T["mbf"]
            mk = K["mbf"]
            for ti, (t0, tn, _) in enumerate(TTD):
                if use_rs:
                    ACT(ssq_b[:, t0:t0 + tn], ssq_b[:, t0:t0 + tn], AF.Sqrt, [SQK(ti), "epsc"], [SQK(ti)], bias=epsc[:, 1:2], scale=1.0 / 2048)
                    RECIP(ssq_b[:, t0:t0 + tn], ssq_b[:, t0:t0 + tn], [SQK(ti)], [SQK(ti)])
            for half in range(2):
                s = next_slot()
                wb = WA[:, s, 0:8192].rearrange("p (c n) -> p c n", c=16)
                wg = WA[:, s, 8192:12288].rearrange("p (k n) -> p k n", k=KD)
                kb_ = wload(s, 0, 8192, wb, wb_d.rearrange("(c p) n -> p c n", p=128)[:, :, half * 512:(half + 1) * 512], 0)
                kg_ = wload(s, 8192, 4096, wg, wcols(win_d, gate_off + half * 512, 512), 1)
                for mi in range(4):
                    m = 4 * half + mi
                    for ti, (t0, tn, _) in enumerate(TTD):
                        pa, pg = (0, 1) if (m + ti) % 2 == 0 else (4, 5)
                        for c in range(16):
                            MM(ps[pa][:, 0:tn], wb[:, c, mi * 128:(mi + 1) * 128], yg[:, c, t0:t0 + tn], c == 0, c == 15, kb_ + [YGK(ti)], [PK[pa]])
                        for k in range(KD):
                            MM(ps[pg][:, 0:tn], wg[:, k, mi * 128:(mi + 1) * 128], uT[:, k, t0:t0 + tn], k == 0, k == KD - 1, kg_ + [UK(ti, k)], [PK[pg]])
                        sg = T["sgt%d" % ((m + ti) % 2)]
                        sgk = K["sgt%d" % ((m + ti) % 2)]
                        ACT(sg[:, 0:tn], ps[pg][:, 0:tn], AF.Sigmoid, [PK[pg]], sgk)
                        if use_rs:
                            TT_(sg[:, 0:tn], sg[:, 0:tn], ssq_b[:, t0:t0 + tn], ALU.mult, sgk + [SQK(ti)], sgk)
                        TT_(mbf[:, m, t0:t0 + tn], ps[pa][:, 0:tn], sg[:, 0:tn], ALU.mult, [PK[pa]] + sgk, mk)
            s = next_slot()
            wo = WA[:, s, 0:8192].rearrange("p (k n) -> p k n", k=KD)
            ko_ = wload(s, 0, 8192, wo, wcols(wout_d, 0, 1024), 0)
            for m in range(KD):
                for ti, (t0, tn, _) in enumerate(TTD):
                    b = 2 + (m + ti) % 2
                    for k in range(KD):
                        MM(ps[b][:, 0:tn], wo[:, k, m * 128:(m + 1) * 128], mbf[:, k, t0:t0 + tn], k == 0, k == KD - 1, ko_ + mk, [PK[b]])
                    TT_(hT[:, m, t0:t0 + tn], hT[:, m, t0:t0 + tn], ps[b][:, 0:tn], ALU.add, [HK(ti, m), PK[b]], [HK(ti, m)])
            CUR[0] = TT

        def ret_head(T, r, p, last_pass):
            s = next_slot()
            wq = WA[:, s, 0:2048].rearrange("p (k n) -> p k n", k=KD)
            wkk = WA[:, s, 2048:4096].rearrange("p (k n) -> p k n", k=KD)
            wv = WA[:, s, 4096:8192].rearrange("p (k n) -> p k n", k=KD)
            wg = WA[:, s, 8192:12288].rearrange("p (k n) -> p k n", k=KD)
            kq = wload(s, 0, 2048, wq, wcols(win_d, OFF_Q + 256 * r, 256), 0)
            kk_ = wload(s, 2048, 2048, wkk, wcols(win_d, OFF_K + 256 * r, 256), 1)
            kv = wload(s, 4096, 4096, wv, wcols(win_d, OFF_V + 512 * r, 512), 2)
            kg = wload(s, 8192, 4096, wg, wcols(win_d, OFF_G + 512 * r, 512), 3)
            gam = GAMMAS[r]
            qkf, rot, qT, qsT, kT, gsg = T["qkf"], T["rot"], T["qT"], T["qsT"], T["kT"], T["gsg"]
            vtm, kd, sm, yrt, sqs, rs, rsn = T["vtm"], T["kd"], T["sm"], T["yrt"], T["sqs"], T["rs"], T["rsn"]
            rbs, qsm, kdm = T["rbs"], T["qsm"], T["kdm"]
            cosT, sinT, qdecT = T["cosT"], T["sinT"], T["qdecT"]
            DMA(qdecT[:, :], qdec_d[r, :, :], (), K["qdecT"], semkey="qdld")
            for ti, (t0, tn, is_s) in enumerate(TT):
                for qi, (wv_, wks) in enumerate(((wq, kq), (wkk, kk_))):
                    for dc in range(2):
                        b = 6 + dc
                        for k in range(KD):
                            MM(ps[b][:, 0:tn], wv_[:, k, dc * 128:(dc + 1) * 128], uT[:, k, t0:t0 + tn], k == 0, k == KD - 1, wks + [UK(ti, k)], [PK[b]])
                        ACT(qkf[:, qi, dc, 0:tn], ps[b][:, 0:tn], AF.Copy, [PK[b]], KR("qkf", 1024 * qi + 512 * dc, 512), scale=(1.0 if qi == 0 else 1.0 / 16.0))
                    t1, t2 = qkf[:, qi, 0, 0:tn], qkf[:, qi, 1, 0:tn]
                    cs, sn = cosT[:, t0:t0 + tn], sinT[:, t0:t0 + tn]
                    dst = qT if qi == 0 else kT
                    dk = K["qT"] if qi == 0 else K["kT"]
                    kq1, kq2 = KR("qkf", 1024 * qi, 512), KR("qkf", 1024 * qi + 512, 512)
                    kr = [KR("rot", 512 * i_, 512) for i_ in range(4)]
                    dk0, dk1 = KR("qT" if qi == 0 else "kT", 0, 256), KR("qT" if qi == 0 else "kT", 256, 256)
                    TT_(rot[:, 0, 0:tn], t1, cs, ALU.mult, kq1 + K["cosT"], kr[0])
                    TT_(rot[:, 1, 0:tn], t2, sn, ALU.mult, kq2 + K["sinT"], kr[1])
                    TT_(rot[:, 2, 0:tn], t1, sn, ALU.mult, kq1 + K["sinT"], kr[2])
                    TT_(rot[:, 3, 0:tn], t2, cs, ALU.mult, kq2 + K["cosT"], kr[3])
                    TT_(dst[:, 0, 0:tn], rot[:, 0, 0:tn], rot[:, 1, 0:tn], ALU.subtract, kr[0] + kr[1], dk0)
                    TT_(dst[:, 1, 0:tn], rot[:, 2, 0:tn], rot[:, 3, 0:tn], ALU.add, kr[2] + kr[3], dk1)
                TT_(qsT[:, :, 0:tn], qT[:, :, 0:tn], qdecT[:, t0:t0 + tn].unsqueeze(1).to_broadcast([128, 2, tn]), ALU.mult, K["qT"] + K["qdecT"], K["qsT"])
                for j in range(4):
                    b = 6 + j % 2
                    for k in range(KD):
                        MM(ps[b][:, 0:tn], wg[:, k, j * 128:(j + 1) * 128], uT[:, k, t0:t0 + tn], k == 0, k == KD - 1, kg + [UK(ti, k)], [PK[b]])
                    sg = T["sgt%d" % (j % 2)]
                    sgk = K["sgt%d" % (j % 2)]
                    ACT(sg[:, 0:tn], ps[b][:, 0:tn], AF.Silu, [PK[b]], sgk)
                    TS(gsg[:, j, 0:tn], sg[:, 0:tn], gret[:, 4 * r + j:4 * r + j + 1], None, ALU.mult, None, sgk + ["gret"], KR("gsg", 256 * j, 256))
                for (c0, Tc, ci, tok0, si) in chunks_of_tile(ti):
                    bat = si is not None
                    if not bat:
                        st, stk = rst[:, r, :, :], [("rst", r)]
                        kdc = kdec[0:Tc, r:r + 1]
                        cdec = gam ** 128
                        DM = dmatT
                        dmk = "dmatT"
                        ACT(rbf[:, :, :], st, AF.Copy, stk, ["rbf"])
                    else:
                        s0 = p * NS
                        stk = K["rs"]
                        for i in range(NS):
                            DMA(rs[:, i, :, :], ret_in_d[s0 + i, r, :, :].rearrange("(c p) e -> p c e", p=128), (), stk, semkey=("rsld", i))
                        kdc = kdec[0:Tc, 4 + r:5 + r]
                        cdec = gam ** DEC_SEQ
                        DM = dmatS
                        dmk = "dmatS"
                        ACT(rbs[:, :, :, :], rs[:, :, :, :], AF.Copy, stk, K["rbs"])
                    for k in range(KD):
                        MM(ps[4][0:Tc, 0:512], uT[:, k, tok0:tok0 + Tc], wv[:, k, :], k == 0, k == KD - 1, [UK(ti, k)] + kv, [PK[4]])
                    ACT(vtm[0:Tc, :], ps[4][0:Tc, :], AF.Copy, [PK[4]], K["vtm"])
                    tp = ps[0][:, :].bitcast(BF16)
                    for dc in range(2):
                        TR(tp[0:Tc, dc * 128:(dc + 1) * 128], kT[:, dc, c0:c0 + Tc], identb[:, :], K["kT"] + ["identb"], [PK[0]])
                    TS(kd[0:Tc, :], tp[0:Tc, 0:256], kdc, None, ALU.mult, None, [PK[0], "kdec"], K["kd"])
                    for dc in range(2):
                        MM(ps[1][0:Tc, 0:Tc], kT[:, dc, c0:c0 + Tc], qT[:, dc, c0:c0 + Tc], dc == 0, dc == 1, K["kT"] + K["qT"], [PK[1]])
                    TT_(sm[0:Tc, 0:Tc], ps[1][0:Tc, 0:Tc], DM[0:Tc, r, 0:Tc], ALU.mult, [PK[1], dmk], K["sm"])
                    if bat:
                        TT_(qsm[:, :, :, 0:Tc], qsT[:, :, c0:c0 + Tc].unsqueeze(2).to_broadcast([128, 2, NS, Tc]),
                            csel[:, :, 0:Tc].unsqueeze(1).to_broadcast([128, 2, NS, Tc]), ALU.mult, K["qsT"] + ["csel"], K["qsm"])
                    for ec in range(4):
                        o_ = ps[5][:, ec * Tc:(ec + 1) * Tc]
                        MM(o_, vtm[0:Tc, ec * 128:(ec + 1) * 128], sm[0:Tc, 0:Tc], True, False, K["vtm"] + K["sm"], [PK[5]])
                        if not bat:
                            MM(o_, rbf[:, 0, ec * 128:(ec + 1) * 128], qsT[:, 0, c0:c0 + Tc], False, False, ["rbf"] + K["qsT"], [PK[5]])
                            MM(o_, rbf[:, 1, ec * 128:(ec + 1) * 128], qsT[:, 1, c0:c0 + Tc], False, True, ["rbf"] + K["qsT"], [PK[5]])
                        else:
                            for i in range(NS):
                                for dc in range(2):
                                    MM(o_, rbs[:, i, dc, ec * 128:(ec + 1) * 128], qsm[:, dc, i, 0:Tc], False, (i == NS - 1 and dc == 1),
                                       K["rbs"] + K["qsm"], [PK[5]])
                    if not bat:
                        for dc in range(2):
                            b = 2 + dc
                            MM(ps[b][:, 0:512], kd[0:Tc, dc * 128:(dc + 1) * 128], vtm[0:Tc, :], True, True, K["kd"] + K["vtm"], [PK[b]])
                            STT(st[:, dc, :], st[:, dc, :], float(cdec), ps[b][:, 0:512], ALU.mult, ALU.add, stk + [PK[b]], stk)
                    else:
                        TT_(kdm[0:Tc, :, :], kd[0:Tc, :].unsqueeze(1).to_broadcast([Tc, NS, 256]),
                            seqsel[0:Tc, :].unsqueeze(2).to_broadcast([Tc, NS, 256]), ALU.mult, K["kd"] + ["seqsel"], K["kdm"])
                        for i in range(NS):
                            for dc in range(2):
                                b = 2 + (2 * i + dc) % 2 + (4 if (2 * i + dc) % 4 >= 2 else 0)
                                MM(ps[b][:, 0:512], kdm[0:Tc, i, dc * 128:(dc + 1) * 128], vtm[0:Tc, :], True, True, K["kdm"] + K["vtm"], [PK[b]])
                                STT(rs[:, i, dc, :], rs[:, i, dc, :], float(cdec), ps[b][:, 0:512], ALU.mult, ALU.add, stk + [PK[b]], stk)
                        for i in range(NS):
                            DMA(ret_s_d[s0 + i, r, :, :].rearrange("(c p) e -> p c e", p=128), rs[:, i, :, :], stk, [("rets", p, r, i)], semkey=("rsst", i))
                    yv = ps[5][:, 0:4 * Tc].rearrange("p (j t) -> p j t", j=4)
                    ACT(sqs[:, :, 0:Tc], yv, AF.Square, [PK[5]], K["sqs"])
                    for j in range(4):
                        MM(ps[1][:, 256:256 + Tc], onesb[:, :], sqs[:, j, 0:Tc], j == 0, j == 3, ["onesb"] + K["sqs"], [PK[1]])
                    ACT(rsn[:, 0:Tc], ps[1][:, 256:256 + Tc], AF.Sqrt, [PK[1], "epsc"], K["rsn"], bias=epsc[:, 0:1], scale=1.0 / 512)
                    RECIP(rsn[:, 0:Tc], rsn[:, 0:Tc], K["rsn"], K["rsn"])
                    TT_(yrt[:, :, 0:Tc], yv, rsn[:, 0:Tc].unsqueeze(1).to_broadcast([128, 4, Tc]), ALU.mult, [PK[5]] + K["rsn"], K["yrt"])
                    TT_(yg[:, 4 * r:4 * r + 4, tok0:tok0 + Tc], yrt[:, :, 0:Tc], gsg[:, :, c0:c0 + Tc], ALU.mult, K["yrt"] + K["gsg"], [YGK(ti)])
            if last_pass:
                DMA(ret_p_d[r, :, :].rearrange("(c p) e -> p c e", p=128), rst[:, r, :, :], [("rst", r)], [("retp", r)], semkey=("retp", r))

        for p in range(DBG["passes"]):
            last = p == NPASS - 1
            for ti, (t0, tn, _) in enumerate(TT):
                DMA(hT[:, :, t0:t0 + tn], xT_d[p, :, :, t0:t0 + tn], (), [HK(ti, m) for m in range(KD)], semkey=("xin", ti))
            for si in range(NS):
                DMA(cvin[:, si, :, :], conv_in_d[p * NS + si, :, :, :], (), ["cvin"], semkey=("cvin", si))
            P.mark('p%d ffn1' % p)
            T = stage_begin()
            T["sq"] = A("sq", [128, KD, 512], BF16)
            T["aT0"] = A("aT0", [128, 4, 512], BF16)
            T["aT1"] = A("aT1", [128, 4, 512], BF16)
            norm_stage(T, 0, tiling=TTD)
            ffn_stage(T, 0)
            if DBG["stop"] <= 1:
                continue
            P.mark('p%d mixnorm' % p)
            T = stage_begin()
            T["sq"] = A("sq", [128, KD, 512], BF16)
            T["smallt"] = A("smallt", [128, 4, 32], F32)
            norm_stage(T, 1)
            dt_stage(T)
            if DBG["stop"] <= 2:
                continue
            P.mark('p%d ssm' % p)
            T = stage_begin()
            for nm, shp, dt_ in (("raw2", [128, 2, 520], F32), ("rawS", [128, 6, NS, 7], F32), ("acc0", [128, 512], F32),
                                 ("acc1", [128, 512], F32), ("xc", [128, 6, 512], BF16), ("sz", [128, 4, 512], BF16),
                                 ("xdt0", [128, 512], BF16), ("xdtt0", [128, 512], BF16), ("btm0", [128, 128], BF16),
                                 ("xdt1", [128, 512], BF16), ("xdtt1", [128, 512], BF16), ("btm1", [128, 128], BF16),
                                 ("sm80", [128, 5, 8], F32), ("sm81", [128, 5, 8], F32), ("wT0", [128, 8, 128], BF16),
                                 ("wT1", [128, 8, 128], BF16), ("decs0", [128, 8 * NS], F32), ("decs1", [128, 8 * NS], F32),
                                 ("Rt", [128, 4, 128], F32), ("Et", [128, 4, 128], F32),
                                 ("cbm", [128, 128], F32), ("ytmp", [128, 512], F32),
                                 ("yt2", [128, 512], F32), ("ytm", [128, 512], F32), ("t4", [128, 512], F32),
                                 ("t3", [128, 4, 128], F32), ("ygf", [128, 4, 128], F32), ("sqs", [128, 4, 128], BF16),
                                 ("hs", [128, NS, 512], F32), ("hbs", [128, NS, 512], BF16),
                                 ("ctm", [128, NS, 32], BF16), ("btmm", [128, NS, 128], BF16)):
                T[nm] = A(nm, shp, dt_)
            T["mbf"] = A("mbf", [128, KD, NTP], BF16, alias="hs")
            for g in range(DBG["groups"]):
                ssm_group(T, g, p, last)
            if last:
                DMA(conv_p_d[:, :, :], convp_st[:, :, :], ["convp_st"], ["convp_out"], semkey="convp_out")
            if DBG["stop"] <= 3:
                continue
            for si in range(NS):
                DMA(conv_s_d[p * NS + si, :, :, :], cvout[:, si, :, :], ["cvout"], [("convs", p, si)], semkey=("cvout", si))
            P.mark('p%d fin_ssm' % p)
            finalize(T, wbs_d, OFF_GA, True)
            if DBG["stop"] <= 4:
                continue
            P.mark('p%d ret' % p)
            T = stage_begin()
            for nm, shp, dt_ in (("qkf", [128, 2, 2, 512], F32), ("rot", [128, 4, 512], F32), ("qT", [128, 2, 512], BF16),
                                 ("qsT", [128, 2, 512], BF16), ("kT", [128, 2, 512], BF16), ("gsg", [128, 4, 512], BF16),
                                 ("vtm", [128, 512], BF16), ("kd", [128, 256], BF16), ("sm", [128, 128], BF16),
                                 ("yrt", [128, 4, 128], F32), ("sqs", [128, 4, 128], BF16), ("rs", [128, NS, 2, 512], F32),
                                 ("rbs", [128, NS, 2, 512], BF16), ("qsm", [128, 2, NS, 32], BF16), ("kdm", [128, NS, 256], BF16),
                                 ("cosT", [128, NTP], F32), ("sinT", [128, NTP], F32), ("qdecT", [128, NTP], F32)):
                T[nm] = A(nm, shp, dt_)
            T["mbf"] = A("mbf", [128, KD, NTP], BF16, alias="qkf")
            DMA(T["cosT"][:, :], cos_d[p, :, :], (), K["cosT"], semkey="cosld")
            DMA(T["sinT"][:, :], sin_d[p, :, :], (), K["sinT"], semkey="sinld")
            for r in range(4):
                ret_head(T, r, p, last)
            if DBG["stop"] <= 5:
                continue
            P.mark('p%d fin_ret' % p)
            finalize(T, wbr_d, OFF_GB, False)
            if DBG["stop"] <= 6:
                continue
            P.mark('p%d ffn2' % p)
            T = stage_begin()
            T["sq"] = A("sq", [128, KD, 512], BF16)
            T["aT0"] = A("aT0", [128, 4, 512], BF16)
            T["aT1"] = A("aT1", [128, 4, 512], BF16)
            T["outT"] = A("outT", [128, KD, 512], F32)
            norm_stage(T, 2, tiling=TTD)
            ffn_stage(T, 1)
            norm_stage(T, 3, out_dma=yT_d[p], tiling=TTD)
        P.emit()
        build_nc.stats = P.stats
        build_nc.marks = P.marks
    return nc


def _host_consts():
    half = 128
    inv = (10000.0 ** (-np.arange(half, dtype=np.float32) / np.float32(half))).astype(np.float32)
    cosT = np.zeros((NPASS, 128, NTP), np.float32)
    sinT = np.zeros((NPASS, 128, NTP), np.float32)
    for p in range(NPASS):
        pos = np.concatenate([np.arange(p * NPT, (p + 1) * NPT, dtype=np.float32),
                              np.tile(PAST + np.arange(DEC_SEQ, dtype=np.float32), NS)]).astype(np.float32)
        ang = (pos[None, :] * inv[:, None]).astype(np.float32)
        cosT[p] = np.cos(ang.astype(np.float64)).astype(np.float32)
        sinT[p] = np.sin(ang.astype(np.float64)).astype(np.float32)
    lg = np.log1p(-np.exp2(-5.0 - np.arange(4, dtype=np.float64)))
    inchunk = np.concatenate([np.arange(NPT) % 128, np.tile(np.arange(DEC_SEQ), NS)]).astype(np.float64)
    qdecT = np.zeros((4, 128, NTP), np.float32)
    for h in range(4):
        qdecT[h] = np.exp((inchunk + 1.0) * lg[h])[None, :]
    kdec = np.zeros((128, 8), np.float32)
    for h in range(4):
        kdec[:, h] = np.exp((127.0 - np.arange(128)) * lg[h])
        kdec[0:DEC_SEQ, 4 + h] = np.exp((DEC_SEQ - 1.0 - np.arange(DEC_SEQ)) * lg[h])
    dmatT = np.zeros((128, 4, 128), np.float32)
    s_ = np.arange(128)[:, None]
    t_ = np.arange(128)[None, :]
    for h in range(4):
        dmatT[:, h, :] = np.where(t_ >= s_, np.exp(np.maximum(t_ - s_, 0) * lg[h]), 0.0)
    tri = (s_ <= t_).astype(np.float32)
    ustrict = (s_ > t_).astype(np.float32)
    identf = np.eye(128, dtype=np.float32)
    seq_of = np.arange(128) // DEC_SEQ
    same = (seq_of[:, None] == seq_of[None, :])
    tri_s = (same & (s_ <= t_)).astype(np.float32)
    ustr_s = (same & (s_ > t_)).astype(np.float32)
    blk_s = same.astype(np.float32)
    seqsel = (seq_of[:, None] == np.arange(NS)[None, :]).astype(np.float32)
    selB = np.ascontiguousarray(np.broadcast_to(seqsel[:, :, None], (128, NS, 128))).astype(np.float32)
    csel = np.ascontiguousarray(np.broadcast_to((np.arange(NS)[:, None] == seq_of[None, :32])[None], (128, NS, 32))).astype(np.float32)
    dmatS = np.zeros((128, 4, 128), np.float32)
    for h in range(4):
        dmatS[:, h, :] = np.where(same & (t_ >= s_), np.exp(np.maximum(t_ - s_, 0) * lg[h]), 0.0)
        kdec[:, 4 + h] = np.exp((DEC_SEQ - 1.0 - (np.arange(128) % DEC_SEQ)) * lg[h])
    return dict(cosT=cosT, sinT=sinT, qdecT=qdecT, kdec=kdec, dmatT=dmatT, tri=tri, ustrict=ustrict, identf=identf,
                tri_s=tri_s, ustr_s=ustr_s, blk_s=blk_s, seqsel=seqsel, selB=selB, csel=csel, dmatS=dmatS)


_NC_CACHE = {}
_PREP_ONLY = [False]


def kernel(x_prompt, x_sample, state_ssm, state_conv, state_ret,
           norm_ffn1, ffn1_w1, ffn1_w3, ffn1_w2, norm_mix, w_in, conv_w, conv_b,
           dt_bias, a_log, ssm_d, ssm_norm, ret_norm, w_branch_ssm, w_branch_ret, w_out,
           norm_ffn2, ffn2_w1, ffn2_w3, ffn2_w2, norm_final):
    f = lambda a: np.ascontiguousarray(np.asarray(a, dtype=np.float32))
    x_prompt, x_sample = f(x_prompt), f(x_sample)
    state_ssm, state_conv, state_ret = f(state_ssm)[0], f(state_conv)[0], f(state_ret)[0]
    SPC = DEC_B // NCORES
    consts = _host_consts()
    pp = lambda v: f(np.asarray(v).reshape(-1, 128).T)
    bc = lambda v: f(np.broadcast_to(np.asarray(v).reshape(1, -1), (128, np.asarray(v).size)))
    shared = dict(
        f1w1=f(ffn1_w1)[0], f1w3=f(ffn1_w3)[0], f1w2=f(ffn1_w2)[0],
        f2w1=f(ffn2_w1)[0], f2w3=f(ffn2_w3)[0], f2w2=f(ffn2_w2)[0],
        w_in=f(w_in)[0], w_bs=f(w_branch_ssm)[0], w_br=f(w_branch_ret)[0], w_out=f(w_out)[0],
        gains=f(np.stack([pp(norm_ffn1), pp(norm_mix), pp(norm_ffn2), pp(norm_final)], axis=1)),
        g_ssm=pp(ssm_norm), g_ret=pp(ret_norm),
        convw=f(np.asarray(conv_w)[0].reshape(4, 24, 128).transpose(2, 1, 0)),
        convb=pp(conv_b), dtb=bc(dt_bias), alog=bc(a_log),
        dpp=f(np.repeat(np.asarray(ssm_d).reshape(-1), 64).reshape(16, 128).T),
        **consts,
    )
    in_maps = []
    for c in range(NCORES):
        xs = x_sample[c * SPC:(c + 1) * SPC]
        xT = np.zeros((NPASS, 128, KD, NTP), np.float32)
        for p in range(NPASS):
            tok = np.concatenate([x_prompt[c, p * NPT:(p + 1) * NPT], xs[p * NS:(p + 1) * NS].reshape(NS * DEC_SEQ, D)], axis=0)
            xT[p] = tok.T.reshape(KD, 128, NTP).transpose(1, 0, 2)
        m = dict(shared)
        m["xT"] = xT
        m["ssmT_in"] = f(state_ssm[c * SPC:(c + 1) * SPC].reshape(SPC, 2048, 128).transpose(0, 2, 1))
        m["conv_in"] = f(state_conv[c * SPC:(c + 1) * SPC].reshape(SPC, 3, 24, 128).transpose(0, 3, 2, 1))
        m["ret_in"] = f(state_ret[c * SPC:(c + 1) * SPC])
        in_maps.append(m)
    if _PREP_ONLY[0]:
        return in_maps
    if "nc" not in _NC_CACHE:
        _NC_CACHE["nc"] = build_nc()
    nc = _NC_CACHE["nc"]
    res = run_bass_kernel_spmd(nc, in_maps, core_ids=list(range(NCORES)))
    return _post(res.results)


def _post(R):
    SPC = DEC_B // NCORES
    y_prompt = np.zeros((NCORES, SEQ, D), np.float32)
    y_sample = np.zeros((DEC_B, DEC_SEQ, D), np.float32)
    ssm_p = np.zeros((1, NCORES, 32, 64, 128), np.float32)
    conv_p = np.zeros((1, NCORES, 3, 3072), np.float32)
    ret_p = np.zeros((1, NCORES, 4, 256, 512), np.float32)
    ssm_s = np.zeros((1, DEC_B, 32, 64, 128), np.float32)
    conv_s = np.zeros((1, DEC_B, 3, 3072), np.float32)
    ret_s = np.zeros((1, DEC_B, 4, 256, 512), np.float32)
    for c in range(len(R)):
        r = R[c]
        yT = r["yT"]
        for p in range(NPASS):
            tok = yT[p].transpose(1, 0, 2).reshape(D, NTP).T
            y_prompt[c, p * NPT:(p + 1) * NPT] = tok[:NPT]
            y_sample[c * SPC + p * NS:c * SPC + (p + 1) * NS] = tok[NPT:].reshape(NS, DEC_SEQ, D)
        ssm_p[0, c] = r["ssmT_p"].T.reshape(32, 64, 128)
        conv_p[0, c] = r["conv_p"].transpose(2, 1, 0).reshape(3, 3072)
        ret_p[0, c] = r["ret_p"]
        ssm_s[0, c * SPC:(c + 1) * SPC] = r["ssmT_s"].transpose(0, 2, 1).reshape(SPC, 32, 64, 128)
        conv_s[0, c * SPC:(c + 1) * SPC] = r["conv_s"].transpose(0, 3, 2, 1).reshape(SPC, 3, 3072)
        ret_s[0, c * SPC:(c + 1) * SPC] = r["ret_s"]
    return (y_prompt, y_sample, ssm_p, conv_p, ret_p, ssm_s, conv_s, ret_s)
```

```python
import math
import contextlib
import numpy as np
import concourse.bass as bass
import concourse.mybir as mybir
from concourse.bass_utils import run_bass_kernel_spmd

F32 = mybir.dt.float32
BF16 = mybir.dt.bfloat16
AF = mybir.ActivationFunctionType
ALU = mybir.AluOpType

NCORES = 8
D = 1024
KD = 8
SEQ = 2048
DEC_B = 128
DEC_SEQ = 4
PAST = 16384
DFF = 2816
IN_DIM = 13344
OFF_Z, OFF_X, OFF_B, OFF_C, OFF_DT = 0, 2048, 4096, 4608, 5120
OFF_Q, OFF_K, OFF_V, OFF_G, OFF_GA, OFF_GB = 5152, 6176, 7200, 9248, 11296, 12320
NPASS = 4
PCH = (SEQ // 128) // NPASS
NS = (DEC_B // NCORES) // NPASS
NPT = PCH * 128
NTP = NPT + NS * DEC_SEQ
NCH = PCH + 1
EPS = 1e-6
EPS_G = 1e-5
GAMMAS = [1.0 - 2.0 ** (-5.0 - h) for h in range(4)]

SAME_ENG_DIST = 4
DBG = {"passes": 4, "stop": 99, "ssm_tiles": 99, "cut": 99, "groups": 4, "jcut": 99, "sub": 99}
STRICT_SAME_ENGINE = False
SEM_GEN_LIMIT = 12000


class Prog:
    ENGS = ("pe", "act", "dve", "pool", "sp")

    def __init__(self, nc):
        self.nc = nc
        self.ops = []
        self.marks = []

    def mark(self, label):
        self.marks.append((label, sum(1 for o in self.ops if o['eng'] == 'pe')))

    def op(self, eng, fn, rd=(), wr=(), dma=False, semkey=None):
        def _flat(ks):
            out = []
            for k_ in ks:
                if isinstance(k_, list):
                    out.extend(_flat(k_))
                else:
                    out.append(k_)
            return tuple(out)
        rd, wr = _flat(rd), _flat(wr)
        self.ops.append(dict(eng=eng, fn=fn, rd=rd, wr=wr, dma=dma, semkey=semkey))

    def emit(self, final_wait_eng="sp"):
        nc = self.nc
        ops = self.ops
        n = len(ops)
        last_w = {}
        readers = {}
        bank_last = {}
        eng_pos = {e: 0 for e in self.ENGS}
        for o in ops:
            o["pos"] = eng_pos[o["eng"]]
            eng_pos[o["eng"]] += 1
        deps = [None] * n
        for i, o in enumerate(ops):
            d = {}
            for k in o["rd"]:
                if k in last_w:
                    d[last_w[k]] = "raw"
            for k in o["wr"]:
                if k in last_w:
                    d.setdefault(last_w[k], "waw")
                for r in readers.get(k, ()):
                    if r != i:
                        d.setdefault(r, "war")
            for k in set(o["rd"]) | set(o["wr"]):
                if isinstance(k, tuple) and k and k[0] == "ps":
                    la = bank_last.setdefault(k, {})
                    for e2, j2 in la.items():
                        if e2 != o["eng"]:
                            d.setdefault(j2, "xeng")
                    la[o["eng"]] = i
            for k in o["rd"]:
                readers.setdefault(k, []).append(i)
            for k in o["wr"]:
                last_w[k] = i
                readers[k] = []
            keep = []
            for j, kind in d.items():
                pj = ops[j]
                if pj["eng"] == o["eng"] and not pj["dma"]:
                    if o["dma"]:
                        keep.append(j)
                    elif o["eng"] == "pe":
                        pass
                    elif STRICT_SAME_ENGINE:
                        keep.append(j)
                    elif kind == "raw" and (o["pos"] - pj["pos"]) <= SAME_ENG_DIST:
                        keep.append(j)
                else:
                    keep.append(j)
            deps[i] = keep
        signaling = [False] * n
        for i in range(n):
            for j in deps[i]:
                signaling[j] = True
        sems = []

        def new_sem(name):
            s = nc.alloc_semaphore(name)
            sems.append(s)
            return s

        eng_sem, eng_cnt, dma_sem, dma_cnt = {}, {}, {}, {}
        sig = [None] * n
        for i, o in enumerate(ops):
            if o["dma"]:
                k = o["semkey"] if o["semkey"] is not None else (o["wr"][0] if o["wr"] else ("dma", i))
                if k not in dma_sem:
                    dma_sem[k] = new_sem("d%d" % len(sems))
                    dma_cnt[k] = 0
                dma_cnt[k] += 16
                sig[i] = (dma_sem[k], dma_cnt[k])
            elif signaling[i]:
                e = o["eng"]
                if e not in eng_sem or eng_cnt[e] >= SEM_GEN_LIMIT:
                    eng_sem[e] = new_sem("e%s%d" % (e, len(sems)))
                    eng_cnt[e] = 0
                eng_cnt[e] += 1
                sig[i] = (eng_sem[e], eng_cnt[e])
        waited = {e: {} for e in self.ENGS}
        waits = [None] * n
        for i, o in enumerate(ops):
            need = {}
            for j in deps[i]:
                s, v = sig[j]
                key = id(s)
                if key not in need or need[key][1] < v:
                    need[key] = (s, v)
            w = []
            for key, (s, v) in need.items():
                if waited[o["eng"]].get(key, 0) >= v:
                    continue
                waited[o["eng"]][key] = v
                w.append((s, v))
            waits[i] = w
        finals = [(dma_sem[k], dma_cnt[k]) for k in dma_sem]
        by_eng = {e: [i for i, o in enumerate(ops) if o["eng"] == e] for e in self.ENGS}
        self.stats = {e: len(by_eng[e]) for e in self.ENGS}
        self.stats["waits"] = sum(len(w) for w in waits)
        self.stats["sems"] = len(sems)

        def run_engine(eng_name, eng):
            for i in by_eng[eng_name]:
                o = ops[i]
                for s, v in waits[i]:
                    eng.wait_ge(s, v)
                ins = o["fn"](eng)
                if sig[i] is not None:
                    ins.then_inc(sig[i][0], 16 if o["dma"] else 1)
            if eng_name == final_wait_eng:
                for s, v in finals:
                    eng.wait_ge(s, v)

        with nc.Block() as block:
            @block.tensor
            def _(e):
                run_engine("pe", e)

            @block.scalar
            def _(e):
                run_engine("act", e)

            @block.vector
            def _(e):
                run_engine("dve", e)

            @block.gpsimd
            def _(e):
                run_engine("pool", e)

            @block.sync
            def _(e):
                run_engine("sp", e)


def token_tiles():
    tiles = []
    t = 0
    while t < NPT:
        n = min(512, NPT - t)
        tiles.append((t, n, False))
        t += n
    tiles.append((NPT, NS * DEC_SEQ, True))
    return tiles


def build_nc():
    nc = bass.Bass("TRN2", target_bir_lowering=False)
    P = Prog(nc)

    def din(name, shape):
        return nc.dram_tensor(name, list(shape), F32, kind="ExternalInput").ap()

    def dout(name, shape):
        return nc.dram_tensor(name, list(shape), F32, kind="ExternalOutput").ap()

    xT_d = din("xT", [NPASS, 128, KD, NTP])
    w1_d = [din("f1w1", [D, DFF]), din("f2w1", [D, DFF])]
    w3_d = [din("f1w3", [D, DFF]), din("f2w3", [D, DFF])]
    w2_d = [din("f1w2", [DFF, D]), din("f2w2", [DFF, D])]
    win_d = din("w_in", [D, IN_DIM])
    wbs_d = din("w_bs", [2048, D])
    wbr_d = din("w_br", [2048, D])
    wout_d = din("w_out", [D, D])
    gains_d = din("gains", [128, 4, KD])
    gssm_d = din("g_ssm", [128, 16])
    gret_d = din("g_ret", [128, 16])
    convw_d = din("convw", [128, 24, 4])
    convb_d = din("convb", [128, 24])
    dtb_d = din("dtb", [128, 32])
    alog_d = din("alog", [128, 32])
    dpp_d = din("dpp", [128, 16])
    cos_d = din("cosT", [NPASS, 128, NTP])
    sin_d = din("sinT", [NPASS, 128, NTP])
    qdec_d = din("qdecT", [4, 128, NTP])
    kdec_d = din("kdec", [128, 8])
    dmat_d = din("dmatT", [128, 4, 128])
    tri_d = din("tri", [128, 128])
    ustr_d = din("ustrict", [128, 128])
    identf_d = din("identf", [128, 128])
    tris_d = din("tri_s", [128, 128])
    ustrs_d = din("ustr_s", [128, 128])
    blks_d = din("blk_s", [128, 128])
    selB_d = din("selB", [128, NS, 128])
    seqsel_d = din("seqsel", [128, NS])
    csel_d = din("csel", [128, NS, 32])
    dmatS_d = din("dmatS", [128, 4, 128])
    ssm_in_d = din("ssmT_in", [NS * NPASS, 128, 2048])
    conv_in_d = din("conv_in", [NS * NPASS, 128, 24, 3])
    ret_in_d = din("ret_in", [NS * NPASS, 4, 256, 512])

    yT_d = dout("yT", [NPASS, 128, KD, NTP])
    ssm_p_d = dout("ssmT_p", [128, 2048])
    conv_p_d = dout("conv_p", [128, 24, 3])
    ret_p_d = dout("ret_p", [4, 256, 512])
    ssm_s_d = dout("ssmT_s", [NS * NPASS, 128, 2048])
    conv_s_d = dout("conv_s", [NS * NPASS, 128, 24, 3])
    ret_s_d = dout("ret_s", [NS * NPASS, 4, 256, 512])

    TT = token_tiles()
    es = contextlib.ExitStack()
    with es:
        def S(name, shape, dt):
            return es.enter_context(nc.sbuf_tensor("s_" + name, list(shape), dt))

        ps = [es.enter_context(nc.psum_tensor("ps%d" % i, [128, 512], F32)) for i in range(8)]
        PK = [("ps", i) for i in range(8)]

        hT = S("hT", [128, KD, NTP], F32)
        uT = S("uT", [128, KD, NTP], BF16)
        yg = S("yg", [128, 16, NTP], BF16)
        NSLOT = 2
        WA = S("WA", [128, NSLOT, 12288], BF16)
        hst = S("hst", [128, 4, 512], F32)
        rst = S("rst", [128, 4, 2, 512], F32)
        hbf = S("hbf", [128, 512], BF16)
        rbf = S("rbf", [128, 2, 512], BF16)
        hist = S("hist", [128, 24, 3], F32)
        convp_st = S("convp_st", [128, 24, 3], F32)
        cvin = S("cvin", [128, NS, 24, 3], F32)
        cvout = S("cvout", [128, NS, 24, 3], F32)
        gains = S("gains", [128, 4, KD], F32)
        gssm = S("gssm", [128, 16], F32)
        gret = S("gret", [128, 16], F32)
        convw = S("convw", [128, 24, 4], F32)
        convb = S("convb", [128, 24], F32)
        dtb = S("dtb", [128, 32], F32)
        nega = S("nega", [128, 32], F32)
        dpp = S("dpp", [128, 16], F32)
        kdec = S("kdec", [128, 8], F32)
        dmatT = S("dmatT", [128, 4, 128], F32)
        tri = S("tri", [128, 128], F32)
        ustr = S("ustr", [128, 128], F32)
        identf = S("identf", [128, 128], F32)
        tri_s = S("tri_s", [128, 128], F32)
        ustr_s = S("ustr_s", [128, 128], F32)
        blk_s = S("blk_s", [128, 128], F32)
        selB = S("selB", [128, NS, 128], F32)
        seqsel = S("seqsel", [128, NS], F32)
        csel = S("csel", [128, NS, 32], F32)
        dmatS = S("dmatS", [128, 4, 128], F32)
        identb = S("identb", [128, 128], BF16)
        onesb = S("onesb", [128, 128], BF16)
        onesf = S("onesf", [128, 128], F32)
        epsc = S("epsc", [128, 4], F32)
        wdt = S("wdt", [128, KD, 32], BF16)
        dt_tm = S("dt_tm", [128, NCH, 32], F32)
        dta_tm = S("dta_tm", [128, NCH, 32], F32)
        ssq_b = S("ssq_b", [128, NTP], F32)
        NAR = 18432
        AR = S("AR", [128, NAR], F32)
        K = {}
        RG = 8
        ar_off = [0]

        ar_pos = {}

        def A(name, shape, dt, alias=None):
            n = 1
            for s_ in shape[1:]:
                n *= s_
            nf = n if dt == F32 else (n + 1) // 2
            nf = (nf + 7) // 8 * 8
            if alias is None:
                o = ar_off[0]
                assert o + nf <= NAR, (name, o, nf)
                ar_off[0] = o + nf
            else:
                o = ar_pos[alias]
                assert o + nf <= NAR, (name, o, nf)
            ar_pos[name] = o
            v = AR[0:shape[0], o:o + nf]
            if dt != F32:
                v = v.bitcast(BF16)
            v = v[:, 0:n]
            if len(shape) == 3:
                v = v.rearrange("p (a b) -> p a b", a=shape[1])
            elif len(shape) == 4:
                v = v.rearrange("p (a b c) -> p a b c", a=shape[1], b=shape[2])
            K[name] = [("AR", r) for r in range(o // RG, (o + nf - 1) // RG + 1)]
            return v

        def KR(name, lo, n):
            o = ar_pos[name] + lo
            return [("AR", r) for r in range(o // RG, (o + n - 1) // RG + 1)]

        def stage_begin():
            ar_off[0] = 0
            T = {}
            T["rsn"] = A("rsn", [128, 512], F32)
            T["sgt0"] = A("sgt0", [128, 512], F32)
            T["sgt1"] = A("sgt1", [128, 512], F32)
            return T

        def MM(out, lhsT, rhs, start, stop, rd, wr):
            P.op("pe", lambda e: e.matmul(out, lhsT=lhsT, rhs=rhs, start=start, stop=stop), rd, wr)

        def TR(out, in_, ident, rd, wr):
            P.op("pe", lambda e: e.transpose(out, in_, ident), rd, wr)

        def ACT(out, in_, func, rd, wr, bias=None, scale=None):
            kw = {}
            if bias is not None:
                kw["bias"] = bias
            if scale is not None:
                kw["scale"] = scale
            P.op("act", lambda e: e.activation(out=out, in_=in_, func=func, **kw), rd, wr)

        def TT_(out, in0, in1, op, rd, wr, eng="dve"):
            P.op(eng, lambda e: e.tensor_tensor(out=out, in0=in0, in1=in1, op=op), rd, wr)

        def TS(out, in0, s1, s2, op0, op1, rd, wr, eng="dve"):
            if s2 is None:
                P.op(eng, lambda e: e.tensor_scalar(out=out, in0=in0, scalar1=s1, scalar2=None, op0=op0), rd, wr)
            else:
                P.op(eng, lambda e: e.tensor_scalar(out=out, in0=in0, scalar1=s1, scalar2=s2, op0=op0, op1=op1), rd, wr)

        def STT(out, in0, scalar, in1, op0, op1, rd, wr, eng="dve"):
            P.op(eng, lambda e: e.scalar_tensor_tensor(out=out, in0=in0, scalar=scalar, in1=in1, op0=op0, op1=op1), rd, wr)

        def CP(out, in_, rd, wr, eng="dve"):
            P.op(eng, lambda e: e.tensor_copy(out=out, in_=in_), rd, wr)

        def RECIP(out, in_, rd, wr):
            P.op("dve", lambda e: e.reciprocal(out=out, in_=in_), rd, wr)

        def MEMSET(ap, val, wr, eng="dve"):
            P.op(eng, lambda e: e.memset(ap, val), (), wr)

        def DMA(out, in_, rd, wr, semkey=None, cast=False):
            P.op("pool" if cast else "sp", lambda e: e.dma_start(out=out, in_=in_), rd, wr, dma=True, semkey=semkey)

        def par2(fa, fb):
            n0 = len(P.ops)
            fa()
            n1 = len(P.ops)
            fb()
            n2 = len(P.ops)
            A_, B_ = P.ops[n0:n1], P.ops[n1:n2]
            merged = []
            ia = ib = 0
            while ia < len(A_) or ib < len(B_):
                if ib >= len(B_) or (ia < len(A_) and ia * len(B_) <= ib * len(A_)):
                    merged.append(A_[ia])
                    ia += 1
                else:
                    merged.append(B_[ib])
                    ib += 1
            P.ops[n0:n2] = merged

        wslot_ctr = [0]

        def next_slot():
            s = wslot_ctr[0] % NSLOT
            wslot_ctr[0] += 1
            return s

        def wreg(s, lo, hi):
            return [("W", s, r) for r in range(lo // 2048, (hi - 1) // 2048 + 1)]

        def wcols(dram, c0, cn):
            return dram.rearrange("(k p) n -> p k n", p=128)[:, :, c0:c0 + cn]

        def wload(s, lo, n, view, src, tag):
            ks = wreg(s, lo, lo + n)
            DMA(view, src, (), ks, cast=True, semkey=("Wsem", s, tag))
            return ks

        for t_, d_, kn in ((gains, gains_d, "gains"), (gssm, gssm_d, "gssm"), (gret, gret_d, "gret"), (convw, convw_d, "convw"),
                           (convb, convb_d, "convb"), (dtb, dtb_d, "dtb"), (dpp, dpp_d, "dpp"), (kdec, kdec_d, "kdec"),
                           (dmatT, dmat_d, "dmatT"), (tri, tri_d, "tri"), (ustr, ustr_d, "ustr"), (identf, identf_d, "identf"),
                           (tri_s, tris_d, "tri_s"), (ustr_s, ustrs_d, "ustr_s"), (blk_s, blks_d, "blk_s"), (selB, selB_d, "selB"),
                           (seqsel, seqsel_d, "seqsel"), (csel, csel_d, "csel"), (dmatS, dmatS_d, "dmatS")):
            sl = tuple(slice(None) for _ in t_.shape)
            DMA(t_[sl], d_[sl], (), [kn])
        DMA(nega[:, :], alog_d[:, :], (), ["nega"])
        ACT(nega[:, :], nega[:, :], AF.Exp, ["nega"], ["nega"])
        TS(nega[:, :], nega[:, :], -1.0, None, ALU.mult, None, ["nega"], ["nega"])
        CP(identb[:, :], identf[:, :], ["identf"], ["identb"])
        MEMSET(onesb[:, :], 1.0, ["onesb"])
        MEMSET(onesf[:, :], 1.0, ["onesf"])
        MEMSET(epsc[:, 0:1], EPS, ["epsc"])
        MEMSET(epsc[:, 1:2], EPS_G, ["epsc"])
        MEMSET(epsc[:, 2:3], 1.0, ["epsc"])
        MEMSET(hst[:, :, :], 0.0, [("hst", g) for g in range(4)])
        MEMSET(rst[:, :, :, :], 0.0, [("rst", r) for r in range(4)])
        MEMSET(hist[:, :, :], 0.0, [("hist", g) for g in range(4)])
        DMA(wdt[:, :, :], wcols(win_d, OFF_DT, 32), (), ["wdt"], cast=True)

        TTD = [(0, NTP // 2, False), (NTP // 2, NTP - NTP // 2, False)]
        SEGB = sorted(set([0, NTP] + [t_[0] for t_ in TT] + [t_[0] for t_ in TTD]))
        SEGS = [(SEGB[i_], SEGB[i_ + 1]) for i_ in range(len(SEGB) - 1)]
        CUR = [TT]

        def segs(ti):
            t0_, tn_ = CUR[0][ti][0], CUR[0][ti][1]
            return [s_ for s_, (a_, b_) in enumerate(SEGS) if a_ < t0_ + tn_ and b_ > t0_]

        HK = lambda ti, m: [("h", s_, m) for s_ in segs(ti)]
        YGK = lambda ti: [("yg", s_) for s_ in segs(ti)]
        SQK = lambda ti: [("ssq", s_) for s_ in segs(ti)]
        UK = lambda ti, k: [("u", s_, k) for s_ in segs(ti)]

        def norm_stage(T, gidx, out_dma=None, tiling=None):
            CUR[0] = tiling if tiling is not None else TT
            sq, rsn = T["sq"], T["rsn"]
            for ti, (t0, tn, _) in enumerate(CUR[0]):
                hk = [HK(ti, m) for m in range(KD)]
                ACT(sq[:, :, 0:tn], hT[:, :, t0:t0 + tn], AF.Square, hk, K["sq"])
                for k in range(KD):
                    MM(ps[7][:, 0:tn], onesb[:, :], sq[:, k, 0:tn], k == 0, k == KD - 1, K["sq"] + ["onesb"], [PK[7]])
                ACT(rsn[:, 0:tn], ps[7][:, 0:tn], AF.Sqrt, [PK[7], "epsc"], K["rsn"], bias=epsc[:, 0:1], scale=1.0 / D)
                RECIP(rsn[:, 0:tn], rsn[:, 0:tn], K["rsn"], K["rsn"])
                if out_dma is None:
                    for k in range(KD):
                        STT(uT[:, k, t0:t0 + tn], hT[:, k, t0:t0 + tn], gains[:, gidx, k:k + 1], rsn[:, 0:tn], ALU.mult, ALU.mult,
                            [HK(ti, k), "gains"] + K["rsn"], [UK(ti, k)])
                else:
                    outT = T["outT"]
                    for k in range(KD):
                        STT(outT[:, k, 0:tn], hT[:, k, t0:t0 + tn], gains[:, gidx, k:k + 1], rsn[:, 0:tn], ALU.mult, ALU.mult,
                            [HK(ti, k), "gains"] + K["rsn"], K["outT"])
                    DMA(out_dma[:, :, t0:t0 + tn], outT[:, :, 0:tn], K["outT"], [("yout", t0)], semkey="outT")
            CUR[0] = TT

        def ffn_stage(T, which):
            CUR[0] = TTD
            w1, w3, w2 = w1_d[which], w3_d[which], w2_d[which]
            aT = [T["aT0"], T["aT1"]]
            j0 = 0
            it = 0
            while j0 < DFF:
                jn = min(512, DFF - j0)
                nj = jn // 128
                s = next_slot()
                w1b = WA[:, s, 0:4096].rearrange("p (k n) -> p k n", k=KD)
                w3b = WA[:, s, 4096:8192].rearrange("p (k n) -> p k n", k=KD)
                w2b = WA[:, s, 8192:12288].rearrange("p (c n) -> p c n", c=4)
                k1 = wload(s, 0, 4096, w1b[:, :, 0:jn], wcols(w1, j0, jn), 0)
                k3 = wload(s, 4096, 4096, w3b[:, :, 0:jn], wcols(w3, j0, jn), 1)
                k2 = wload(s, 8192, 4096, w2b[:, 0:nj, :], w2[j0:j0 + jn, :].rearrange("(c p) n -> p c n", p=128), 2)
                for ti, (t0, tn, _) in enumerate(TTD):
                    a = aT[it % 2]
                    akj = [KR("aT%d" % (it % 2), 256 * jc_, 256) for jc_ in range(4)]
                    it += 1
                    for jc in range(nj):
                        gb, ub = jc % 2, 2 + jc % 2
                        for k in range(KD):
                            MM(ps[gb][:, 0:tn], w1b[:, k, jc * 128:(jc + 1) * 128], uT[:, k, t0:t0 + tn], k == 0, k == KD - 1,
                               k1 + [UK(ti, k)], [PK[gb]])
                        for k in range(KD):
                            MM(ps[ub][:, 0:tn], w3b[:, k, jc * 128:(jc + 1) * 128], uT[:, k, t0:t0 + tn], k == 0, k == KD - 1,
                               k3 + [UK(ti, k)], [PK[ub]])
                        sg = T["sgt%d" % (jc % 2)]
                        sgk = K["sgt%d" % (jc % 2)]
                        ACT(sg[:, 0:tn], ps[gb][:, 0:tn], AF.Silu, [PK[gb]], sgk)
                        TT_(a[:, jc, 0:tn], sg[:, 0:tn], ps[ub][:, 0:tn], ALU.mult, sgk + [PK[ub]], akj[jc])
                    for m in range(KD):
                        ob = 4 + m % 2
                        for jc in range(nj):
                            MM(ps[ob][:, 0:tn], w2b[:, jc, m * 128:(m + 1) * 128], a[:, jc, 0:tn], jc == 0, jc == nj - 1,
                               k2 + akj[jc], [PK[ob]])
                        STT(hT[:, m, t0:t0 + tn], ps[ob][:, 0:tn], 0.5, hT[:, m, t0:t0 + tn], ALU.mult, ALU.add,
                            [PK[ob], HK(ti, m)], [HK(ti, m)])
                j0 += jn
            CUR[0] = TT

        def chunks_of_tile(ti):
            t0, tn, is_s = TT[ti]
            if not is_s:
                return [(c * 128, 128, (t0 // 128) + c, t0 + c * 128, None) for c in range(tn // 128)]
            return [(0, NS * DEC_SEQ, PCH, t0, "batch")]

        def dt_stage(T):
            smallt = T["smallt"]
            for ti in range(len(TT)):
                for (c0, Tc, ci, tok0, si) in chunks_of_tile(ti):
                    for k in range(KD):
                        MM(ps[1][0:Tc, 0:32], uT[:, k, tok0:tok0 + Tc], wdt[:, k, :], k == 0, k == KD - 1, [UK(ti, k), "wdt"], [PK[1]])
                    xb, ax, ee, ll = smallt[0:Tc, 0, :], smallt[0:Tc, 1, :], smallt[0:Tc, 2, :], smallt[0:Tc, 3, :]
                    kk = K["smallt"]
                    TT_(xb, ps[1][0:Tc, 0:32], dtb[0:Tc, :], ALU.add, [PK[1], "dtb"], kk)
                    ACT(ax, xb, AF.Abs, kk, kk)
                    ACT(ee, ax, AF.Exp, kk, kk, scale=-1.0)
                    ACT(ll, ee, AF.Ln, kk + ["epsc"], kk, bias=epsc[0:Tc, 2:3])
                    STT(dt_tm[0:Tc, ci, :], xb, 0.0, ll, ALU.max, ALU.add, kk, [("dt", ci)])
                    TT_(dta_tm[0:Tc, ci, :], dt_tm[0:Tc, ci, :], nega[0:Tc, :], ALU.mult, [("dt", ci), "nega"], [("dta", ci)])

        def ssm_group(T, g, p, last_pass):
            s = next_slot()
            wz = WA[:, s, 0:4096].rearrange("p (k n) -> p k n", k=KD)
            wx = WA[:, s, 4096:8192].rearrange("p (k n) -> p k n", k=KD)
            wbc = WA[:, s, 8192:10240].rearrange("p (k n) -> p k n", k=KD)
            kz = wload(s, 0, 4096, wz, wcols(win_d, OFF_Z + 512 * g, 512), 0)
            kx = wload(s, 4096, 4096, wx, wcols(win_d, OFF_X + 512 * g, 512), 1)
            kb1 = wload(s, 8192, 2048, wbc[:, :, 0:128], wcols(win_d, OFF_B + 128 * g, 128), 2)
            kb2 = wload(s, 8192, 2048, wbc[:, :, 128:256], wcols(win_d, OFF_C + 128 * g, 128), 3)
            kbc = kb1
            gch = [4 * g + j for j in range(4)] + [16 + g, 20 + g]
            hk_ = ("hist", g)
            raw2, rawS, acc, xc, sz = T["raw2"], T["rawS"], [T["acc0"], T["acc1"]], T["xc"], T["sz"]
            Rt, Et, cbm = T["Rt"], T["Et"], T["cbm"]
            ytmp, yt2, ytm, t4, t3, ygf, sqs, hs = T["ytmp"], T["yt2"], T["ytm"], T["t4"], T["t3"], T["ygf"], T["sqs"], T["hs"]
            hbs, ctm, btmm = T["hbs"], T["ctm"], T["btmm"]
            for ti, (t0, tn, is_s) in enumerate(TT[:DBG["ssm_tiles"]]):
                for j in range(6):
                    cj = gch[j]
                    wv_ = wx[:, :, j * 128:(j + 1) * 128] if j < 4 else wbc[:, :, (j - 4) * 128:(j - 3) * 128]
                    wkeys = kx if j < 4 else kbc
                    b = 6 + j % 2
                    rw = raw2[:, j % 2, :]
                    rwk = KR("raw2", 520 * (j % 2), 520)
                    for k in range(KD):
                        MM(ps[b][:, 0:tn], wv_[:, k, :], uT[:, k, t0:t0 + tn], k == 0, k == KD - 1, wkeys + [UK(ti, k)], [PK[b]])
                    a_ = acc[j % 2]
                    akk = K["acc%d" % (j % 2)]
                    if not is_s:
                        CP(rw[:, 0:3], hist[:, 6 * g + j, :], [hk_], rwk)
                        ACT(rw[:, 3:3 + tn], ps[b][:, 0:tn], AF.Copy, [PK[b]], rwk)
                        CP(hist[:, 6 * g + j, :], rw[:, tn:tn + 3], rwk, [hk_])
                        if last_pass:
                            CP(convp_st[:, cj, :], rw[:, tn:tn + 3], rwk, ["convp_st"])
                        win = lambda kk_: rw[:, kk_:kk_ + tn]
                        av = a_[:, 0:tn]
                        xo = xc[:, j, 0:tn]
                        rdk = rwk
                    else:
                        CP(rawS[:, j, :, 0:3], cvin[:, :, cj, :], ["cvin"], K["rawS"])
                        ACT(rawS[:, j, :, 3:7], ps[b][:, 0:tn].rearrange("p (s t) -> p s t", t=DEC_SEQ), AF.Copy, [PK[b]], K["rawS"])
                        CP(cvout[:, :, cj, :], rawS[:, j, :, 4:7], K["rawS"], ["cvout"])
                        win = lambda kk_: rawS[:, j, :, kk_:kk_ + DEC_SEQ]
                        av = a_[:, 0:tn].rearrange("p (s t) -> p s t", t=DEC_SEQ)
                        xo = xc[:, j, 0:tn].rearrange("p (s t) -> p s t", t=DEC_SEQ)
                        rdk = K["rawS"]
                    if DBG["jcut"] <= 1:
                        continue
                    TS(av, win(0), convw[:, cj, 0:1], convb[:, cj:cj + 1], ALU.mult, ALU.add, rdk + ["convw", "convb"], akk)
                    if DBG["jcut"] <= 2:
                        continue
                    for kk_ in range(1, 4):
                        STT(av, win(kk_), convw[:, cj, kk_:kk_ + 1], av, ALU.mult, ALU.add, rdk + ["convw"] + akk, akk)
                    if DBG["jcut"] <= 3:
                        continue
                    ACT(xo, av, AF.Silu, akk, KR("xc", 256 * j, 256))
                for j in range(4):
                    b = 6 + j % 2
                    for k in range(KD):
                        MM(ps[b][:, 0:tn], wz[:, k, j * 128:(j + 1) * 128], uT[:, k, t0:t0 + tn], k == 0, k == KD - 1, kz + [UK(ti, k)], [PK[b]])
                    ACT(sz[:, j, 0:tn], ps[b][:, 0:tn], AF.Silu, [PK[b]], KR("sz", 256 * j, 256))
                hsl = slice(8 * g, 8 * g + 8)

                def phaseA(ch, par):
                    (c0, Tc, ci, tok0, si) = ch
                    bat = si is not None
                    TRI, USTR = (tri_s, ustr_s) if bat else (tri, ustr)
                    trk, usk = ("tri_s", "ustr_s") if bat else ("tri", "ustr")
                    xdt, xdtt, btm, wT, sm8, decs = (T["xdt%d" % par], T["xdtt%d" % par], T["btm%d" % par], T["wT%d" % par],
                                                     T["sm8%d" % par], T["decs%d" % par])
                    kxdt, kxdtt, kbtm, kwT, k8, kdecs = (K["xdt%d" % par], K["xdtt%d" % par], K["btm%d" % par], K["wT%d" % par],
                                                         K["sm8%d" % par], K["decs%d" % par])
                    la_sb, expla, lal_sb, dec_b, tail = sm8[:, 0, :], sm8[:, 1, :], sm8[:, 2, :], sm8[:, 3, :], sm8[:, 4, :]
                    k_la, k_ex, k_lal, k_dec, k_tl = [KR("sm8%d" % par, 8 * i_, 8) for i_ in range(5)]
                    tp = ps[0][:, :].bitcast(BF16)
                    for j in range(5):
                        TR(tp[0:Tc, j * 128:(j + 1) * 128], xc[:, j, c0:c0 + Tc], identb[:, :], KR("xc", 256 * j, 256) + ["identb"], [PK[0]])
                    dts = dt_tm[0:Tc, ci, hsl]
                    TT_(xdt[0:Tc, :].rearrange("p (h d) -> p h d", h=8), tp[0:Tc, 0:512].rearrange("p (h d) -> p h d", h=8),
                        dts.unsqueeze(2).to_broadcast([Tc, 8, 64]), ALU.mult, [PK[0], ("dt", ci)], kxdt)
                    CP(btm[0:Tc, :], tp[0:Tc, 512:640], [PK[0]], kbtm)
                    dtas = dta_tm[0:Tc, ci, hsl]
                    MM(ps[1][0:Tc, 0:8], TRI[0:Tc, 0:Tc], dtas, True, True, [trk, ("dta", ci)], [PK[1]])
                    if not bat:
                        MM(ps[1][:, 8:16], onesf[0:Tc, :], dtas, True, True, ["onesf", ("dta", ci)], [PK[1]])
                    else:
                        MM(ps[1][0:Tc, 8:16], blk_s[0:Tc, 0:Tc], dtas, True, True, ["blk_s", ("dta", ci)], [PK[1]])
                        for i in range(NS):
                            MM(ps[1][:, 16 + 8 * i:24 + 8 * i], selB[0:Tc, i, :], dtas, True, True, ["selB", ("dta", ci)], [PK[1]])
                    MM(ps[1][0:Tc, 128:128 + Tc], xc[:, 4, c0:c0 + Tc], xc[:, 5, c0:c0 + Tc], True, True, KR("xc", 1024, 512), [PK[1]])
                    ACT(la_sb[0:Tc, :], ps[1][0:Tc, 0:8], AF.Copy, [PK[1]], k_la)
                    ACT(expla[0:Tc, :], ps[1][0:Tc, 0:8], AF.Exp, [PK[1]], k_ex)
                    if not bat:
                        ACT(lal_sb[:, :], ps[1][:, 8:16], AF.Copy, [PK[1]], k_lal)
                        ACT(dec_b[:, :], ps[1][:, 8:16], AF.Exp, [PK[1]], k_dec)
                    else:
                        ACT(lal_sb[0:Tc, :], ps[1][0:Tc, 8:16], AF.Copy, [PK[1]], k_lal)
                        ACT(decs[:, :], ps[1][:, 16:16 + 8 * NS], AF.Exp, [PK[1]], kdecs)
                    TT_(cbm[0:Tc, 0:Tc], ps[1][0:Tc, 128:128 + Tc], TRI[0:Tc, 0:Tc], ALU.mult, [PK[1], trk], K["cbm"])
                    TT_(tail[0:Tc, :], lal_sb[0:Tc, :], la_sb[0:Tc, :], ALU.subtract, k_lal + k_la, k_tl)
                    ACT(tail[0:Tc, :], tail[0:Tc, :], AF.Exp, k_tl, k_tl)
                    TT_(xdtt[0:Tc, :].rearrange("p (h d) -> p h d", h=8), xdt[0:Tc, :].rearrange("p (h d) -> p h d", h=8),
                        tail[0:Tc, :].unsqueeze(2).to_broadcast([Tc, 8, 64]), ALU.mult, kxdt + k_tl, kxdtt)
                    for hf in range(2):
                        TT_(Rt[0:Tc, :, 0:Tc], dta_tm[0:Tc, ci, 8 * g + 4 * hf:8 * g + 4 * hf + 4].unsqueeze(2).to_broadcast([Tc, 4, Tc]),
                            TRI[0:Tc, 0:Tc].unsqueeze(1).to_broadcast([Tc, 4, Tc]), ALU.mult, [("dta", ci), trk], K["Rt"])
                        segv = ps[2 + hf][0:Tc, 0:4 * Tc].rearrange("p (h t) -> p h t", h=4)
                        MM(segv, USTR[0:Tc, 0:Tc], Rt[0:Tc, :, 0:Tc], True, True, [usk] + K["Rt"], [PK[2 + hf]])
                        ACT(Et[0:Tc, :, 0:Tc], segv, AF.Exp, [PK[2 + hf]], K["Et"])
                        TT_(wT[0:Tc, 4 * hf:4 * hf + 4, 0:Tc], Et[0:Tc, :, 0:Tc], cbm[0:Tc, 0:Tc].unsqueeze(1).to_broadcast([Tc, 4, Tc]), ALU.mult,
                            K["Et"] + K["cbm"], kwT)

                def phaseB(ch, par):
                    (c0, Tc, ci, tok0, si) = ch
                    bat = si is not None
                    xdt, xdtt, btm, wT, sm8, decs = (T["xdt%d" % par], T["xdtt%d" % par], T["btm%d" % par], T["wT%d" % par],
                                                     T["sm8%d" % par], T["decs%d" % par])
                    kxdt, kxdtt, kbtm, kwT, k8, kdecs = (K["xdt%d" % par], K["xdtt%d" % par], K["btm%d" % par], K["wT%d" % par],
                                                         K["sm8%d" % par], K["decs%d" % par])
                    expla, dec_b = sm8[:, 1, :], sm8[:, 3, :]
                    k_la, k_ex, k_lal, k_dec, k_tl = [KR("sm8%d" % par, 8 * i_, 8) for i_ in range(5)]
                    if not bat:
                        st, stk = hst[:, g, :], [("hst", g)]
                        ACT(hbf[:, :], st, AF.Copy, stk, ["hbf"])
                    else:
                        s0 = p * NS
                        stk = K["hs"]
                        DMA(hs[:, :, :], ssm_in_d[s0:s0 + NS, :, 512 * g:512 * g + 512].rearrange("s n f -> n s f"), (), stk, semkey="hsld")
                        ACT(hbs[:, :, :], hs[:, :, :], AF.Copy, stk, K["hbs"])
                    for hh in range(8):
                        MM(ps[4][0:Tc, hh * 64:(hh + 1) * 64], wT[0:Tc, hh, 0:Tc], xdt[0:Tc, hh * 64:(hh + 1) * 64], True, True,
                           kwT + kxdt, [PK[4]])
                    if not bat:
                        MM(ps[5][0:Tc, 0:512], xc[:, 5, c0:c0 + Tc], hbf[:, :], True, True, KR("xc", 1280, 256) + ["hbf"], [PK[5]])
                    else:
                        TT_(ctm[:, :, 0:Tc], xc[:, 5, c0:c0 + Tc].unsqueeze(1).to_broadcast([128, NS, Tc]), csel[:, :, 0:Tc], ALU.mult,
                            KR("xc", 1280, 256) + ["csel"], K["ctm"])
                        for i in range(NS):
                            MM(ps[5][0:Tc, 0:512], ctm[:, i, 0:Tc], hbs[:, i, :], i == 0, i == NS - 1, K["ctm"] + K["hbs"], [PK[5]])
                    ACT(ytmp[0:Tc, :], ps[4][0:Tc, :], AF.Copy, [PK[4]], K["ytmp"])
                    TT_(yt2[0:Tc, :].rearrange("p (h d) -> p h d", h=8), ps[5][0:Tc, :].rearrange("p (h d) -> p h d", h=8),
                        expla[0:Tc, :].unsqueeze(2).to_broadcast([Tc, 8, 64]), ALU.mult, [PK[5]] + k_ex, K["yt2"])
                    TT_(ytm[0:Tc, :], ytmp[0:Tc, :], yt2[0:Tc, :], ALU.add, K["ytmp"] + K["yt2"], K["ytm"])
                    if not bat:
                        MM(ps[7][:, 0:512], btm[0:Tc, :], xdtt[0:Tc, :], True, True, kbtm + kxdtt, [PK[7]])
                        TT_(t4[:, :].rearrange("p (h d) -> p h d", h=8), st.rearrange("p (h d) -> p h d", h=8),
                            dec_b[:, :].unsqueeze(2).to_broadcast([128, 8, 64]), ALU.mult, stk + k_dec, K["t4"])
                        TT_(st, t4[:, :], ps[7][:, 0:512], ALU.add, K["t4"] + [PK[7]], stk)
                    else:
                        TT_(btmm[0:Tc, :, :], btm[0:Tc, :].unsqueeze(1).to_broadcast([Tc, NS, 128]),
                            seqsel[0:Tc, :].unsqueeze(2).to_broadcast([Tc, NS, 128]), ALU.mult, kbtm + ["seqsel"], K["btmm"])
                        for i in range(NS):
                            b = 7 if i % 2 == 0 else 5
                            MM(ps[b][:, 0:512], btmm[0:Tc, i, :], xdtt[0:Tc, :], True, True, K["btmm"] + kxdtt, [PK[b]])
                            TT_(t4[:, :].rearrange("p (h d) -> p h d", h=8), hs[:, i, :].rearrange("p (h d) -> p h d", h=8),
                                decs[:, 8 * i:8 * i + 8].unsqueeze(2).to_broadcast([128, 8, 64]), ALU.mult, stk + kdecs, K["t4"])
                            TT_(hs[:, i, :], t4[:, :], ps[b][:, 0:512], ALU.add, K["t4"] + [PK[b]], stk)
                        DMA(ssm_s_d[s0:s0 + NS, :, 512 * g:512 * g + 512].rearrange("s n f -> n s f"), hs[:, :, :], stk, [("ssms", p, g)], semkey="hsst")
                    for j in range(4):
                        TR(ps[6][:, j * Tc:(j + 1) * Tc], ytm[0:Tc, j * 128:(j + 1) * 128], identf[0:Tc, 0:Tc], K["ytm"] + ["identf"], [PK[6]])
                    TT_(t3[:, :, 0:Tc], xc[:, 0:4, c0:c0 + Tc], dpp[:, 4 * g:4 * g + 4].unsqueeze(2).to_broadcast([128, 4, Tc]), ALU.mult,
                        KR("xc", 0, 1024) + ["dpp"], K["t3"])
                    TT_(t3[:, :, 0:Tc], t3[:, :, 0:Tc], ps[6][:, 0:4 * Tc].rearrange("p (j t) -> p j t", j=4), ALU.add, K["t3"] + [PK[6]], K["t3"])
                    TT_(ygf[:, :, 0:Tc], t3[:, :, 0:Tc], sz[:, :, c0:c0 + Tc], ALU.mult, K["t3"] + K["sz"], K["ygf"])
                    ACT(sqs[:, :, 0:Tc], ygf[:, :, 0:Tc], AF.Square, K["ygf"], K["sqs"])
                    TT_(yg[:, 4 * g:4 * g + 4, tok0:tok0 + Tc], ygf[:, :, 0:Tc],
                        gssm[:, 4 * g:4 * g + 4].unsqueeze(2).to_broadcast([128, 4, Tc]), ALU.mult, K["ygf"] + ["gssm"], [YGK(ti)])
                    for j in range(4):
                        MM(ps[4][:, 0:Tc], onesb[:, :], sqs[:, j, 0:Tc], j == 0, j == 3, ["onesb"] + K["sqs"], [PK[4]])
                    if g == 0:
                        CP(ssq_b[:, tok0:tok0 + Tc], ps[4][:, 0:Tc], [PK[4]], [SQK(ti)])
                    else:
                        TT_(ssq_b[:, tok0:tok0 + Tc], ssq_b[:, tok0:tok0 + Tc], ps[4][:, 0:Tc], ALU.add, [PK[4], SQK(ti)], [SQK(ti)])

                chs = chunks_of_tile(ti)
                phaseA(chs[0], 0)
                for idx in range(1, len(chs)):
                    par2(lambda: phaseA(chs[idx], idx % 2), lambda: phaseB(chs[idx - 1], (idx - 1) % 2))
                phaseB(chs[-1], (len(chs) - 1) % 2)
            if last_pass:
                DMA(ssm_p_d[:, 512 * g:512 * g + 512], hst[:, g, :], [("hst", g)], [("ssmp", g)], semkey=("ssmp", g))

        def finalize(T, wb_d, gate_off, use_rs):
            CUR[0] = TTD
            mbf = T["mbf"]
            mk = K["mbf"]
            for ti, (t0, tn, _) in enumerate(TTD):
                if use_rs:
                    ACT(ssq_b[:, t0:t0 + tn], ssq_b[:, t0:t0 + tn], AF.Sqrt, [SQK(ti), "epsc"], [SQK(ti)], bias=epsc[:, 1:2], scale=1.0 / 2048)
                    RECIP(ssq_b[:, t0:t0 + tn], ssq_b[:, t0:t0 + tn], [SQK(ti)], [SQK(ti)])
            for half in range(2):
                s = next_slot()
                wb = WA[:, s, 0:8192].rearrange("p (c n) -> p c n", c=16)
                wg = WA[:, s, 8192:12288].rearrange("p (k n) -> p k n", k=KD)
                kb_ = wload(s, 0, 8192, wb, wb_d.rearrange("(c p) n -> p c n", p=128)[:, :, half * 512:(half + 1) * 512], 0)
                kg_ = wload(s, 8192, 4096, wg, wcols(win_d, gate_off + half * 512, 512), 1)
                for mi in range(4):
                    m = 4 * half + mi
                    for ti, (t0, tn, _) in enumerate(TTD):
                        pa, pg = (0, 1) if (m + ti) % 2 == 0 else (4, 5)
                        for c in range(16):
                            MM(ps[pa][:, 0:tn], wb[:, c, mi * 128:(mi + 1) * 128], yg[:, c, t0:t0 + tn], c == 0, c == 15, kb_ + [YGK(ti)], [PK[pa]])
                        for k in range(KD):
                            MM(ps[pg][:, 0:tn], wg[:, k, mi * 128:(mi + 1) * 128], uT[:, k, t0:t0 + tn], k == 0, k == KD - 1, kg_ + [UK(ti, k)], [PK[pg]])
                        sg = T["sgt%d" % ((m + ti) % 2)]
                        sgk = K["sgt%d" % ((m + ti) % 2)]
                        ACT(sg[:, 0:tn], ps[pg][:, 0:tn], AF.Sigmoid, [PK[pg]], sgk)
                        if use_rs:
                            TT_(sg[:, 0:tn], sg[:, 0:tn], ssq_b[:, t0:t0 + tn], ALU.mult, sgk + [SQK(ti)], sgk)
                        TT_(mbf[:, m, t0:t0 + tn], ps[pa][:, 0:tn], sg[:, 0:tn], ALU.mult, [PK[pa]] + sgk, mk)
            s = next_slot()
            wo = WA[:, s, 0:8192].rearrange("p (k n) -> p k n", k=KD)
            ko_ = wload(s, 0, 8192, wo, wcols(wout_d, 0, 1024), 0)
            for m in range(KD):
                for ti, (t0, tn, _) in enumerate(TTD):
                    b = 2 + (m + ti) % 2
                    for k in range(KD):
                        MM(ps[b][:, 0:tn], wo[:, k, m * 128:(m + 1) * 128], mbf[:, k, t0:t0 + tn], k == 0, k == KD - 1, ko_ + mk, [PK[b]])
                    TT_(hT[:, m, t0:t0 + tn], hT[:, m, t0:t0 + tn], ps[b][:, 0:tn], ALU.add, [HK(ti, m), PK[b]], [HK(ti, m)])
            CUR[0] = TT

        def ret_head(T, r, p, last_pass):
            s = next_slot()
            wq = WA[:, s, 0:2048].rearrange("p (k n) -> p k n", k=KD)
            wkk = WA[:, s, 2048:4096].rearrange("p (k n) -> p k n", k=KD)
            wv = WA[:, s, 4096:8192].rearrange("p (k n) -> p k n", k=KD)
            wg = WA[:, s, 8192:12288].rearrange("p (k n) -> p k n", k=KD)
            kq = wload(s, 0, 2048, wq, wcols(win_d, OFF_Q + 256 * r, 256), 0)
            kk_ = wload(s, 2048, 2048, wkk, wcols(win_d, OFF_K + 256 * r, 256), 1)
            kv = wload(s, 4096, 4096, wv, wcols(win_d, OFF_V + 512 * r, 512), 2)
            kg = wload(s, 8192, 4096, wg, wcols(win_d, OFF_G + 512 * r, 512), 3)
            gam = GAMMAS[r]
            qkf, rot, qT, qsT, kT, gsg = T["qkf"], T["rot"], T["qT"], T["qsT"], T["kT"], T["gsg"]
            vtm, kd, sm, yrt, sqs, rs, rsn = T["vtm"], T["kd"], T["sm"], T["yrt"], T["sqs"], T["rs"], T["rsn"]
            rbs, qsm, kdm = T["rbs"], T["qsm"], T["kdm"]
            cosT, sinT, qdecT = T["cosT"], T["sinT"], T["qdecT"]
            DMA(qdecT[:, :], qdec_d[r, :, :], (), K["qdecT"], semkey="qdld")
            for ti, (t0, tn, is_s) in enumerate(TT):
                for qi, (wv_, wks) in enumerate(((wq, kq), (wkk, kk_))):
                    for dc in range(2):
                        b = 6 + dc
                        for k in range(KD):
                            MM(ps[b][:, 0:tn], wv_[:, k, dc * 128:(dc + 1) * 128], uT[:, k, t0:t0 + tn], k == 0, k == KD - 1, wks + [UK(ti, k)], [PK[b]])
                        ACT(qkf[:, qi, dc, 0:tn], ps[b][:, 0:tn], AF.Copy, [PK[b]], KR("qkf", 1024 * qi + 512 * dc, 512), scale=(1.0 if qi == 0 else 1.0 / 16.0))
                    t1, t2 = qkf[:, qi, 0, 0:tn], qkf[:, qi, 1, 0:tn]
                    cs, sn = cosT[:, t0:t0 + tn], sinT[:, t0:t0 + tn]
                    dst = qT if qi == 0 else kT
                    dk = K["qT"] if qi == 0 else K["kT"]
                    kq1, kq2 = KR("qkf", 1024 * qi, 512), KR("qkf", 1024 * qi + 512, 512)
                    kr = [KR("rot", 512 * i_, 512) for i_ in range(4)]
                    dk0, dk1 = KR("qT" if qi == 0 else "kT", 0, 256), KR("qT" if qi == 0 else "kT", 256, 256)
                    TT_(rot[:, 0, 0:tn], t1, cs, ALU.mult, kq1 + K["cosT"], kr[0])
                    TT_(rot[:, 1, 0:tn], t2, sn, ALU.mult, kq2 + K["sinT"], kr[1])
                    TT_(rot[:, 2, 0:tn], t1, sn, ALU.mult, kq1 + K["sinT"], kr[2])
                    TT_(rot[:, 3, 0:tn], t2, cs, ALU.mult, kq2 + K["cosT"], kr[3])
                    TT_(dst[:, 0, 0:tn], rot[:, 0, 0:tn], rot[:, 1, 0:tn], ALU.subtract, kr[0] + kr[1], dk0)
                    TT_(dst[:, 1, 0:tn], rot[:, 2, 0:tn], rot[:, 3, 0:tn], ALU.add, kr[2] + kr[3], dk1)
                TT_(qsT[:, :, 0:tn], qT[:, :, 0:tn], qdecT[:, t0:t0 + tn].unsqueeze(1).to_broadcast([128, 2, tn]), ALU.mult, K["qT"] + K["qdecT"], K["qsT"])
                for j in range(4):
                    b = 6 + j % 2
                    for k in range(KD):
                        MM(ps[b][:, 0:tn], wg[:, k, j * 128:(j + 1) * 128], uT[:, k, t0:t0 + tn], k == 0, k == KD - 1, kg + [UK(ti, k)], [PK[b]])
                    sg = T["sgt%d" % (j % 2)]
                    sgk = K["sgt%d" % (j % 2)]
                    ACT(sg[:, 0:tn], ps[b][:, 0:tn], AF.Silu, [PK[b]], sgk)
                    TS(gsg[:, j, 0:tn], sg[:, 0:tn], gret[:, 4 * r + j:4 * r + j + 1], None, ALU.mult, None, sgk + ["gret"], KR("gsg", 256 * j, 256))
                for (c0, Tc, ci, tok0, si) in chunks_of_tile(ti):
                    bat = si is not None
                    if not bat:
                        st, stk = rst[:, r, :, :], [("rst", r)]
                        kdc = kdec[0:Tc, r:r + 1]
                        cdec = gam ** 128
                        DM = dmatT
                        dmk = "dmatT"
                        ACT(rbf[:, :, :], st, AF.Copy, stk, ["rbf"])
                    else:
                        s0 = p * NS
                        stk = K["rs"]
                        for i in range(NS):
                            DMA(rs[:, i, :, :], ret_in_d[s0 + i, r, :, :].rearrange("(c p) e -> p c e", p=128), (), stk, semkey=("rsld", i))
                        kdc = kdec[0:Tc, 4 + r:5 + r]
                        cdec = gam ** DEC_SEQ
                        DM = dmatS
                        dmk = "dmatS"
                        ACT(rbs[:, :, :, :], rs[:, :, :, :], AF.Copy, stk, K["rbs"])
                    for k in range(KD):
                        MM(ps[4][0:Tc, 0:512], uT[:, k, tok0:tok0 + Tc], wv[:, k, :], k == 0, k == KD - 1, [UK(ti, k)] + kv, [PK[4]])
                    ACT(vtm[0:Tc, :], ps[4][0:Tc, :], AF.Copy, [PK[4]], K["vtm"])
                    tp = ps[0][:, :].bitcast(BF16)
                    for dc in range(2):
                        TR(tp[0:Tc, dc * 128:(dc + 1) * 128], kT[:, dc, c0:c0 + Tc], identb[:, :], K["kT"] + ["identb"], [PK[0]])
                    TS(kd[0:Tc, :], tp[0:Tc, 0:256], kdc, None, ALU.mult, None, [PK[0], "kdec"], K["kd"])
                    for dc in range(2):
                        MM(ps[1][0:Tc, 0:Tc], kT[:, dc, c0:c0 + Tc], qT[:, dc, c0:c0 + Tc], dc == 0, dc == 1, K["kT"] + K["qT"], [PK[1]])
                    TT_(sm[0:Tc, 0:Tc], ps[1][0:Tc, 0:Tc], DM[0:Tc, r, 0:Tc], ALU.mult, [PK[1], dmk], K["sm"])
                    if bat:
                        TT_(qsm[:, :, :, 0:Tc], qsT[:, :, c0:c0 + Tc].unsqueeze(2).to_broadcast([128, 2, NS, Tc]),
                            csel[:, :, 0:Tc].unsqueeze(1).to_broadcast([128, 2, NS, Tc]), ALU.mult, K["qsT"] + ["csel"], K["qsm"])
                    for ec in range(4):
                        o_ = ps[5][:, ec * Tc:(ec + 1) * Tc]
                        MM(o_, vtm[0:Tc, ec * 128:(ec + 1) * 128], sm[0:Tc, 0:Tc], True, False, K["vtm"] + K["sm"], [PK[5]])
                        if not bat:
                            MM(o_, rbf[:, 0, ec * 128:(ec + 1) * 128], qsT[:, 0, c0:c0 + Tc], False, False, ["rbf"] + K["qsT"], [PK[5]])
                            MM(o_, rbf[:, 1, ec * 128:(ec + 1) * 128], qsT[:, 1, c0:c0 + Tc], False, True, ["rbf"] + K["qsT"], [PK[5]])
                        else:
                            for i in range(NS):
                                for dc in range(2):
                                    MM(o_, rbs[:, i, dc, ec * 128:(ec + 1) * 128], qsm[:, dc, i, 0:Tc], False, (i == NS - 1 and dc == 1),
                                       K["rbs"] + K["qsm"], [PK[5]])
                    if not bat:
                        for dc in range(2):
                            b = 2 + dc
                            MM(ps[b][:, 0:512], kd[0:Tc, dc * 128:(dc + 1) * 128], vtm[0:Tc, :], True, True, K["kd"] + K["vtm"], [PK[b]])
                            STT(st[:, dc, :], st[:, dc, :], float(cdec), ps[b][:, 0:512], ALU.mult, ALU.add, stk + [PK[b]], stk)
                    else:
                        TT_(kdm[0:Tc, :, :], kd[0:Tc, :].unsqueeze(1).to_broadcast([Tc, NS, 256]),
                            seqsel[0:Tc, :].unsqueeze(2).to_broadcast([Tc, NS, 256]), ALU.mult, K["kd"] + ["seqsel"], K["kdm"])
                        for i in range(NS):
                            for dc in range(2):
                                b = 2 + (2 * i + dc) % 2 + (4 if (2 * i + dc) % 4 >= 2 else 0)
                                MM(ps[b][:, 0:512], kdm[0:Tc, i, dc * 128:(dc + 1) * 128], vtm[0:Tc, :], True, True, K["kdm"] + K["vtm"], [PK[b]])
                                STT(rs[:, i, dc, :], rs[:, i, dc, :], float(cdec), ps[b][:, 0:512], ALU.mult, ALU.add, stk + [PK[b]], stk)
                        for i in range(NS):
                            DMA(ret_s_d[s0 + i, r, :, :].rearrange("(c p) e -> p c e", p=128), rs[:, i, :, :], stk, [("rets", p, r, i)], semkey=("rsst", i))
                    yv = ps[5][:, 0:4 * Tc].rearrange("p (j t) -> p j t", j=4)
                    ACT(sqs[:, :, 0:Tc], yv, AF.Square, [PK[5]], K["sqs"])
                    for j in range(4):
                        MM(ps[1][:, 256:256 + Tc], onesb[:, :], sqs[:, j, 0:Tc], j == 0, j == 3, ["onesb"] + K["sqs"], [PK[1]])
                    ACT(rsn[:, 0:Tc], ps[1][:, 256:256 + Tc], AF.Sqrt, [PK[1], "epsc"], K["rsn"], bias=epsc[:, 0:1], scale=1.0 / 512)
                    RECIP(rsn[:, 0:Tc], rsn[:, 0:Tc], K["rsn"], K["rsn"])
                    TT_(yrt[:, :, 0:Tc], yv, rsn[:, 0:Tc].unsqueeze(1).to_broadcast([128, 4, Tc]), ALU.mult, [PK[5]] + K["rsn"], K["yrt"])
                    TT_(yg[:, 4 * r:4 * r + 4, tok0:tok0 + Tc], yrt[:, :, 0:Tc], gsg[:, :, c0:c0 + Tc], ALU.mult, K["yrt"] + K["gsg"], [YGK(ti)])
            if last_pass:
                DMA(ret_p_d[r, :, :].rearrange("(c p) e -> p c e", p=128), rst[:, r, :, :], [("rst", r)], [("retp", r)], semkey=("retp", r))

        for p in range(DBG["passes"]):
            last = p == NPASS - 1
            for ti, (t0, tn, _) in enumerate(TT):
                DMA(hT[:, :, t0:t0 + tn], xT_d[p, :, :, t0:t0 + tn], (), [HK(ti, m) for m in range(KD)], semkey=("xin", ti))
            for si in range(NS):
                DMA(cvin[:, si, :, :], conv_in_d[p * NS + si, :, :, :], (), ["cvin"], semkey=("cvin", si))
            P.mark('p%d ffn1' % p)
            T = stage_begin()
            T["sq"] = A("sq", [128, KD, 512], BF16)
            T["aT0"] = A("aT0", [128, 4, 512], BF16)
            T["aT1"] = A("aT1", [128, 4, 512], BF16)
            norm_stage(T, 0, tiling=TTD)
            ffn_stage(T, 0)
            if DBG["stop"] <= 1:
                continue
            P.mark('p%d mixnorm' % p)
            T = stage_begin()
            T["sq"] = A("sq", [128, KD, 512], BF16)
            T["smallt"] = A("smallt", [128, 4, 32], F32)
            norm_stage(T, 1)
            dt_stage(T)
            if DBG["stop"] <= 2:
                continue
            P.mark('p%d ssm' % p)
            T = stage_begin()
            for nm, shp, dt_ in (("raw2", [128, 2, 520], F32), ("rawS", [128, 6, NS, 7], F32), ("acc0", [128, 512], F32),
                                 ("acc1", [128, 512], F32), ("xc", [128, 6, 512], BF16), ("sz", [128, 4, 512], BF16),
                                 ("xdt0", [128, 512], BF16), ("xdtt0", [128, 512], BF16), ("btm0", [128, 128], BF16),
                                 ("xdt1", [128, 512], BF16), ("xdtt1", [128, 512], BF16), ("btm1", [128, 128], BF16),
                                 ("sm80", [128, 5, 8], F32), ("sm81", [128, 5, 8], F32), ("wT0", [128, 8, 128], BF16),
                                 ("wT1", [128, 8, 128], BF16), ("decs0", [128, 8 * NS], F32), ("decs1", [128, 8 * NS], F32),
                                 ("Rt", [128, 4, 128], F32), ("Et", [128, 4, 128], F32),
                                 ("cbm", [128, 128], F32), ("ytmp", [128, 512], F32),
                                 ("yt2", [128, 512], F32), ("ytm", [128, 512], F32), ("t4", [128, 512], F32),
                                 ("t3", [128, 4, 128], F32), ("ygf", [128, 4, 128], F32), ("sqs", [128, 4, 128], BF16),
                                 ("hs", [128, NS, 512], F32), ("hbs", [128, NS, 512], BF16),
                                 ("ctm", [128, NS, 32], BF16), ("btmm", [128, NS, 128], BF16)):
                T[nm] = A(nm, shp, dt_)
            T["mbf"] = A("mbf", [128, KD, NTP], BF16, alias="hs")
            for g in range(DBG["groups"]):
                ssm_group(T, g, p, last)
            if last:
                DMA(conv_p_d[:, :, :], convp_st[:, :, :], ["convp_st"], ["convp_out"], semkey="convp_out")
            if DBG["stop"] <= 3:
                continue
            for si in range(NS):
                DMA(conv_s_d[p * NS + si, :, :, :], cvout[:, si, :, :], ["cvout"], [("convs", p, si)], semkey=("cvout", si))
            P.mark('p%d fin_ssm' % p)
            finalize(T, wbs_d, OFF_GA, True)
            if DBG["stop"] <= 4:
                continue
            P.mark('p%d ret' % p)
            T = stage_begin()
            for nm, shp, dt_ in (("qkf", [128, 2, 2, 512], F32), ("rot", [128, 4, 512], F32), ("qT", [128, 2, 512], BF16),
                                 ("qsT", [128, 2, 512], BF16), ("kT", [128, 2, 512], BF16), ("gsg", [128, 4, 512], BF16),
                                 ("vtm", [128, 512], BF16), ("kd", [128, 256], BF16), ("sm", [128, 128], BF16),
                                 ("yrt", [128, 4, 128], F32), ("sqs", [128, 4, 128], BF16), ("rs", [128, NS, 2, 512], F32),
                                 ("rbs", [128, NS, 2, 512], BF16), ("qsm", [128, 2, NS, 32], BF16), ("kdm", [128, NS, 256], BF16),
                                 ("cosT", [128, NTP], F32), ("sinT", [128, NTP], F32), ("qdecT", [128, NTP], F32)):
                T[nm] = A(nm, shp, dt_)
            T["mbf"] = A("mbf", [128, KD, NTP], BF16, alias="qkf")
            DMA(T["cosT"][:, :], cos_d[p, :, :], (), K["cosT"], semkey="cosld")
            DMA(T["sinT"][:, :], sin_d[p, :, :], (), K["sinT"], semkey="sinld")
            for r in range(4):
                ret_head(T, r, p, last)
            if DBG["stop"] <= 5:
                continue
            P.mark('p%d fin_ret' % p)
            finalize(T, wbr_d, OFF_GB, False)
            if DBG["stop"] <= 6:
                continue
            P.mark('p%d ffn2' % p)
            T = stage_begin()
            T["sq"] = A("sq", [128, KD, 512], BF16)
            T["aT0"] = A("aT0", [128, 4, 512], BF16)
            T["aT1"] = A("aT1", [128, 4, 512], BF16)
            T["outT"] = A("outT", [128, KD, 512], F32)
            norm_stage(T, 2, tiling=TTD)
            ffn_stage(T, 1)
            norm_stage(T, 3, out_dma=yT_d[p], tiling=TTD)
        P.emit()
        build_nc.stats = P.stats
        build_nc.marks = P.marks
    return nc


def _host_consts():
    half = 128
    inv = (10000.0 ** (-np.arange(half, dtype=np.float32) / np.float32(half))).astype(np.float32)
    cosT = np.zeros((NPASS, 128, NTP), np.float32)
    sinT = np.zeros((NPASS, 128, NTP), np.float32)
    for p in range(NPASS):
        pos = np.concatenate([np.arange(p * NPT, (p + 1) * NPT, dtype=np.float32),
                              np.tile(PAST + np.arange(DEC_SEQ, dtype=np.float32), NS)]).astype(np.float32)
        ang = (pos[None, :] * inv[:, None]).astype(np.float32)
        cosT[p] = np.cos(ang.astype(np.float64)).astype(np.float32)
        sinT[p] = np.sin(ang.astype(np.float64)).astype(np.float32)
    lg = np.log1p(-np.exp2(-5.0 - np.arange(4, dtype=np.float64)))
    inchunk = np.concatenate([np.arange(NPT) % 128, np.tile(np.arange(DEC_SEQ), NS)]).astype(np.float64)
    qdecT = np.zeros((4, 128, NTP), np.float32)
    for h in range(4):
        qdecT[h] = np.exp((inchunk + 1.0) * lg[h])[None, :]
    kdec = np.zeros((128, 8), np.float32)
    for h in range(4):
        kdec[:, h] = np.exp((127.0 - np.arange(128)) * lg[h])
        kdec[0:DEC_SEQ, 4 + h] = np.exp((DEC_SEQ - 1.0 - np.arange(DEC_SEQ)) * lg[h])
    dmatT = np.zeros((128, 4, 128), np.float32)
    s_ = np.arange(128)[:, None]
    t_ = np.arange(128)[None, :]
    for h in range(4):
        dmatT[:, h, :] = np.where(t_ >= s_, np.exp(np.maximum(t_ - s_, 0) * lg[h]), 0.0)
    tri = (s_ <= t_).astype(np.float32)
    ustrict = (s_ > t_).astype(np.float32)
    identf = np.eye(128, dtype=np.float32)
    seq_of = np.arange(128) // DEC_SEQ
    same = (seq_of[:, None] == seq_of[None, :])
    tri_s = (same & (s_ <= t_)).astype(np.float32)
    ustr_s = (same & (s_ > t_)).astype(np.float32)
    blk_s = same.astype(np.float32)
    seqsel = (seq_of[:, None] == np.arange(NS)[None, :]).astype(np.float32)
    selB = np.ascontiguousarray(np.broadcast_to(seqsel[:, :, None], (128, NS, 128))).astype(np.float32)
    csel = np.ascontiguousarray(np.broadcast_to((np.arange(NS)[:, None] == seq_of[None, :32])[None], (128, NS, 32))).astype(np.float32)
    dmatS = np.zeros((128, 4, 128), np.float32)
    for h in range(4):
        dmatS[:, h, :] = np.where(same & (t_ >= s_), np.exp(np.maximum(t_ - s_, 0) * lg[h]), 0.0)
        kdec[:, 4 + h] = np.exp((DEC_SEQ - 1.0 - (np.arange(128) % DEC_SEQ)) * lg[h])
    return dict(cosT=cosT, sinT=sinT, qdecT=qdecT, kdec=kdec, dmatT=dmatT, tri=tri, ustrict=ustrict, identf=identf,
                tri_s=tri_s, ustr_s=ustr_s, blk_s=blk_s, seqsel=seqsel, selB=selB, csel=csel, dmatS=dmatS)


_NC_CACHE = {}
_PREP_ONLY = [False]


def kernel(x_prompt, x_sample, state_ssm, state_conv, state_ret,
           norm_ffn1, ffn1_w1, ffn1_w3, ffn1_w2, norm_mix, w_in, conv_w, conv_b,
           dt_bias, a_log, ssm_d, ssm_norm, ret_norm, w_branch_ssm, w_branch_ret, w_out,
           norm_ffn2, ffn2_w1, ffn2_w3, ffn2_w2, norm_final):
    f = lambda a: np.ascontiguousarray(np.asarray(a, dtype=np.float32))
    x_prompt, x_sample = f(x_prompt), f(x_sample)
    state_ssm, state_conv, state_ret = f(state_ssm)[0], f(state_conv)[0], f(state_ret)[0]
    SPC = DEC_B // NCORES
    consts = _host_consts()
    pp = lambda v: f(np.asarray(v).reshape(-1, 128).T)
    bc = lambda v: f(np.broadcast_to(np.asarray(v).reshape(1, -1), (128, np.asarray(v).size)))
    shared = dict(
        f1w1=f(ffn1_w1)[0], f1w3=f(ffn1_w3)[0], f1w2=f(ffn1_w2)[0],
        f2w1=f(ffn2_w1)[0], f2w3=f(ffn2_w3)[0], f2w2=f(ffn2_w2)[0],
        w_in=f(w_in)[0], w_bs=f(w_branch_ssm)[0], w_br=f(w_branch_ret)[0], w_out=f(w_out)[0],
        gains=f(np.stack([pp(norm_ffn1), pp(norm_mix), pp(norm_ffn2), pp(norm_final)], axis=1)),
        g_ssm=pp(ssm_norm), g_ret=pp(ret_norm),
        convw=f(np.asarray(conv_w)[0].reshape(4, 24, 128).transpose(2, 1, 0)),
        convb=pp(conv_b), dtb=bc(dt_bias), alog=bc(a_log),
        dpp=f(np.repeat(np.asarray(ssm_d).reshape(-1), 64).reshape(16, 128).T),
        **consts,
    )
    in_maps = []
    for c in range(NCORES):
        xs = x_sample[c * SPC:(c + 1) * SPC]
        xT = np.zeros((NPASS, 128, KD, NTP), np.float32)
        for p in range(NPASS):
            tok = np.concatenate([x_prompt[c, p * NPT:(p + 1) * NPT], xs[p * NS:(p + 1) * NS].reshape(NS * DEC_SEQ, D)], axis=0)
            xT[p] = tok.T.reshape(KD, 128, NTP).transpose(1, 0, 2)
        m = dict(shared)
        m["xT"] = xT
        m["ssmT_in"] = f(state_ssm[c * SPC:(c + 1) * SPC].reshape(SPC, 2048, 128).transpose(0, 2, 1))
        m["conv_in"] = f(state_conv[c * SPC:(c + 1) * SPC].reshape(SPC, 3, 24, 128).transpose(0, 3, 2, 1))
        m["ret_in"] = f(state_ret[c * SPC:(c + 1) * SPC])
        in_maps.append(m)
    if _PREP_ONLY[0]:
        return in_maps
    if "nc" not in _NC_CACHE:
        _NC_CACHE["nc"] = build_nc()
    nc = _NC_CACHE["nc"]
    res = run_bass_kernel_spmd(nc, in_maps, core_ids=list(range(NCORES)))
    return _post(res.results)


def _post(R):
    SPC = DEC_B // NCORES
    y_prompt = np.zeros((NCORES, SEQ, D), np.float32)
    y_sample = np.zeros((DEC_B, DEC_SEQ, D), np.float32)
    ssm_p = np.zeros((1, NCORES, 32, 64, 128), np.float32)
    conv_p = np.zeros((1, NCORES, 3, 3072), np.float32)
    ret_p = np.zeros((1, NCORES, 4, 256, 512), np.float32)
    ssm_s = np.zeros((1, DEC_B, 32, 64, 128), np.float32)
    conv_s = np.zeros((1, DEC_B, 3, 3072), np.float32)
    ret_s = np.zeros((1, DEC_B, 4, 256, 512), np.float32)
    for c in range(len(R)):
        r = R[c]
        yT = r["yT"]
        for p in range(NPASS):
            tok = yT[p].transpose(1, 0, 2).reshape(D, NTP).T
            y_prompt[c, p * NPT:(p + 1) * NPT] = tok[:NPT]
            y_sample[c * SPC + p * NS:c * SPC + (p + 1) * NS] = tok[NPT:].reshape(NS, DEC_SEQ, D)
        ssm_p[0, c] = r["ssmT_p"].T.reshape(32, 64, 128)
        conv_p[0, c] = r["conv_p"].transpose(2, 1, 0).reshape(3, 3072)
        ret_p[0, c] = r["ret_p"]
        ssm_s[0, c * SPC:(c + 1) * SPC] = r["ssmT_s"].transpose(0, 2, 1).reshape(SPC, 32, 64, 128)
        conv_s[0, c * SPC:(c + 1) * SPC] = r["conv_s"].transpose(0, 3, 2, 1).reshape(SPC, 3, 3072)
        ret_s[0, c * SPC:(c + 1) * SPC] = r["ret_s"]
    return (y_prompt, y_sample, ssm_p, conv_p, ret_p, ssm_s, conv_s, ret_s)
```
